# Optimizing a Trainium2 kernel written in Bass

```python
import jax, jax.numpy as jnp
from jax import lax
import numpy as np

D_MODEL = 1024
BATCH = 8
SEQ = 4096
DEPTH = 4

CHUNK = 64
PLE_DIM = 256
N_EVEN = (DEPTH + 1) // 2
N_ODD = DEPTH // 2
D_A = D_MODEL
CONV_A = 31
D_B = D_MODEL
HEAD_DIM = 64
H_B = D_B // HEAD_DIM
N_GROUPS = 4
N_STATE = 128
CONV_B = 4
XBC_DIM = D_B + 2 * N_GROUPS * N_STATE
E_IN = 2 * D_A + D_B + XBC_DIM + H_B
D_C = D_MODEL
CONV_C = 3
D_FF = 2816
CONV_F = 3
LN_EPS = 1e-5

kernel_name = "hybrid_conformer_ssd_shortconv_trunk"


def _layer_norm(x, g, b):
    xf = x.astype(jnp.float32)
    mu = jnp.mean(xf, axis=-1, keepdims=True)
    var = jnp.mean(jnp.square(xf - mu), axis=-1, keepdims=True)
    return ((xf - mu) * lax.rsqrt(var + LN_EPS) * g + b).astype(x.dtype)


def _rms_norm(x, g):
    xf = x.astype(jnp.float32)
    return (xf * lax.rsqrt(jnp.mean(jnp.square(xf), axis=-1, keepdims=True) + LN_EPS) * g).astype(x.dtype)


def _dwconv_causal(x, w, b=None):
    k, c = w.shape
    y = lax.conv_general_dilated(x, w[:, None, :].astype(x.dtype), window_strides=(1,),
                                 padding=[(k - 1, 0)], dimension_numbers=("NWC", "WIO", "NWC"),
                                 feature_group_count=c)
    if b is not None:
        y = y + b
    return y


def _conformer_conv(u, conv_w, conv_b, ln_g, ln_b):
    a = u[..., :D_A] * jax.nn.sigmoid(u[..., D_A:])
    a = _dwconv_causal(a, conv_w, conv_b)
    return jax.nn.silu(_layer_norm(a, ln_g, ln_b))


def _ssd(xh, dt, a, bm, cm):
    bsz, l, h, p = xh.shape
    g, n = bm.shape[2], bm.shape[3]
    hg = h // g
    c = l // CHUNK
    x = xh.reshape(bsz, c, CHUNK, g, hg, p)
    dtc = dt.reshape(bsz, c, CHUNK, g, hg)
    bc = bm.reshape(bsz, c, CHUNK, g, n)
    cc = cm.reshape(bsz, c, CHUNK, g, n)
    cum = jnp.cumsum(dtc * a.reshape(g, hg), axis=2)
    mask = jnp.tril(jnp.ones((CHUNK, CHUNK), dtype=bool))[:, :, None, None]
    seg = cum[:, :, :, None] - cum[:, :, None, :]
    decay = jnp.where(mask, jnp.exp(jnp.where(mask, seg, 0.0)), 0.0)
    cb = jnp.einsum("bclgn,bcsgn->bclsg", cc, bc)
    w = cb[..., None] * decay * dtc[:, :, None]
    y_diag = jnp.einsum("bclsgh,bcsghp->bclghp", w, x)
    decay_states = jnp.exp(cum[:, :, -1:] - cum) * dtc
    states = jnp.einsum("bclgn,bclgh,bclghp->bcghpn", bc, decay_states, x)
    chunk_decay = jnp.exp(cum[:, :, -1])

    def step(hstate, inp):
        dec, st = inp
        return hstate * dec[..., None, None] + st, hstate

    h0 = jnp.zeros((bsz, g, hg, p, n), dtype=states.dtype)
    _, prev = lax.scan(step, h0, (jnp.moveaxis(chunk_decay, 1, 0), jnp.moveaxis(states, 1, 0)))
    prev = jnp.moveaxis(prev, 0, 1)
    y_off = jnp.einsum("bclgn,bcghpn,bclgh->bclghp", cc, prev, jnp.exp(cum))
    return (y_diag + y_off).reshape(bsz, l, h, p).astype(xh.dtype)


def _mamba2(z, xbc, dt_raw, conv_w, conv_b, dt_bias, a_log, d_skip, norm_g):
    bsz, l, _ = z.shape
    xbc = jax.nn.silu(_dwconv_causal(xbc, conv_w, conv_b))
    gn = N_GROUPS * N_STATE
    xs = xbc[..., :D_B].reshape(bsz, l, H_B, HEAD_DIM)
    bm = xbc[..., D_B:D_B + gn].reshape(bsz, l, N_GROUPS, N_STATE)
    cm = xbc[..., D_B + gn:].reshape(bsz, l, N_GROUPS, N_STATE)
    dt = jax.nn.softplus(dt_raw.astype(jnp.float32) + dt_bias)
    a = -jnp.exp(a_log.astype(jnp.float32))
    y = _ssd(xs, dt, a, bm, cm) + d_skip[:, None] * xs
    y = y.reshape(bsz, l, D_B) * jax.nn.silu(z)
    return _rms_norm(y, norm_g)


def _short_conv(x, w_in, conv_w, w_out):
    u = x @ w_in
    bg, cg, v = u[..., :D_C], u[..., D_C:2 * D_C], u[..., 2 * D_C:]
    return (bg * _dwconv_causal(cg * v, conv_w)) @ w_out


def _conv_ffn(x, w_up, conv_w, conv_b, w_down):
    h = _dwconv_causal(x @ w_up, conv_w, conv_b)
    return (jax.nn.silu(h[..., :D_FF]) * h[..., D_FF:]) @ w_down


def setup_inputs(seed: int = 0) -> dict:
    key = jax.random.key(seed)
    ks = jax.random.split(key, 32)
    beta = (8.0 * DEPTH) ** -0.25
    nrm = jax.random.normal

    def dense(k, shape, fan_in, scale=1.0):
        return nrm(k, shape, jnp.float32) * (fan_in ** -0.5) * scale

    dt0 = jnp.exp(jax.random.uniform(ks[8], (N_EVEN, H_B), jnp.float32) * (np.log(0.1) - np.log(0.001)) + np.log(0.001))
    return {
        "x": nrm(ks[0], (BATCH, SEQ, D_MODEL), jnp.float32),
        "p": nrm(ks[1], (DEPTH, BATCH, SEQ, PLE_DIM), jnp.float32),
        "e_w_in": dense(ks[2], (N_EVEN, D_MODEL, E_IN), D_MODEL),
        "e_conv_a_w": dense(ks[3], (N_EVEN, CONV_A, D_A), CONV_A),
        "e_conv_a_b": 0.02 * nrm(ks[4], (N_EVEN, D_A), jnp.float32),
        "e_ln_a_g": 1.0 + 0.02 * nrm(ks[5], (N_EVEN, D_A), jnp.float32),
        "e_ln_a_b": 0.02 * nrm(ks[6], (N_EVEN, D_A), jnp.float32),
        "e_conv_b_w": dense(ks[7], (N_EVEN, CONV_B, XBC_DIM), CONV_B),
        "e_conv_b_b": 0.02 * nrm(ks[9], (N_EVEN, XBC_DIM), jnp.float32),
        "e_dt_bias": dt0 + jnp.log(-jnp.expm1(-dt0)),
        "e_a_log": jnp.log(jax.random.uniform(ks[10], (N_EVEN, H_B), jnp.float32, 1.0, 16.0)),
        "e_d_skip": 1.0 + 0.1 * nrm(ks[11], (N_EVEN, H_B), jnp.float32),
        "e_norm_b_g": 1.0 + 0.02 * nrm(ks[12], (N_EVEN, D_B), jnp.float32),
        "e_w_out": dense(ks[13], (N_EVEN, D_A + D_B, D_MODEL), D_A + D_B, beta),
        "o_w_in": dense(ks[14], (N_ODD, D_MODEL, 3 * D_C), D_MODEL),
        "o_conv_w": dense(ks[15], (N_ODD, CONV_C, D_C), CONV_C),
        "o_w_out": dense(ks[16], (N_ODD, D_C, D_MODEL), D_C, beta),
        "f_w_up": dense(ks[17], (DEPTH, D_MODEL, 2 * D_FF), D_MODEL),
        "f_conv_w": dense(ks[18], (DEPTH, CONV_F, 2 * D_FF), CONV_F),
        "f_conv_b": 0.02 * nrm(ks[19], (DEPTH, 2 * D_FF), jnp.float32),
        "f_w_down": dense(ks[20], (DEPTH, D_FF, D_MODEL), D_FF, beta),
        "ple_w_proj": dense(ks[21], (DEPTH, PLE_DIM, D_MODEL), PLE_DIM, beta),
        "ple_w_gate": dense(ks[22], (DEPTH, D_MODEL, D_MODEL), D_MODEL),
        "ln_g": 1.0 + 0.02 * nrm(ks[23], (DEPTH, 2, D_MODEL), jnp.float32),
        "ln_b": 0.02 * nrm(ks[24], (DEPTH, 2, D_MODEL), jnp.float32),
    }


def reference(x, p, e_w_in, e_conv_a_w, e_conv_a_b, e_ln_a_g, e_ln_a_b, e_conv_b_w, e_conv_b_b,
              e_dt_bias, e_a_log, e_d_skip, e_norm_b_g, e_w_out, o_w_in, o_conv_w, o_w_out,
              f_w_up, f_conv_w, f_conv_b, f_w_down, ple_w_proj, ple_w_gate, ln_g, ln_b):
    alpha = (2.0 * DEPTH) ** 0.25
    o_a = 2 * D_A
    o_x = o_a + D_B
    o_dt = o_x + XBC_DIM
    for i in range(DEPTH):
        j = i // 2
        if i % 2 == 0:
            u = x @ e_w_in[j]
            ya = _conformer_conv(u[..., :o_a], e_conv_a_w[j], e_conv_a_b[j], e_ln_a_g[j], e_ln_a_b[j])
            yb = _mamba2(u[..., o_a:o_x], u[..., o_x:o_dt], u[..., o_dt:], e_conv_b_w[j], e_conv_b_b[j],
                         e_dt_bias[j], e_a_log[j], e_d_skip[j], e_norm_b_g[j])
            mix = jnp.concatenate([ya, yb], axis=-1) @ e_w_out[j]
        else:
            mix = _short_conv(x, o_w_in[j], o_conv_w[j], o_w_out[j])
        x = _layer_norm(alpha * x + mix, ln_g[i, 0], ln_b[i, 0])
        ffn = _conv_ffn(x, f_w_up[i], f_conv_w[i], f_conv_b[i], f_w_down[i])
        ple = (p[i] @ ple_w_proj[i]) * jax.nn.sigmoid(x @ ple_w_gate[i])
        x = _layer_norm(alpha * x + ffn + ple, ln_g[i, 1], ln_b[i, 1])
    return x
```

```python
import numpy as np
import concourse.bass as bass
import concourse.mybir as mybir
from concourse.bass_utils import run_bass_kernel_spmd

F32 = mybir.dt.float32
BF16 = mybir.dt.bfloat16
AF = mybir.ActivationFunctionType
ALU = mybir.AluOpType

D = 1024
SEQ = 4096
DEPTH = 4
T = 512
NCH = D // 128
PLE = 256
DFF = 2816
NFF = DFF // 128
E_IN = 5136
LN_EPS = 1e-5
ALPHA = float((2.0 * DEPTH) ** 0.25)
WSLOT = 4096
NSLOT = 3
CONV_ROWS = 512


class Buf:
    __slots__ = ("name", "w", "r")

    def __init__(self, name=""):
        self.name = name
        self.w = None
        self.r = {}


class Emit:
    def __init__(self, nc):
        self.nc = nc
        self.eng = {"pe": nc.tensor, "act": nc.scalar, "dve": nc.vector, "pool": nc.gpsimd, "sp": nc.sync}
        self.sem = {k: nc.alloc_semaphore("s_" + k) for k in self.eng}
        self.cnt = {k: 0 for k in self.eng}
        self.waited = {}
        self.prog = {k: [] for k in self.eng}
        self.n_inst = 0
        self.n_wait = 0

    def dma_chan(self, name):
        key = "dma:" + name
        self.sem[key] = self.nc.alloc_semaphore("d_" + name)
        self.cnt[key] = 0
        return key

    def _need(self, e, deps):
        best = {}
        for k, c in deps:
            if best.get(k, -1) < c:
                best[k] = c
        for k, c in best.items():
            if self.waited.get((e, k), -1) >= c:
                continue
            self.prog[e].append(("w", self.sem[k], c))
            self.waited[(e, k)] = c
            self.n_wait += 1

    @staticmethod
    def _deps(reads, writes):
        deps = []
        for b in reads:
            if b.w is not None:
                deps.append(b.w)
        for b in writes:
            if b.w is not None:
                deps.append(b.w)
            deps.extend(b.r.items())
        return deps

    def op(self, e, fn, R=(), W=(), same_ok=False):
        deps = self._deps(R, W)
        if same_ok:
            deps = [d for d in deps if d[0] != e]
        self._need(e, deps)
        self.cnt[e] += 1
        c = self.cnt[e]
        self.prog[e].append(("i", fn, self.sem[e], 1))
        for b in R:
            b.r[e] = c
        for b in W:
            b.w = (e, c)
            b.r = {}
        self.n_inst += 1

    def dma(self, q, chan, out, in_, R=(), W=(), **kw):
        self._need(q, self._deps(R, W))
        self.cnt[chan] += 16
        c = self.cnt[chan]
        eng = self.eng[q]
        self.prog[q].append(("i", (lambda: eng.dma_start(out=out, in_=in_, **kw)), self.sem[chan], 16))
        for b in R:
            b.r[chan] = c
        for b in W:
            b.w = (chan, c)
            b.r = {}
        self.n_inst += 1

    def wait_all(self, e, bufs):
        deps = []
        for b in bufs:
            if b.w is not None:
                deps.append(b.w)
            deps.extend(b.r.items())
        self._need(e, deps)

    def barrier(self):
        ks = ["pe", "act", "dve", "pool"]
        for e in ks:
            self._need(e, [(k, self.cnt[k]) for k in ks if k != e and self.cnt[k] > 0])

    def flush(self):
        with self.nc.Block() as block:
            for k, reg in (("sp", block.sync), ("act", block.scalar), ("dve", block.vector),
                           ("pool", block.gpsimd), ("pe", block.tensor)):
                def body(eng, prog=self.prog[k]):
                    for it in prog:
                        if it[0] == "w":
                            eng.wait_ge(it[1], it[2])
                        else:
                            it[1]().then_inc(it[2], it[3])
                reg(body)

    def mm(self, out, lhsT, rhs, start, stop, R, W):
        nc = self.nc
        self.op("pe", lambda: nc.tensor.matmul(out, lhsT=lhsT, rhs=rhs, start=start, stop=stop),
                R=R, W=W, same_ok=True)

    def tr(self, out, in_, ident, R, W):
        nc = self.nc
        self.op("pe", lambda: nc.tensor.transpose(out, in_, ident), R=R, W=W, same_ok=True)

    def act(self, out, in_, func, R, W, bias=None, scale=None):
        nc = self.nc
        kw = {}
        if bias is not None:
            kw["bias"] = bias
        if scale is not None:
            kw["scale"] = scale
        self.op("act", lambda: nc.scalar.activation(out=out, in_=in_, func=func, **kw), R=R, W=W)

    def tt(self, e, out, in0, in1, op, R, W):
        eng = self.eng[e]
        self.op(e, lambda: eng.tensor_tensor(out=out, in0=in0, in1=in1, op=op), R=R, W=W)

    def stt(self, out, in0, scalar, in1, op0, op1, R, W):
        nc = self.nc
        self.op("dve", lambda: nc.vector.scalar_tensor_tensor(out=out, in0=in0, scalar=scalar, in1=in1,
                                                              op0=op0, op1=op1), R=R, W=W)

    def ts(self, e, out, in0, s1, s2, op0, op1, R, W):
        eng = self.eng[e]
        if op1 is None:
            self.op(e, lambda: eng.tensor_scalar(out=out, in0=in0, scalar1=s1, scalar2=None, op0=op0), R=R, W=W)
        else:
            self.op(e, lambda: eng.tensor_scalar(out=out, in0=in0, scalar1=s1, scalar2=s2, op0=op0, op1=op1),
                    R=R, W=W)

    def copy(self, e, out, in_, R, W):
        if e == "act":
            nc = self.nc
            self.op("act", lambda: nc.scalar.copy(out=out, in_=in_), R=R, W=W)
        else:
            eng = self.eng[e]
            self.op(e, lambda: eng.tensor_copy(out=out, in_=in_), R=R, W=W)

    def memset(self, e, ap, val, W):
        eng = self.eng[e]
        self.op(e, lambda: eng.memset(ap, val), W=W)

    def recip(self, out, in_, R, W):
        nc = self.nc
        self.op("dve", lambda: nc.vector.reciprocal(out=out, in_=in_), R=R, W=W)


class TL:
    def __init__(self, nc, name, shape, dtype, nb=1, psum=False):
        if psum:
            self.t = nc.alloc_psum_tensor(name, shape, dtype)
        else:
            self.t = nc.alloc_sbuf_tensor(name, shape, dtype)
        self.b = [Buf(f"{name}{i}") for i in range(nb)]
        self.shape = shape

    def __getitem__(self, k):
        return self.t[k]


def _even_in_groups():
    g = []
    for s, kind, c0 in ((1024, "a2", 0), (1536, "a2", 4), (0, "a1", 0), (512, "a1", 4),
                        (2048, "z", 0), (2560, "z", 4), (3072, "u", 0), (3584, "u", 4),
                        (4096, "u", 8), (4608, "u", 12)):
        g.append((list(range(s, s + 512)), [(kind, c0 + i) for i in range(4)]))
    return g


def _ffn_up_groups():
    g = []
    for i in range(5):
        g.append((list(range(512 * i, 512 * i + 512)), [("h1", 4 * i + k) for k in range(4)]))
        g.append((list(range(DFF + 512 * i, DFF + 512 * i + 512)), [("h2", 4 * i + k) for k in range(4)]))
    cols = list(range(2560, 2816)) + list(range(DFF + 2560, DFF + 2816))
    g.append((cols, [("h1", 20), ("h1", 21), ("h2", 20), ("h2", 21)]))
    return g


def layer_plan(i):
    j = i // 2
    plan = []
    if i % 2 == 0:
        for cols, tags in _even_in_groups():
            plan.append(dict(src="e_w_in", idx=j, cols=cols, KC=8, tags=tags, stage="ein"))
        plan.append(dict(src="e_w_in", idx=j, cols=list(range(5120, 5136)), KC=8, tags=[("dt", 0)], stage="edt"))
        for m in range(4):
            plan.append(dict(src="e_w_out", idx=j, cols=list(range(256 * m, 256 * m + 256)), KC=16,
                             tags=[("o", 2 * m), ("o", 2 * m + 1)], stage="eout"))
    else:
        for s, kind in ((2048, "v"), (2560, "v"), (1024, "cg"), (1536, "cg"), (0, "bg"), (512, "bg")):
            c0 = ((s % 1024) // 512) * 4
            plan.append(dict(src="o_w_in", idx=j, cols=list(range(s, s + 512)), KC=8,
                             tags=[(kind, c0 + k) for k in range(4)], stage="oin"))
        for m in range(2):
            plan.append(dict(src="o_w_out", idx=j, cols=list(range(512 * m, 512 * m + 512)), KC=8,
                             tags=[("o", 4 * m + k) for k in range(4)], stage="oout"))
    for cols, tags in _ffn_up_groups():
        plan.append(dict(src="f_w_up", idx=i, cols=cols, KC=8, tags=tags, stage="fup"))
    for m in range(8):
        plan.append(dict(src="f_w_down", idx=i, cols=list(range(128 * m, 128 * m + 128)), KC=NFF,
                         tags=[("d", m)], stage="fdown"))
    for m in range(2):
        plan.append(dict(src="ple_w_gate", idx=i, cols=list(range(512 * m, 512 * m + 512)), KC=8,
                         tags=[("g", 4 * m + k) for k in range(4)], stage="pgate"))
    plan.append(dict(src="ple_w_proj", idx=i, cols=list(range(1024)), KC=2, tags=[("p", k) for k in range(8)],
                     stage="pproj"))
    return plan


def full_plan():
    off = 0
    plans = []
    for i in range(DEPTH):
        p = layer_plan(i)
        for g in p:
            g["G"] = len(g["cols"])
            g["off"] = off
            n = 128 * g["KC"] * g["G"]
            off += n
        blk = CONV_ROWS * 2048
        off = ((off + blk - 1) // blk) * blk
        plans.append(p)
    return plans, off


_PLANS, _WTOTAL = full_plan()


def pack_weights(inp):
    flat = np.zeros(_WTOTAL, dtype=np.float32)
    for p in _PLANS:
        for g in p:
            Wm = np.asarray(inp[g["src"]][g["idx"]])
            sub = Wm[:, g["cols"]]
            sub = sub.reshape(g["KC"], 128, g["G"]).transpose(1, 0, 2)
            n = sub.size
            flat[g["off"]:g["off"] + n] = sub.reshape(-1)
    return flat.reshape(-1, 2048)


def _smalls_layout():
    lay = {}
    off = 0

    def add(name, n):
        nonlocal off
        lay[name] = off
        off += n
    for j in range(2):
        add(f"caw{j}", 8 * 31); add(f"cab{j}", 8); add(f"lag{j}", 8); add(f"lab{j}", 8)
        add(f"cbw{j}", 16 * 4); add(f"cbb{j}", 16); add(f"nbg{j}", 8); add(f"dsk{j}", 8)
        add(f"dtb{j}", 16); add(f"alog{j}", 16)
        add(f"ocw{j}", 8 * 3)
    for i in range(DEPTH):
        add(f"fcw{i}", 44 * 3); add(f"fcb{i}", 44)
        add(f"g1_{i}", 8); add(f"b1_{i}", 8); add(f"g2_{i}", 8); add(f"b2_{i}", 8)
    return lay, off


_SL, _NS = _smalls_layout()


def pack_smalls(inp):
    s = np.zeros((128, _NS), dtype=np.float32)

    def vec(name, v):
        v = np.asarray(v, dtype=np.float32)
        n = v.shape[0] // 128
        s[:, _SL[name]:_SL[name] + n] = v.reshape(n, 128).T

    def conv(name, w):
        w = np.asarray(w, dtype=np.float32)
        K, C = w.shape
        n = C // 128
        s[:, _SL[name]:_SL[name] + n * K] = w.reshape(K, n, 128).transpose(2, 1, 0).reshape(128, n * K)

    def row(name, v):
        v = np.asarray(v, dtype=np.float32)
        s[:, _SL[name]:_SL[name] + v.shape[0]] = v[None, :]

    for j in range(2):
        conv(f"caw{j}", inp["e_conv_a_w"][j]); vec(f"cab{j}", inp["e_conv_a_b"][j])
        vec(f"lag{j}", inp["e_ln_a_g"][j]); vec(f"lab{j}", inp["e_ln_a_b"][j])
        conv(f"cbw{j}", inp["e_conv_b_w"][j]); vec(f"cbb{j}", inp["e_conv_b_b"][j])
        vec(f"nbg{j}", inp["e_norm_b_g"][j])
        vec(f"dsk{j}", np.repeat(np.asarray(inp["e_d_skip"][j]), 64))
        row(f"dtb{j}", inp["e_dt_bias"][j]); row(f"alog{j}", inp["e_a_log"][j])
        conv(f"ocw{j}", inp["o_conv_w"][j])
    for i in range(DEPTH):
        conv(f"fcw{i}", inp["f_conv_w"][i]); vec(f"fcb{i}", inp["f_conv_b"][i])
        vec(f"g1_{i}", inp["ln_g"][i, 0]); vec(f"b1_{i}", inp["ln_b"][i, 0])
        vec(f"g2_{i}", inp["ln_g"][i, 1]); vec(f"b2_{i}", inp["ln_b"][i, 1])
    return s


def make_consts():
    c = np.zeros((128, 5, 128), dtype=np.float32)
    j = np.arange(128)[:, None]
    l = np.arange(128)[None, :]
    c[:, 0] = (j == l)
    c[:, 1] = (j <= l)
    c[:, 2] = (j > l)
    c[:, 3] = 1.0
    c[:, 4] = 1.0 / 1024.0
    return c.reshape(128, 640)


F32R = mybir.dt.float32r
GRAN = 1024
NPE_A = 15


class View:
    def __init__(self, ap, grans_per_chunk):
        self.ap = ap
        self._g = grans_per_chunk
        self.b = []
        seen = set()
        for gl in grans_per_chunk:
            for g in gl:
                if id(g) not in seen:
                    seen.add(id(g))
                    self.b.append(g)

    def bl(self, c):
        return self._g[c]

    def __getitem__(self, k):
        return self.ap[k]


def _tl_bl(self, c):
    return [self.b[c]]


TL.bl = _tl_bl


class Prog:
    def __init__(self, n_tiles, layers, seq_len):
        self.n_tiles = n_tiles
        self.layers = layers
        self.seq_len = seq_len
        nc = bass.Bass("TRN2", target_bir_lowering=False)
        self.nc = nc
        em = Emit(nc)
        self.em = em
        L = seq_len
        self.xT = nc.dram_tensor("xT", [D, L], F32, kind="ExternalInput").ap()
        self.pT = nc.dram_tensor("pT", [DEPTH, PLE, L], F32, kind="ExternalInput").ap()
        self.wf = nc.dram_tensor("wf", [_WTOTAL // 2048, 2048], F32, kind="ExternalInput").ap()
        self.sm_d = nc.dram_tensor("smalls", [128, _NS], F32, kind="ExternalInput").ap()
        self.cs_d = nc.dram_tensor("consts", [128, 640], F32, kind="ExternalInput").ap()
        self.yT = nc.dram_tensor("yT", [D, L], F32, kind="ExternalOutput").ap()
        self.wsb = nc.dram_tensor("wsb", [_WTOTAL // 2048, 2048], BF16, kind="Internal").ap()
        self.conv_bufs = [Buf(f"cv{r}") for r in range(_WTOTAL // (2048 * CONV_ROWS))]
        self.ch_conv = em.dma_chan("conv")
        self.conv_done = set()

        self.sm = TL(nc, "sm", [128, _NS], F32)
        self.cs = TL(nc, "cs", [128, 5, 128], F32)
        self.identb = TL(nc, "identb", [128, 128], BF16)
        self.maskb = TL(nc, "maskb", [128, 128], BF16)
        self.arow = TL(nc, "arow", [128, 2, 16], F32)
        self.epsb = TL(nc, "epsb", [128, 1], F32)
        self.xf = TL(nc, "xf", [128, NCH, T], F32, nb=NCH)
        self.xb = TL(nc, "xb", [128, NCH, T], BF16, nb=NCH)
        self.pb = TL(nc, "pb", [128, 2, T], BF16)
        self.wr = [TL(nc, f"wr{s}", [128, WSLOT], BF16) for s in range(NSLOT)]
        self.ch_w = [em.dma_chan(f"w{s}") for s in range(NSLOT)]
        self.ch_x = em.dma_chan("x")
        self.ch_p = em.dma_chan("p")
        self.ch_y = em.dma_chan("y")
        self.ch_c = em.dma_chan("c")
        self.banks = [TL(nc, f"bk{i}", [128, 512], F32, psum=True) for i in range(8)]
        self.bk_i = 0
        self.held = set()
        self.mean = TL(nc, "mean", [128, T], F32)
        self.rstd = TL(nc, "rstd", [128, T], F32)
        self.var = TL(nc, "var", [128, T], F32)
        self.sq = [TL(nc, f"sq{i}", [128, T], BF16) for i in range(3)]
        self.xbt = [TL(nc, f"xbt{i}", [128, T], BF16) for i in range(3)]
        self.sq_i = 0
        self.onesb = TL(nc, "onesb", [128, 128], BF16)
        self.stF = [TL(nc, f"stF{i}", [128, 44, 2], F32) for i in range(DEPTH)]
        self.stA = [TL(nc, f"stA{j}", [128, 8, 30], BF16) for j in range(2)]
        self.stB = [TL(nc, f"stB{j}", [128, 16, 3], BF16) for j in range(2)]
        self.stC = [TL(nc, f"stC{j}", [128, 8, 2], BF16) for j in range(2)]
        self.H = [TL(nc, f"H{j}", [128, 1024], F32) for j in range(2)]
        self.Hb = TL(nc, "Hb", [128, 1024], BF16)
        ARENA = 120 * 1024
        self.arena = nc.alloc_sbuf_tensor("arena", [128, ARENA // 2], BF16)
        self.arena_size = ARENA
        self.gran = [Buf(f"gr{k}") for k in range(ARENA // GRAN)]

        self.wseq_n = 0
        self.wissued = 0
        self.wflat_seq = []
        for t in range(n_tiles):
            for i in layers:
                for g in _PLANS[i]:
                    self.wflat_seq.append((t, i, g))

    def carve(self, specs):
        out = {}
        off = 0
        for name, shp, dt in specs:
            esz = 4 if dt == F32 else 2
            n = int(np.prod(shp))
            off = (off + 31) // 32 * 32
            a = self.arena[:, off // 2: off // 2 + n * esz // 2]
            if dt == F32:
                a = a.bitcast(F32)
            if len(shp) == 2:
                a = a.rearrange("p (a b) -> p a b", a=shp[0])
            elif len(shp) == 3:
                a = a.rearrange("p (a b c) -> p a b c", a=shp[0], b=shp[1])
            nchunk = shp[0] if len(shp) > 1 else 1
            cb = n * esz // nchunk
            grans = []
            for k in range(nchunk):
                s0 = off + k * cb
                e0 = s0 + cb - 1
                grans.append(self.gran[s0 // GRAN: e0 // GRAN + 1])
            out[name] = View(a, grans)
            off += n * esz
        assert off <= self.arena_size, (off, self.arena_size)
        return out

    def bank(self, hold=False):
        for _ in range(8):
            i = self.bk_i
            self.bk_i = (self.bk_i + 1) % 8
            if i not in self.held:
                if hold:
                    self.held.add(i)
                return self.banks[i]
        raise RuntimeError("all PSUM banks held")

    def release(self, bk):
        self.held.discard(self.banks.index(bk))

    def issue_conv(self, layer):
        if layer in self.conv_done or layer >= DEPTH:
            return
        self.conv_done.add(layer)
        p = _PLANS[layer]
        lo = p[0]["off"] // (2048 * CONV_ROWS)
        last = p[-1]
        hi = (last["off"] + 128 * last["KC"] * last["G"] + 2048 * CONV_ROWS - 1) // (2048 * CONV_ROWS)
        for r in range(lo, hi):
            self.em.dma("pool", self.ch_conv, self.wsb[r * CONV_ROWS:(r + 1) * CONV_ROWS, :],
                        self.wf[r * CONV_ROWS:(r + 1) * CONV_ROWS, :], W=[self.conv_bufs[r]])

    def _issue_w(self, n):
        t, i, g = self.wflat_seq[n]
        s = n % NSLOT
        KC, G = g["KC"], g["G"]
        cnt = 128 * KC * G
        r0 = g["off"] // (2048 * CONV_ROWS)
        r1 = (g["off"] + cnt - 1) // (2048 * CONV_ROWS)
        src = bass.AP(self.wsb.tensor, g["off"], [[KC * G, 128], [1, KC * G]])
        self.em.dma("sp", self.ch_w[s], self.wr[s][:, 0:KC * G], src,
                    R=[self.conv_bufs[r] for r in range(r0, r1 + 1)], W=self.wr[s].b)

    def wget(self, g):
        n = self.wseq_n
        assert self.wflat_seq[n][2] is g, (n, g["src"], self.wflat_seq[n][2]["src"])
        while self.wissued < min(len(self.wflat_seq), n + NSLOT):
            self._issue_w(self.wissued)
            self.wissued += 1
        self.wseq_n += 1
        s = n % NSLOT
        KC, G = g["KC"], g["G"]
        v = self.wr[s][:, 0:KC * G].rearrange("p (k g) -> p k g", k=KC)
        return self.wr[s], v

    def sms(self, name, idx):
        o = _SL[name] + idx
        return self.sm[:, o:o + 1]


class Stats:
    def __init__(self, P, want_mean=True):
        self.P = P
        self.want_mean = want_mean
        self.bm = P.bank(hold=True) if want_mean else None
        self.bq = P.bank(hold=True)
        self.n = 0

    def add(self, ap, bufs):
        P, em = self.P, self.P.em
        onesb = P.onesb[:, :]
        sq = P.sq[P.sq_i]
        xbt = P.xbt[P.sq_i]
        P.sq_i = (P.sq_i + 1) % 3
        c = self.n
        if c % 2 == 0:
            em.act(sq[:, :], ap, AF.Square, R=bufs, W=sq.b)
        else:
            em.tt("dve", sq[:, :], ap, ap, ALU.mult, R=bufs, W=sq.b)
        if self.want_mean:
            em.copy("act", xbt[:, :], ap, R=bufs, W=xbt.b)
            em.mm(self.bm[:, :], onesb, xbt[:, :], c == 0, c == NCH - 1, R=xbt.b + P.onesb.b, W=self.bm.b)
        em.mm(self.bq[:, :], onesb, sq[:, :], c == 0, c == NCH - 1, R=sq.b + P.onesb.b, W=self.bq.b)
        self.n += 1

    def finish(self):
        P, em = self.P, self.P.em
        assert self.n == NCH
        if self.want_mean:
            em.copy("act", P.mean[:, :], self.bm[:, :], R=self.bm.b, W=P.mean.b)
            em.act(P.var[:, :], self.bm[:, :], AF.Square, R=self.bm.b, W=P.var.b)
            em.tt("dve", P.var[:, :], self.bq[:, :], P.var[:, :], ALU.subtract, R=self.bq.b + P.var.b, W=P.var.b)
            em.act(P.var[:, :], P.var[:, :], AF.Sqrt, R=P.var.b + P.epsb.b, W=P.var.b, bias=P.epsb[:, 0:1])
            P.release(self.bm)
        else:
            em.act(P.var[:, :], self.bq[:, :], AF.Sqrt, R=self.bq.b + P.epsb.b, W=P.var.b, bias=P.epsb[:, 0:1])
        P.release(self.bq)
        em.recip(P.rstd[:, :], P.var[:, :], R=P.var.b, W=P.rstd.b)


def residual_ln(P, st, gname, bname):
    em = P.em
    st.finish()
    for c in range(NCH):
        xb_ = P.xf.bl(c)
        em.tt("dve", P.xf[:, c, :], P.xf[:, c, :], P.mean[:, :], ALU.subtract, R=xb_ + P.mean.b, W=xb_)
        em.tt("dve", P.xf[:, c, :], P.xf[:, c, :], P.rstd[:, :], ALU.mult, R=xb_ + P.rstd.b, W=xb_)
        em.act(P.xb[:, c, :], P.xf[:, c, :], AF.Identity, R=xb_ + P.sm.b, W=P.xb.bl(c),
               bias=P.sms(bname, c), scale=P.sms(gname, c))
        em.ts("pool", P.xf[:, c, :], P.xf[:, c, :], P.sms(gname, c), P.sms(bname, c), ALU.mult, ALU.add,
              R=xb_ + P.sm.b, W=xb_)


def build_diag(P, dst_ap, taps, wname, cidx, nt, W):
    em = P.em
    for n, k in enumerate(taps):
        em.ts("pool", dst_ap[:, n, :], P.identb[:, :], P.sms(wname, cidx * nt + k), 0.0, ALU.mult, ALU.add,
              R=P.identb.b + P.sm.b, W=W)


def dve_taps(P, acc_ap, acc_b, src_ap_fn, src_b, taps, wname, cidx, nt):
    em = P.em
    for k in taps:
        em.stt(acc_ap, src_ap_fn(k), P.sms(wname, cidx * nt + k), acc_ap, ALU.mult, ALU.add,
               R=src_b + acc_b + P.sm.b, W=acc_b)


def ffn_stage(P, i, tile):
    em = P.em
    A = P.carve([("g", [NFF, T], BF16), ("hs", [4, T + 2], F32), ("acc", [4, T], F32),
                 ("s1", [4, T], BF16), ("ffo", [NCH, T], F32)])
    plan = [g for g in _PLANS[i] if g["stage"] == "fup"]
    n_i = 0
    stF = P.stF[i]
    s1slot = {}
    fcw, fcb = f"fcw{i}", f"fcb{i}"
    for g in plan:
        slot, wv = P.wget(g)
        for mi, (kind, ch) in enumerate(g["tags"]):
            cidx = ch if kind == "h1" else NFF + ch
            bk = P.bank()
            for kc in range(8):
                em.mm(bk[:, :], wv[:, kc, mi * 128:(mi + 1) * 128], P.xb[:, kc, :], kc == 0, kc == 7,
                      R=slot.b + P.xb.bl(kc), W=bk.b)
            hi = n_i % 4
            n_i += 1
            hb = A["hs"].bl(hi)
            ab_ = A["acc"].bl(hi)
            em.copy("pool", A["hs"][:, hi, 0:2], stF[:, cidx, :], R=stF.b, W=hb)
            em.copy("act", A["hs"][:, hi, 2:T + 2], bk[:, :], R=bk.b, W=hb)
            em.act(A["acc"][:, hi, :], bk[:, :], AF.Identity, R=bk.b + P.sm.b, W=ab_,
                   bias=P.sms(fcb, cidx), scale=P.sms(fcw, cidx * 3 + 2))
            em.copy("pool", stF[:, cidx, :], A["hs"][:, hi, T:T + 2], R=hb, W=stF.b)
            dve_taps(P, A["acc"][:, hi, :], ab_, lambda k: A["hs"][:, hi, k:k + T], hb, [1, 0], fcw, cidx, 3)
            if kind == "h1":
                si = ch % 4
                s1slot[ch] = si
                em.act(A["s1"][:, si, :], A["acc"][:, hi, :], AF.Silu, R=ab_, W=A["s1"].bl(si))
            else:
                si = s1slot[ch]
                em.tt("dve", A["g"][:, ch, :], A["acc"][:, hi, :], A["s1"][:, si, :], ALU.mult,
                      R=ab_ + A["s1"].bl(si), W=A["g"].bl(ch))
    downs = [g for g in _PLANS[i] if g["stage"] == "fdown"]
    gate_groups = [g for g in _PLANS[i] if g["stage"] == "pgate"]
    proj_group = [g for g in _PLANS[i] if g["stage"] == "pproj"][0]
    for g in downs:
        slot, wv = P.wget(g)
        m = g["tags"][0][1]
        bk = P.bank()
        for kc in range(NFF):
            em.mm(bk[:, :], wv[:, kc, :], A["g"][:, kc, :], kc == 0, kc == NFF - 1,
                  R=slot.b + A["g"].bl(kc), W=bk.b)
        em.stt(P.xf[:, m, :], P.xf[:, m, :], ALPHA, bk[:, :], ALU.mult, ALU.add, R=bk.b + P.xf.bl(m), W=P.xf.bl(m))
    for g in gate_groups:
        slot, wv = P.wget(g)
        for mi, (kind, m) in enumerate(g["tags"]):
            bk = P.bank()
            for kc in range(8):
                em.mm(bk[:, :], wv[:, kc, mi * 128:(mi + 1) * 128], P.xb[:, kc, :], kc == 0, kc == 7,
                      R=slot.b + P.xb.bl(kc), W=bk.b)
            em.act(A["ffo"][:, m, :], bk[:, :], AF.Sigmoid, R=bk.b, W=A["ffo"].bl(m))
    slot, wv = P.wget(proj_group)
    st = Stats(P)
    for m in range(NCH):
        bk = P.bank()
        for kc in range(2):
            em.mm(bk[:, :], wv[:, kc, m * 128:(m + 1) * 128], P.pb[:, kc, :], kc == 0, kc == 1,
                  R=slot.b + P.pb.b, W=bk.b)
        em.tt("dve", A["ffo"][:, m, :], bk[:, :], A["ffo"][:, m, :], ALU.mult, R=bk.b + A["ffo"].bl(m), W=A["ffo"].bl(m))
        em.tt("dve", P.xf[:, m, :], P.xf[:, m, :], A["ffo"][:, m, :], ALU.add, R=P.xf.bl(m) + A["ffo"].bl(m), W=P.xf.bl(m))
        st.add(P.xf[:, m, :], P.xf.bl(m))
    residual_ln(P, st, f"g2_{i}", f"b2_{i}")


def odd_mixer(P, i, tile):
    em = P.em
    j = i // 2
    A = P.carve([("v", [NCH, T], BF16), ("cv", [NCH, T + 2], BF16), ("bg", [NCH, T], BF16),
                 ("mx", [NCH, T], BF16), ("acc", [2, T], F32)])
    stC = P.stC[j]
    ocw = f"ocw{j}"
    em.copy("pool", A["cv"][:, :, 0:2], stC[:, :, :], R=stC.b, W=A["cv"].b)
    for g in [g for g in _PLANS[i] if g["stage"] == "oin"]:
        slot, wv = P.wget(g)
        for mi, (kind, c) in enumerate(g["tags"]):
            bk = P.bank()
            for kc in range(8):
                em.mm(bk[:, :], wv[:, kc, mi * 128:(mi + 1) * 128], P.xb[:, kc, :], kc == 0, kc == 7,
                      R=slot.b + P.xb.bl(kc), W=bk.b)
            if kind == "v":
                em.copy("act", A["v"][:, c, :], bk[:, :], R=bk.b, W=A["v"].bl(c))
            elif kind == "cg":
                em.tt("dve", A["cv"][:, c, 2:T + 2], bk[:, :], A["v"][:, c, :], ALU.mult,
                      R=bk.b + A["v"].bl(c), W=A["cv"].bl(c))
            else:
                em.copy("act", A["bg"][:, c, :], bk[:, :], R=bk.b, W=A["bg"].bl(c))
    em.copy("pool", stC[:, :, :], A["cv"][:, :, T:T + 2], R=A["cv"].b, W=stC.b)
    for c in range(NCH):
        ai = c % 2
        ab_ = A["acc"].bl(ai)
        em.ts("dve", A["acc"][:, ai, :], A["cv"][:, c, 2:T + 2], P.sms(ocw, c * 3 + 2), None, ALU.mult, None,
              R=A["cv"].bl(c) + P.sm.b, W=ab_)
        dve_taps(P, A["acc"][:, ai, :], ab_, lambda k: A["cv"][:, c, k:k + T], A["cv"].bl(c), [1, 0], ocw, c, 3)
        em.tt("dve", A["mx"][:, c, :], A["acc"][:, ai, :], A["bg"][:, c, :], ALU.mult, R=ab_ + A["bg"].bl(c), W=A["mx"].bl(c))
    st = Stats(P)
    for g in [g for g in _PLANS[i] if g["stage"] == "oout"]:
        slot, wv = P.wget(g)
        for mi, (kind, m) in enumerate(g["tags"]):
            bk = P.bank()
            for kc in range(8):
                em.mm(bk[:, :], wv[:, kc, mi * 128:(mi + 1) * 128], A["mx"][:, kc, :], kc == 0, kc == 7,
                      R=slot.b + A["mx"].bl(kc), W=bk.b)
            em.stt(P.xf[:, m, :], P.xf[:, m, :], ALPHA, bk[:, :], ALU.mult, ALU.add, R=bk.b + P.xf.bl(m), W=P.xf.bl(m))
            st.add(P.xf[:, m, :], P.xf.bl(m))
    residual_ln(P, st, f"g1_{i}", f"b1_{i}")


def even_mixer(P, i, tile):
    em = P.em
    j = i // 2
    A = P.carve([
        ("sgya", [NCH, T], BF16),
        ("ab", [NCH, T + 30], BF16),
        ("zs", [NCH, T], BF16),
        ("ub", [16, T + 3], BF16),
        ("cf", [NCH, T], F32),
        ("ys", [NCH, T], F32),
        ("yb", [NCH, T], BF16),
        ("dga", [2, NPE_A if NPE_A else 1, 128], BF16),
        ("acc", [2, T], F32),
        ("xdt", [16, 64], BF16), ("xdd", [16, 64], BF16), ("btok", [4, 128], BF16),
        ("rhsu", [16, 128], F32), ("E", [16, 128], BF16), ("cbm", [4, 128], BF16),
        ("Wp", [16, 128], BF16), ("ytok", [8, 128], F32),
        ("dtt", [4, 16], F32), ("dta", [4, 16], F32), ("ex3", [1, 48], F32),
    ])
    stA, stB = P.stA[j], P.stB[j]
    H = P.H[j]
    caw, cbw = f"caw{j}", f"cbw{j}"
    em.copy("pool", A["ab"][:, :, 0:30], stA[:, :, :], R=stA.b, W=A["ab"].b)
    em.copy("pool", A["ub"][:, :, 0:3], stB[:, :, :], R=stB.b, W=A["ub"].b)
    for g in [g for g in _PLANS[i] if g["stage"] == "ein"]:
        slot, wv = P.wget(g)
        for mi, (kind, c) in enumerate(g["tags"]):
            bk = P.bank()
            for kc in range(8):
                em.mm(bk[:, :], wv[:, kc, mi * 128:(mi + 1) * 128], P.xb[:, kc, :], kc == 0, kc == 7,
                      R=slot.b + P.xb.bl(kc), W=bk.b)
            if kind == "a2":
                em.act(A["sgya"][:, c, :], bk[:, :], AF.Sigmoid, R=bk.b, W=A["sgya"].bl(c))
            elif kind == "a1":
                em.tt("dve", A["ab"][:, c, 30:T + 30], bk[:, :], A["sgya"][:, c, :], ALU.mult,
                      R=bk.b + A["sgya"].bl(c), W=A["ab"].bl(c))
            elif kind == "z":
                em.act(A["zs"][:, c, :], bk[:, :], AF.Silu, R=bk.b, W=A["zs"].bl(c))
            else:
                em.copy("act", A["ub"][:, c, 3:T + 3], bk[:, :], R=bk.b, W=A["ub"].bl(c))
                ai = c % 2
                em.act(A["acc"][:, ai, :], bk[:, :], AF.Identity, R=bk.b + P.sm.b, W=A["acc"].bl(ai),
                       bias=P.sms(f"cbb{j}", c), scale=P.sms(cbw, c * 4 + 3))
                dve_taps(P, A["acc"][:, ai, :], A["acc"].bl(ai), lambda k: A["ub"][:, c, k:k + T], A["ub"].bl(c),
                         [2, 1, 0], cbw, c, 4)
                em.copy("pool", stB[:, c, :], A["ub"][:, c, T:T + 3], R=A["ub"].bl(c), W=stB.b)
                em.act(A["ub"][:, c, 3:T + 3], A["acc"][:, ai, :], AF.Silu, R=A["acc"].bl(ai), W=A["ub"].bl(c))
    em.copy("pool", stA[:, :, :], A["ab"][:, :, T:T + 30], R=A["ab"].b, W=stA.b)
    g = [g for g in _PLANS[i] if g["stage"] == "edt"][0]
    slot, wv = P.wget(g)
    bkd = P.bank()
    for q in range(4):
        for kc in range(8):
            em.mm(bkd[:, q * 16:(q + 1) * 16], P.xb[:, kc, q * 128:(q + 1) * 128], wv[:, kc, :], kc == 0, kc == 7,
                  R=slot.b + P.xb.bl(kc), W=bkd.b)
    dtb = P.sm[:, _SL[f"dtb{j}"]:_SL[f"dtb{j}"] + 16]
    em.tt("dve", A["dtt"][:, :, :], bkd[:, 0:64].rearrange("p (q h) -> p q h", q=4),
          dtb.unsqueeze(1).to_broadcast([128, 4, 16]), ALU.add, R=bkd.b + P.sm.b, W=A["dtt"].b)
    em.act(A["dtt"][:, :, :], A["dtt"][:, :, :], AF.Exp, R=A["dtt"].b, W=A["dtt"].b)
    em.act(A["dtt"][:, :, :], A["dtt"][:, :, :], AF.Ln, R=A["dtt"].b, W=A["dtt"].b, bias=P.cs[:, 3, 0:1])
    em.tt("dve", A["dta"][:, :, :], A["dtt"][:, :, :], P.arow[:, j, :].unsqueeze(1).to_broadcast([128, 4, 16]),
          ALU.mult, R=A["dtt"].b + P.arow.b, W=A["dta"].b)

    st_a = Stats(P)

    def conv_a_chunk(c):
        di = c % 2
        cfb = A["cf"].bl(c)
        pe_taps = list(range(NPE_A))
        dv_taps = list(range(NPE_A, 31))
        if pe_taps:
            build_diag(P, A["dga"][:, di], pe_taps, caw, c, 31, A["dga"].bl(di))
            bk = P.bank()
            for n, k in enumerate(pe_taps):
                em.mm(bk[:, :], A["dga"][:, di, n, :], A["ab"][:, c, k:k + T], n == 0, n == len(pe_taps) - 1,
                      R=A["dga"].bl(di) + A["ab"].bl(c), W=bk.b)
            em.act(A["cf"][:, c, :], bk[:, :], AF.Identity, R=bk.b + P.sm.b, W=cfb, bias=P.sms(f"cab{j}", c))
        else:
            k = dv_taps.pop()
            em.ts("dve", A["cf"][:, c, :], A["ab"][:, c, k:k + T], P.sms(caw, c * 31 + k), P.sms(f"cab{j}", c),
                  ALU.mult, ALU.add, R=A["ab"].bl(c) + P.sm.b, W=cfb)
        dve_taps(P, A["cf"][:, c, :], cfb, lambda k: A["ab"][:, c, k:k + T], A["ab"].bl(c), dv_taps, caw, c, 31)
        st_a.add(A["cf"][:, c, :], cfb)

    U = P.cs[:, 1, :]
    SLm = P.cs[:, 2, :]
    ones = P.cs[:, 3, :]
    identf = P.cs[:, 0, :]
    csb = P.cs.b
    for q in range(4):
        tq = slice(3 + q * 128, 3 + (q + 1) * 128)
        bA = P.bank()
        bB = P.bank()
        pa = bA[:, :].bitcast(BF16).rearrange("p (c l) -> p c l", c=8)
        pb_ = bB[:, 0:256].bitcast(BF16).rearrange("p (c l) -> p c l", c=4)
        for c in range(8):
            em.tr(pa[:, c, :], A["ub"][:, c, tq], P.identb[:, :], R=A["ub"].bl(c) + P.identb.b, W=bA.b)
        for c in range(4):
            em.tr(pb_[:, c, :], A["ub"][:, 8 + c, tq], P.identb[:, :], R=A["ub"].bl(8 + c) + P.identb.b, W=bB.b)
        em.tt("dve", A["xdt"][:, :, :], bA[:, :].bitcast(BF16).rearrange("p (h d) -> p h d", h=16),
              A["dtt"][:, q, :].unsqueeze(2).to_broadcast([128, 16, 64]), ALU.mult,
              R=bA.b + A["dtt"].b, W=A["xdt"].b)
        em.copy("act", A["btok"][:, :, :], pb_, R=bB.b, W=A["btok"].b)
        bS = P.bank()
        dta_q = A["dta"][:, q, :]
        em.mm(bS[:, 0:16], U, dta_q, True, True, R=csb + A["dta"].b, W=bS.b)
        em.mm(bS[:, 16:32], SLm, dta_q, True, True, R=csb + A["dta"].b, W=bS.b)
        em.mm(bS[:, 32:48], ones, dta_q, True, True, R=csb + A["dta"].b, W=bS.b)
        em.act(A["ex3"][:, 0, :], bS[:, 0:48], AF.Exp, R=bS.b, W=A["ex3"].b)
        ex3 = A["ex3"][:, 0, :]
        em.tt("dve", A["xdd"][:, :, :], A["xdt"][:, :, :],
              ex3[:, 16:32].unsqueeze(2).to_broadcast([128, 16, 64]), ALU.mult,
              R=A["xdt"].b + A["ex3"].b, W=A["xdd"].b)
        em.tt("dve", A["rhsu"][:, :, :], dta_q.unsqueeze(2).to_broadcast([128, 16, 128]),
              U.unsqueeze(1).to_broadcast([128, 16, 128]), ALU.mult, R=A["dta"].b + csb, W=A["rhsu"].b)
        for b4 in range(4):
            bk = P.bank()
            em.mm(bk[:, :], SLm,
                  A["rhsu"][:, 4 * b4:4 * b4 + 4, :].rearrange("p h l -> p (h l)"), True, True,
                  R=csb + A["rhsu"].b, W=bk.b)
            em.act(A["E"][:, 4 * b4:4 * b4 + 4, :].rearrange("p h l -> p (h l)"), bk[:, :], AF.Exp, R=bk.b, W=A["E"].b)
        bC = P.bank()
        for gg in range(4):
            em.mm(bC[:, gg * 128:(gg + 1) * 128], A["ub"][:, 8 + gg, tq], A["ub"][:, 12 + gg, tq], True, True,
                  R=A["ub"].bl(8 + gg) + A["ub"].bl(12 + gg), W=bC.b)
        em.tt("dve", A["cbm"][:, :, :], bC[:, :].rearrange("p (g l) -> p g l", g=4),
              P.maskb[:, :].unsqueeze(1).to_broadcast([128, 4, 128]), ALU.mult, R=bC.b + P.maskb.b, W=A["cbm"].b)
        em.tt("dve", A["Wp"][:, :, :].rearrange("p (g h) l -> p g h l", g=4),
              A["E"][:, :, :].rearrange("p (g h) l -> p g h l", g=4),
              A["cbm"][:, :, :].unsqueeze(2).to_broadcast([128, 4, 4, 128]), ALU.mult,
              R=A["E"].b + A["cbm"].b, W=A["Wp"].b)
        bY0, bY1 = P.bank(), P.bank()
        for h in range(16):
            bk = bY0 if h < 8 else bY1
            em.mm(bk[:, (h % 8) * 64:(h % 8 + 1) * 64], A["Wp"][:, h, :], A["xdt"][:, h, :], True, True,
                  R=A["Wp"].b + A["xdt"].b, W=bk.b)
        bO0, bO1 = P.bank(), P.bank()
        for gg in range(4):
            bk = bO0 if gg < 2 else bO1
            em.mm(bk[:, (gg % 2) * 256:(gg % 2 + 1) * 256], A["ub"][:, 12 + gg, tq], P.Hb[:, gg * 256:(gg + 1) * 256],
                  True, True, R=A["ub"].bl(12 + gg) + P.Hb.b, W=bk.b)
        for hh, (bo, by) in enumerate(((bO0, bY0), (bO1, bY1))):
            yt = A["ytok"][:, 4 * hh:4 * hh + 4, :].rearrange("p c l -> p (c l)")
            em.tt("dve", yt.rearrange("p (h d) -> p h d", h=8), bo[:, :].rearrange("p (h d) -> p h d", h=8),
                  ex3[:, 8 * hh:8 * hh + 8].unsqueeze(2).to_broadcast([128, 8, 64]), ALU.mult,
                  R=bo.b + A["ex3"].b, W=A["ytok"].b)
            em.tt("dve", yt, yt, by[:, :], ALU.add, R=by.b + A["ytok"].b, W=A["ytok"].b)
        bT0, bT1 = P.bank(), P.bank()
        for c in range(8):
            bk = bT0 if c < 4 else bT1
            em.tr(bk[:, (c % 4) * 128:(c % 4 + 1) * 128], A["ytok"][:, c, :], identf, R=A["ytok"].b + csb, W=bk.b)
        for hh, bk in enumerate((bT0, bT1)):
            wb_ = []
            for c in range(4 * hh, 4 * hh + 4):
                wb_ += A["ys"].bl(c)
            em.copy("act", A["ys"][:, 4 * hh:4 * hh + 4, q * 128:(q + 1) * 128],
                    bk[:, :].rearrange("p (c l) -> p c l", c=4), R=bk.b, W=wb_)
        bH0, bH1 = P.bank(), P.bank()
        for gg in range(4):
            bk = bH0 if gg < 2 else bH1
            em.mm(bk[:, (gg % 2) * 256:(gg % 2 + 1) * 256], A["btok"][:, gg, :],
                  A["xdd"][:, 4 * gg:4 * gg + 4, :].rearrange("p h d -> p (h d)"), True, True,
                  R=A["btok"].b + A["xdd"].b, W=bk.b)
        em.tt("dve", H[:, :].rearrange("p (h d) -> p h d", h=16), H[:, :].rearrange("p (h d) -> p h d", h=16),
              ex3[:, 32:48].unsqueeze(2).to_broadcast([128, 16, 64]), ALU.mult, R=H.b + A["ex3"].b, W=H.b)
        for hh, bk in enumerate((bH0, bH1)):
            em.tt("dve", H[:, hh * 512:(hh + 1) * 512], H[:, hh * 512:(hh + 1) * 512], bk[:, :], ALU.add,
                  R=H.b + bk.b, W=H.b)
        em.copy("act", P.Hb[:, :], H[:, :], R=H.b, W=P.Hb.b)
        conv_a_chunk(2 * q)
        conv_a_chunk(2 * q + 1)
    st_a.finish()
    for c in range(NCH):
        cfb = A["cf"].bl(c)
        em.tt("dve", A["cf"][:, c, :], A["cf"][:, c, :], P.mean[:, :], ALU.subtract, R=cfb + P.mean.b, W=cfb)
        em.tt("dve", A["cf"][:, c, :], A["cf"][:, c, :], P.rstd[:, :], ALU.mult, R=cfb + P.rstd.b, W=cfb)
        em.act(A["sgya"][:, c, :], A["cf"][:, c, :], AF.Silu, R=cfb + P.sm.b, W=A["sgya"].bl(c),
               bias=P.sms(f"lab{j}", c), scale=P.sms(f"lag{j}", c))
    st_b = Stats(P, want_mean=False)
    for c in range(NCH):
        ysb = A["ys"].bl(c)
        em.stt(A["ys"][:, c, :], A["ub"][:, c, 3:T + 3], P.sms(f"dsk{j}", c), A["ys"][:, c, :], ALU.mult, ALU.add,
               R=A["ub"].bl(c) + ysb + P.sm.b, W=ysb)
        em.tt("dve", A["ys"][:, c, :], A["ys"][:, c, :], A["zs"][:, c, :], ALU.mult, R=ysb + A["zs"].bl(c), W=ysb)
        st_b.add(A["ys"][:, c, :], ysb)
    st_b.finish()
    for c in range(NCH):
        em.stt(A["yb"][:, c, :], A["ys"][:, c, :], P.sms(f"nbg{j}", c), P.rstd[:, :], ALU.mult, ALU.mult,
               R=A["ys"].bl(c) + P.sm.b + P.rstd.b, W=A["yb"].bl(c))
    st = Stats(P)
    for g in [g for g in _PLANS[i] if g["stage"] == "eout"]:
        slot, wv = P.wget(g)
        for mi, (kind, m) in enumerate(g["tags"]):
            bk = P.bank()
            for kc in range(16):
                src = A["sgya"] if kc < 8 else A["yb"]
                em.mm(bk[:, :], wv[:, kc, mi * 128:(mi + 1) * 128], src[:, kc % 8, :], kc == 0, kc == 15,
                      R=slot.b + src.bl(kc % 8), W=bk.b)
            em.stt(P.xf[:, m, :], P.xf[:, m, :], ALPHA, bk[:, :], ALU.mult, ALU.add, R=bk.b + P.xf.bl(m), W=P.xf.bl(m))
            st.add(P.xf[:, m, :], P.xf.bl(m))
    residual_ln(P, st, f"g1_{i}", f"b1_{i}")


def build_program(n_tiles=SEQ // T, layers=(0, 1, 2, 3), seq_len=SEQ):
    P = Prog(n_tiles, list(layers), seq_len)
    nc, em = P.nc, P.em
    em.dma("sp", P.ch_c, P.sm[:, :], P.sm_d, W=P.sm.b)
    em.dma("sp", P.ch_c, P.cs[:, :, :], P.cs_d.rearrange("p (a b) -> p a b", a=5), W=P.cs.b)
    em.copy("dve", P.identb[:, :], P.cs[:, 0, :], R=P.cs.b, W=P.identb.b)
    em.copy("dve", P.maskb[:, :], P.cs[:, 1, :], R=P.cs.b, W=P.maskb.b)
    em.copy("dve", P.onesb[:, :], P.cs[:, 4, :], R=P.cs.b, W=P.onesb.b)
    em.memset("dve", P.epsb[:, :], LN_EPS, W=P.epsb.b)
    for j in range(2):
        al = P.sm[:, _SL[f"alog{j}"]:_SL[f"alog{j}"] + 16]
        em.act(P.arow[:, j, :], al, AF.Exp, R=P.sm.b, W=P.arow.b)
    em.ts("dve", P.arow[:, :, :], P.arow[:, :, :], -1.0, None, ALU.mult, None, R=P.arow.b, W=P.arow.b)
    for i in range(DEPTH):
        em.memset("pool", P.stF[i][:, :, :], 0.0, W=P.stF[i].b)
    for j in range(2):
        em.memset("pool", P.stA[j][:, :, :], 0.0, W=P.stA[j].b)
        em.memset("pool", P.stB[j][:, :, :], 0.0, W=P.stB[j].b)
        em.memset("pool", P.stC[j][:, :, :], 0.0, W=P.stC[j].b)
        em.memset("pool", P.H[j][:, :], 0.0, W=P.H[j].b)
    xTr = P.xT.rearrange("(c p) t -> p c t", p=128)
    yTr = P.yT.rearrange("(c p) t -> p c t", p=128)
    P.issue_conv(P.layers[0])
    for t in range(n_tiles):
        t0 = t * T
        em.dma("act", P.ch_x, P.xf[:, :, :], xTr[:, :, t0:t0 + T], W=P.xf.b)
        for c in range(NCH):
            em.copy("dve", P.xb[:, c, :], P.xf[:, c, :], R=P.xf.bl(c), W=P.xb.bl(c))
        for li, i in enumerate(P.layers):
            if t == 0 and li + 1 < len(P.layers):
                P.issue_conv(P.layers[li + 1])
            j = i // 2
            pTr = P.pT[i].rearrange("(c p) t -> p c t", p=128)
            em.dma("pool", P.ch_p, P.pb[:, :, :], pTr[:, :, t0:t0 + T], W=P.pb.b)
            if i % 2 == 0:
                em.copy("act", P.Hb[:, :], P.H[j][:, :], R=P.H[j].b, W=P.Hb.b)
                even_mixer(P, i, t)
            else:
                odd_mixer(P, i, t)
            ffn_stage(P, i, t)
        em.dma("act", P.ch_y, yTr[:, :, t0:t0 + T], P.xf[:, :, :], R=P.xf.b)
    em.wait_all("act", P.xf.b)
    em.flush()
    return P


_CACHE = {}


def kernel(**inputs):
    x = np.asarray(inputs["x"], dtype=np.float32)
    p = np.asarray(inputs["p"], dtype=np.float32)
    B = x.shape[0]
    wf = pack_weights(inputs)
    sm = pack_smalls(inputs)
    cs = make_consts()
    if "P" not in _CACHE:
        _CACHE["P"] = build_program()
    P = _CACHE["P"]
    in_maps = []
    for b in range(B):
        in_maps.append({
            "xT": np.ascontiguousarray(x[b].T),
            "pT": np.ascontiguousarray(p[:, b].transpose(0, 2, 1)),
            "wf": wf, "smalls": sm, "consts": cs,
        })
    res = run_bass_kernel_spmd(P.nc, in_maps, core_ids=list(range(B)))
    out = np.stack([np.ascontiguousarray(r["yT"].T) for r in res.results], axis=0)
    return out.astype(np.float32)
```

```python
import numpy as np
import concourse.bass as bass
import concourse.mybir as mybir
from concourse.bass_utils import run_bass_kernel_spmd

F32 = mybir.dt.float32
BF16 = mybir.dt.bfloat16
AF = mybir.ActivationFunctionType
ALU = mybir.AluOpType

D = 1024
SEQ = 4096
DEPTH = 4
T = 512
NCH = D // 128
PLE = 256
DFF = 2816
NFF = DFF // 128
E_IN = 5136
LN_EPS = 1e-5
ALPHA = float((2.0 * DEPTH) ** 0.25)
WSLOT = 4096
NSLOT = 3
CONV_ROWS = 512


class Buf:
    __slots__ = ("name", "w", "r")

    def __init__(self, name=""):
        self.name = name
        self.w = None
        self.r = {}


class Emit:
    def __init__(self, nc):
        self.nc = nc
        self.eng = {"pe": nc.tensor, "act": nc.scalar, "dve": nc.vector, "pool": nc.gpsimd, "sp": nc.sync}
        self.sem = {k: nc.alloc_semaphore("s_" + k) for k in self.eng}
        self.cnt = {k: 0 for k in self.eng}
        self.waited = {}
        self.prog = {k: [] for k in self.eng}
        self.n_inst = 0
        self.n_wait = 0

    def dma_chan(self, name):
        key = "dma:" + name
        self.sem[key] = self.nc.alloc_semaphore("d_" + name)
        self.cnt[key] = 0
        return key

    def _need(self, e, deps):
        best = {}
        for k, c in deps:
            if best.get(k, -1) < c:
                best[k] = c
        for k, c in best.items():
            if self.waited.get((e, k), -1) >= c:
                continue
            self.prog[e].append(("w", self.sem[k], c))
            self.waited[(e, k)] = c
            self.n_wait += 1

    @staticmethod
    def _deps(reads, writes):
        deps = []
        for b in reads:
            if b.w is not None:
                deps.append(b.w)
        for b in writes:
            if b.w is not None:
                deps.append(b.w)
            deps.extend(b.r.items())
        return deps

    def op(self, e, fn, R=(), W=(), same_ok=False):
        deps = self._deps(R, W)
        if same_ok:
            deps = [d for d in deps if d[0] != e]
        self._need(e, deps)
        self.cnt[e] += 1
        c = self.cnt[e]
        self.prog[e].append(("i", fn, self.sem[e], 1))
        for b in R:
            b.r[e] = c
        for b in W:
            b.w = (e, c)
            b.r = {}
        self.n_inst += 1

    def dma(self, q, chan, out, in_, R=(), W=(), **kw):
        self._need(q, self._deps(R, W))
        self.cnt[chan] += 16
        c = self.cnt[chan]
        eng = self.eng[q]
        self.prog[q].append(("i", (lambda: eng.dma_start(out=out, in_=in_, **kw)), self.sem[chan], 16))
        for b in R:
            b.r[chan] = c
        for b in W:
            b.w = (chan, c)
            b.r = {}
        self.n_inst += 1

    def wait_all(self, e, bufs):
        deps = []
        for b in bufs:
            if b.w is not None:
                deps.append(b.w)
            deps.extend(b.r.items())
        self._need(e, deps)

    def barrier(self):
        ks = ["pe", "act", "dve", "pool"]
        for e in ks:
            self._need(e, [(k, self.cnt[k]) for k in ks if k != e and self.cnt[k] > 0])

    def flush(self):
        with self.nc.Block() as block:
            for k, reg in (("sp", block.sync), ("act", block.scalar), ("dve", block.vector),
                           ("pool", block.gpsimd), ("pe", block.tensor)):
                def body(eng, prog=self.prog[k]):
                    for it in prog:
                        if it[0] == "w":
                            eng.wait_ge(it[1], it[2])
                        else:
                            it[1]().then_inc(it[2], it[3])
                reg(body)

    def mm(self, out, lhsT, rhs, start, stop, R, W):
        nc = self.nc
        self.op("pe", lambda: nc.tensor.matmul(out, lhsT=lhsT, rhs=rhs, start=start, stop=stop),
                R=R, W=W, same_ok=True)

    def tr(self, out, in_, ident, R, W):
        nc = self.nc
        self.op("pe", lambda: nc.tensor.transpose(out, in_, ident), R=R, W=W, same_ok=True)

    def act(self, out, in_, func, R, W, bias=None, scale=None):
        nc = self.nc
        kw = {}
        if bias is not None:
            kw["bias"] = bias
        if scale is not None:
            kw["scale"] = scale
        self.op("act", lambda: nc.scalar.activation(out=out, in_=in_, func=func, **kw), R=R, W=W)

    def tt(self, e, out, in0, in1, op, R, W):
        eng = self.eng[e]
        self.op(e, lambda: eng.tensor_tensor(out=out, in0=in0, in1=in1, op=op), R=R, W=W)

    def stt(self, out, in0, scalar, in1, op0, op1, R, W):
        nc = self.nc
        self.op("dve", lambda: nc.vector.scalar_tensor_tensor(out=out, in0=in0, scalar=scalar, in1=in1,
                                                              op0=op0, op1=op1), R=R, W=W)

    def ts(self, e, out, in0, s1, s2, op0, op1, R, W):
        eng = self.eng[e]
        if op1 is None:
            self.op(e, lambda: eng.tensor_scalar(out=out, in0=in0, scalar1=s1, scalar2=None, op0=op0), R=R, W=W)
        else:
            self.op(e, lambda: eng.tensor_scalar(out=out, in0=in0, scalar1=s1, scalar2=s2, op0=op0, op1=op1),
                    R=R, W=W)

    def copy(self, e, out, in_, R, W):
        if e == "act":
            nc = self.nc
            self.op("act", lambda: nc.scalar.copy(out=out, in_=in_), R=R, W=W)
        else:
            eng = self.eng[e]
            self.op(e, lambda: eng.tensor_copy(out=out, in_=in_), R=R, W=W)

    def memset(self, e, ap, val, W):
        eng = self.eng[e]
        self.op(e, lambda: eng.memset(ap, val), W=W)

    def recip(self, out, in_, R, W):
        nc = self.nc
        self.op("dve", lambda: nc.vector.reciprocal(out=out, in_=in_), R=R, W=W)


class TL:
    def __init__(self, nc, name, shape, dtype, nb=1, psum=False):
        if psum:
            self.t = nc.alloc_psum_tensor(name, shape, dtype)
        else:
            self.t = nc.alloc_sbuf_tensor(name, shape, dtype)
        self.b = [Buf(f"{name}{i}") for i in range(nb)]
        self.shape = shape

    def __getitem__(self, k):
        return self.t[k]


def _even_in_groups():
    g = []
    for s, kind, c0 in ((1024, "a2", 0), (1536, "a2", 4), (0, "a1", 0), (512, "a1", 4),
                        (2048, "z", 0), (2560, "z", 4), (3072, "u", 0), (3584, "u", 4),
                        (4096, "u", 8), (4608, "u", 12)):
        g.append((list(range(s, s + 512)), [(kind, c0 + i) for i in range(4)]))
    return g


def _ffn_up_groups():
    g = []
    for i in range(5):
        g.append((list(range(512 * i, 512 * i + 512)), [("h1", 4 * i + k) for k in range(4)]))
        g.append((list(range(DFF + 512 * i, DFF + 512 * i + 512)), [("h2", 4 * i + k) for k in range(4)]))
    cols = list(range(2560, 2816)) + list(range(DFF + 2560, DFF + 2816))
    g.append((cols, [("h1", 20), ("h1", 21), ("h2", 20), ("h2", 21)]))
    return g


def layer_plan(i):
    j = i // 2
    plan = []
    if i % 2 == 0:
        for cols, tags in _even_in_groups():
            plan.append(dict(src="e_w_in", idx=j, cols=cols, KC=8, tags=tags, stage="ein"))
        plan.append(dict(src="e_w_in", idx=j, cols=list(range(5120, 5136)), KC=8, tags=[("dt", 0)], stage="edt"))
        for m in range(4):
            plan.append(dict(src="e_w_out", idx=j, cols=list(range(256 * m, 256 * m + 256)), KC=16,
                             tags=[("o", 2 * m), ("o", 2 * m + 1)], stage="eout"))
    else:
        for s, kind in ((2048, "v"), (2560, "v"), (1024, "cg"), (1536, "cg"), (0, "bg"), (512, "bg")):
            c0 = ((s % 1024) // 512) * 4
            plan.append(dict(src="o_w_in", idx=j, cols=list(range(s, s + 512)), KC=8,
                             tags=[(kind, c0 + k) for k in range(4)], stage="oin"))
        for m in range(2):
            plan.append(dict(src="o_w_out", idx=j, cols=list(range(512 * m, 512 * m + 512)), KC=8,
                             tags=[("o", 4 * m + k) for k in range(4)], stage="oout"))
    for cols, tags in _ffn_up_groups():
        plan.append(dict(src="f_w_up", idx=i, cols=cols, KC=8, tags=tags, stage="fup"))
    for m in range(8):
        plan.append(dict(src="f_w_down", idx=i, cols=list(range(128 * m, 128 * m + 128)), KC=NFF,
                         tags=[("d", m)], stage="fdown"))
    for m in range(2):
        plan.append(dict(src="ple_w_gate", idx=i, cols=list(range(512 * m, 512 * m + 512)), KC=8,
                         tags=[("g", 4 * m + k) for k in range(4)], stage="pgate"))
    plan.append(dict(src="ple_w_proj", idx=i, cols=list(range(1024)), KC=2, tags=[("p", k) for k in range(8)],
                     stage="pproj"))
    return plan


def full_plan():
    off = 0
    plans = []
    for i in range(DEPTH):
        p = layer_plan(i)
        for g in p:
            g["G"] = len(g["cols"])
            g["off"] = off
            n = 128 * g["KC"] * g["G"]
            off += n
        blk = CONV_ROWS * 2048
        off = ((off + blk - 1) // blk) * blk
        plans.append(p)
    return plans, off


_PLANS, _WTOTAL = full_plan()


def pack_weights(inp):
    flat = np.zeros(_WTOTAL, dtype=np.float32)
    for p in _PLANS:
        for g in p:
            Wm = np.asarray(inp[g["src"]][g["idx"]])
            sub = Wm[:, g["cols"]]
            sub = sub.reshape(g["KC"], 128, g["G"]).transpose(1, 0, 2)
            n = sub.size
            flat[g["off"]:g["off"] + n] = sub.reshape(-1)
    return flat.reshape(-1, 2048)


def _smalls_layout():
    lay = {}
    off = 0

    def add(name, n):
        nonlocal off
        lay[name] = off
        off += n
    for j in range(2):
        add(f"caw{j}", 8 * 31); add(f"cab{j}", 8); add(f"lag{j}", 8); add(f"lab{j}", 8)
        add(f"cbw{j}", 16 * 4); add(f"cbb{j}", 16); add(f"nbg{j}", 8); add(f"dsk{j}", 8)
        add(f"dtb{j}", 16); add(f"alog{j}", 16)
        add(f"ocw{j}", 8 * 3)
    for i in range(DEPTH):
        add(f"fcw{i}", 44 * 3); add(f"fcb{i}", 44)
        add(f"g1_{i}", 8); add(f"b1_{i}", 8); add(f"g2_{i}", 8); add(f"b2_{i}", 8)
    return lay, off


_SL, _NS = _smalls_layout()


def pack_smalls(inp):
    s = np.zeros((128, _NS), dtype=np.float32)

    def vec(name, v):
        v = np.asarray(v, dtype=np.float32)
        n = v.shape[0] // 128
        s[:, _SL[name]:_SL[name] + n] = v.reshape(n, 128).T

    def conv(name, w):
        w = np.asarray(w, dtype=np.float32)
        K, C = w.shape
        n = C // 128
        s[:, _SL[name]:_SL[name] + n * K] = w.reshape(K, n, 128).transpose(2, 1, 0).reshape(128, n * K)

    def row(name, v):
        v = np.asarray(v, dtype=np.float32)
        s[:, _SL[name]:_SL[name] + v.shape[0]] = v[None, :]

    for j in range(2):
        conv(f"caw{j}", inp["e_conv_a_w"][j]); vec(f"cab{j}", inp["e_conv_a_b"][j])
        vec(f"lag{j}", inp["e_ln_a_g"][j]); vec(f"lab{j}", inp["e_ln_a_b"][j])
        conv(f"cbw{j}", inp["e_conv_b_w"][j]); vec(f"cbb{j}", inp["e_conv_b_b"][j])
        vec(f"nbg{j}", inp["e_norm_b_g"][j])
        vec(f"dsk{j}", np.repeat(np.asarray(inp["e_d_skip"][j]), 64))
        row(f"dtb{j}", inp["e_dt_bias"][j]); row(f"alog{j}", inp["e_a_log"][j])
        conv(f"ocw{j}", inp["o_conv_w"][j])
    for i in range(DEPTH):
        conv(f"fcw{i}", inp["f_conv_w"][i]); vec(f"fcb{i}", inp["f_conv_b"][i])
        vec(f"g1_{i}", inp["ln_g"][i, 0]); vec(f"b1_{i}", inp["ln_b"][i, 0])
        vec(f"g2_{i}", inp["ln_g"][i, 1]); vec(f"b2_{i}", inp["ln_b"][i, 1])
    return s


def make_consts():
    c = np.zeros((128, 5, 128), dtype=np.float32)
    j = np.arange(128)[:, None]
    l = np.arange(128)[None, :]
    c[:, 0] = (j == l)
    c[:, 1] = (j <= l)
    c[:, 2] = (j > l)
    c[:, 3] = 1.0
    c[:, 4] = 1.0 / 1024.0
    return c.reshape(128, 640)


F32R = mybir.dt.float32r
GRAN = 1024
NPE_A = 15


class View:
    def __init__(self, ap, grans_per_chunk):
        self.ap = ap
        self._g = grans_per_chunk
        self.b = []
        seen = set()
        for gl in grans_per_chunk:
            for g in gl:
                if id(g) not in seen:
                    seen.add(id(g))
                    self.b.append(g)

    def bl(self, c):
        return self._g[c]

    def __getitem__(self, k):
        return self.ap[k]


def _tl_bl(self, c):
    return [self.b[c]]


TL.bl = _tl_bl


class Prog:
    def __init__(self, n_tiles, layers, seq_len):
        self.n_tiles = n_tiles
        self.layers = layers
        self.seq_len = seq_len
        nc = bass.Bass("TRN2", target_bir_lowering=False)
        self.nc = nc
        em = Emit(nc)
        self.em = em
        L = seq_len
        self.xT = nc.dram_tensor("xT", [D, L], F32, kind="ExternalInput").ap()
        self.pT = nc.dram_tensor("pT", [DEPTH, PLE, L], F32, kind="ExternalInput").ap()
        self.wf = nc.dram_tensor("wf", [_WTOTAL // 2048, 2048], F32, kind="ExternalInput").ap()
        self.sm_d = nc.dram_tensor("smalls", [128, _NS], F32, kind="ExternalInput").ap()
        self.cs_d = nc.dram_tensor("consts", [128, 640], F32, kind="ExternalInput").ap()
        self.yT = nc.dram_tensor("yT", [D, L], F32, kind="ExternalOutput").ap()
        self.wsb = nc.dram_tensor("wsb", [_WTOTAL // 2048, 2048], BF16, kind="Internal").ap()
        self.conv_bufs = [Buf(f"cv{r}") for r in range(_WTOTAL // (2048 * CONV_ROWS))]
        self.ch_conv = em.dma_chan("conv")
        self.conv_done = set()

        self.sm = TL(nc, "sm", [128, _NS], F32)
        self.cs = TL(nc, "cs", [128, 5, 128], F32)
        self.identb = TL(nc, "identb", [128, 128], BF16)
        self.maskb = TL(nc, "maskb", [128, 128], BF16)
        self.arow = TL(nc, "arow", [128, 2, 16], F32)
        self.epsb = TL(nc, "epsb", [128, 1], F32)
        self.xf = TL(nc, "xf", [128, NCH, T], F32, nb=NCH)
        self.xb = TL(nc, "xb", [128, NCH, T], BF16, nb=NCH)
        self.pb = TL(nc, "pb", [128, 2, T], BF16)
        self.wr = [TL(nc, f"wr{s}", [128, WSLOT], BF16) for s in range(NSLOT)]
        self.ch_w = [em.dma_chan(f"w{s}") for s in range(NSLOT)]
        self.ch_x = em.dma_chan("x")
        self.ch_p = em.dma_chan("p")
        self.ch_y = em.dma_chan("y")
        self.ch_c = em.dma_chan("c")
        self.banks = [TL(nc, f"bk{i}", [128, 512], F32, psum=True) for i in range(8)]
        self.bk_i = 0
        self.held = set()
        self.mean = TL(nc, "mean", [128, T], F32)
        self.rstd = TL(nc, "rstd", [128, T], F32)
        self.var = TL(nc, "var", [128, T], F32)
        self.sq = [TL(nc, f"sq{i}", [128, T], BF16) for i in range(4)]
        self.xbt = [TL(nc, f"xbt{i}", [128, T], BF16) for i in range(4)]
        self.sq_i = 0
        self.onesb = TL(nc, "onesb", [128, 128], BF16)
        self.stF = [TL(nc, f"stF{i}", [128, 44, 2], F32) for i in range(DEPTH)]
        self.stA = [TL(nc, f"stA{j}", [128, 8, 30], BF16) for j in range(2)]
        self.stB = [TL(nc, f"stB{j}", [128, 16, 3], BF16) for j in range(2)]
        self.stC = [TL(nc, f"stC{j}", [128, 8, 2], BF16) for j in range(2)]
        self.H = [TL(nc, f"H{j}", [128, 1024], F32) for j in range(2)]
        self.Hb = TL(nc, "Hb", [128, 1024], BF16)
        ARENA = 120 * 1024
        self.arena = nc.alloc_sbuf_tensor("arena", [128, ARENA // 2], BF16)
        self.arena_size = ARENA
        self.gran = [Buf(f"gr{k}") for k in range(ARENA // GRAN)]

        self.wseq_n = 0
        self.wissued = 0
        self.wflat_seq = []
        for t in range(n_tiles):
            for i in layers:
                for g in _PLANS[i]:
                    self.wflat_seq.append((t, i, g))

    def carve(self, specs):
        out = {}
        off = 0
        top = 0
        offs = {}
        for spec in specs:
            name, shp, dt = spec[0], spec[1], spec[2]
            at = spec[3] if len(spec) > 3 else None
            esz = 4 if dt == F32 else 2
            n = int(np.prod(shp))
            if at == "top":
                off = top
            elif at is not None:
                off = offs[at]
            off = (off + 31) // 32 * 32
            offs[name] = off
            a = self.arena[:, off // 2: off // 2 + n * esz // 2]
            if dt == F32:
                a = a.bitcast(F32)
            if len(shp) == 2:
                a = a.rearrange("p (a b) -> p a b", a=shp[0])
            elif len(shp) == 3:
                a = a.rearrange("p (a b c) -> p a b c", a=shp[0], b=shp[1])
            nchunk = shp[0] if len(shp) > 1 else 1
            cb = n * esz // nchunk
            grans = []
            for k in range(nchunk):
                s0 = off + k * cb
                e0 = s0 + cb - 1
                grans.append(self.gran[s0 // GRAN: e0 // GRAN + 1])
            out[name] = View(a, grans)
            off += n * esz
            top = max(top, off)
        assert top <= self.arena_size, (top, self.arena_size)
        return out

    def bank(self, hold=False):
        for _ in range(8):
            i = self.bk_i
            self.bk_i = (self.bk_i + 1) % 8
            if i not in self.held:
                if hold:
                    self.held.add(i)
                return self.banks[i]
        raise RuntimeError("all PSUM banks held")

    def release(self, bk):
        self.held.discard(self.banks.index(bk))

    def issue_conv(self, layer):
        if layer in self.conv_done or layer >= DEPTH:
            return
        self.conv_done.add(layer)
        p = _PLANS[layer]
        lo = p[0]["off"] // (2048 * CONV_ROWS)
        last = p[-1]
        hi = (last["off"] + 128 * last["KC"] * last["G"] + 2048 * CONV_ROWS - 1) // (2048 * CONV_ROWS)
        for r in range(lo, hi):
            self.em.dma("pool", self.ch_conv, self.wsb[r * CONV_ROWS:(r + 1) * CONV_ROWS, :],
                        self.wf[r * CONV_ROWS:(r + 1) * CONV_ROWS, :], W=[self.conv_bufs[r]])

    def _issue_w(self, n):
        t, i, g = self.wflat_seq[n]
        s = n % NSLOT
        KC, G = g["KC"], g["G"]
        cnt = 128 * KC * G
        r0 = g["off"] // (2048 * CONV_ROWS)
        r1 = (g["off"] + cnt - 1) // (2048 * CONV_ROWS)
        src = bass.AP(self.wsb.tensor, g["off"], [[KC * G, 128], [1, KC * G]])
        self.em.dma("sp", self.ch_w[s], self.wr[s][:, 0:KC * G], src,
                    R=[self.conv_bufs[r] for r in range(r0, r1 + 1)], W=self.wr[s].b)

    def wget(self, g):
        n = self.wseq_n
        assert self.wflat_seq[n][2] is g, (n, g["src"], self.wflat_seq[n][2]["src"])
        while self.wissued < min(len(self.wflat_seq), n + NSLOT):
            self._issue_w(self.wissued)
            self.wissued += 1
        self.wseq_n += 1
        s = n % NSLOT
        KC, G = g["KC"], g["G"]
        v = self.wr[s][:, 0:KC * G].rearrange("p (k g) -> p k g", k=KC)
        return self.wr[s], v

    def sms(self, name, idx):
        o = _SL[name] + idx
        return self.sm[:, o:o + 1]


class Skew:
    def __init__(self, depth=1):
        self.depth = depth
        self.q = []

    def push(self, fn):
        self.q.append(fn)
        while len(self.q) > self.depth:
            self.q.pop(0)()

    def flush(self):
        while self.q:
            self.q.pop(0)()


class Stats:
    def __init__(self, P, want_mean=True):
        self.P = P
        self.want_mean = want_mean
        self.bm = P.bank(hold=True) if want_mean else None
        self.bq = P.bank(hold=True)
        self.n = 0
        self.pend = None

    def _flush(self):
        if self.pend is not None:
            self.pend()
            self.pend = None

    def add(self, ap, bufs):
        P, em = self.P, self.P.em
        onesb = P.onesb[:, :]
        sq = P.sq[P.sq_i]
        xbt = P.xbt[P.sq_i]
        P.sq_i = (P.sq_i + 1) % 4
        c = self.n
        if c % 2 == 0:
            em.act(sq[:, :], ap, AF.Square, R=bufs, W=sq.b)
        else:
            em.tt("dve", sq[:, :], ap, ap, ALU.mult, R=bufs, W=sq.b)
        if self.want_mean:
            em.copy("act", xbt[:, :], ap, R=bufs, W=xbt.b)
        self._flush()

        def mms():
            if self.want_mean:
                em.mm(self.bm[:, :], onesb, xbt[:, :], c == 0, c == NCH - 1, R=xbt.b + P.onesb.b, W=self.bm.b)
            em.mm(self.bq[:, :], onesb, sq[:, :], c == 0, c == NCH - 1, R=sq.b + P.onesb.b, W=self.bq.b)
        self.pend = mms
        self.n += 1

    def finish(self):
        P, em = self.P, self.P.em
        assert self.n == NCH
        self._flush()
        if self.want_mean:
            em.copy("act", P.mean[:, :], self.bm[:, :], R=self.bm.b, W=P.mean.b)
            em.act(P.var[:, :], self.bm[:, :], AF.Square, R=self.bm.b, W=P.var.b)
            em.tt("dve", P.var[:, :], self.bq[:, :], P.var[:, :], ALU.subtract, R=self.bq.b + P.var.b, W=P.var.b)
            em.act(P.var[:, :], P.var[:, :], AF.Sqrt, R=P.var.b + P.epsb.b, W=P.var.b, bias=P.epsb[:, 0:1])
            P.release(self.bm)
        else:
            em.act(P.var[:, :], self.bq[:, :], AF.Sqrt, R=self.bq.b + P.epsb.b, W=P.var.b, bias=P.epsb[:, 0:1])
        P.release(self.bq)
        em.recip(P.rstd[:, :], P.var[:, :], R=P.var.b, W=P.rstd.b)


def residual_ln(P, st, gname, bname):
    em = P.em
    st.finish()
    for c in range(NCH):
        xb_ = P.xf.bl(c)
        em.tt("dve", P.xf[:, c, :], P.xf[:, c, :], P.mean[:, :], ALU.subtract, R=xb_ + P.mean.b, W=xb_)
        em.tt("dve", P.xf[:, c, :], P.xf[:, c, :], P.rstd[:, :], ALU.mult, R=xb_ + P.rstd.b, W=xb_)
        em.act(P.xb[:, c, :], P.xf[:, c, :], AF.Identity, R=xb_ + P.sm.b, W=P.xb.bl(c),
               bias=P.sms(bname, c), scale=P.sms(gname, c))
        em.ts("pool", P.xf[:, c, :], P.xf[:, c, :], P.sms(gname, c), P.sms(bname, c), ALU.mult, ALU.add,
              R=xb_ + P.sm.b, W=xb_)


def build_diag(P, dst_ap, taps, wname, cidx, nt, W):
    em = P.em
    for n, k in enumerate(taps):
        em.ts("pool", dst_ap[:, n, :], P.identb[:, :], P.sms(wname, cidx * nt + k), 0.0, ALU.mult, ALU.add,
              R=P.identb.b + P.sm.b, W=W)


def dve_taps(P, acc_ap, acc_b, src_ap_fn, src_b, taps, wname, cidx, nt):
    em = P.em
    for k in taps:
        em.stt(acc_ap, src_ap_fn(k), P.sms(wname, cidx * nt + k), acc_ap, ALU.mult, ALU.add,
               R=src_b + acc_b + P.sm.b, W=acc_b)


def ffn_stage(P, i, tile):
    em = P.em
    A = P.carve([("g", [NFF, T], BF16), ("hs", [4, T + 2], F32), ("acc", [4, T], F32),
                 ("s1", [4, T], BF16), ("ffo", [NCH, T], F32)])
    plan = [g for g in _PLANS[i] if g["stage"] == "fup"]
    n_i = 0
    stF = P.stF[i]
    s1slot = {}
    sk = Skew(1)
    fcw, fcb = f"fcw{i}", f"fcb{i}"
    for g in plan:
        slot, wv = P.wget(g)
        for mi, (kind, ch) in enumerate(g["tags"]):
            cidx = ch if kind == "h1" else NFF + ch
            bk = P.bank()
            for kc in range(8):
                em.mm(bk[:, :], wv[:, kc, mi * 128:(mi + 1) * 128], P.xb[:, kc, :], kc == 0, kc == 7,
                      R=slot.b + P.xb.bl(kc), W=bk.b)
            hi = n_i % 4
            n_i += 1
            hb = A["hs"].bl(hi)
            ab_ = A["acc"].bl(hi)
            em.copy("pool", A["hs"][:, hi, 0:2], stF[:, cidx, :], R=stF.b, W=hb)
            em.copy("act", A["hs"][:, hi, 2:T + 2], bk[:, :], R=bk.b, W=hb)
            em.act(A["acc"][:, hi, :], bk[:, :], AF.Identity, R=bk.b + P.sm.b, W=ab_,
                   bias=P.sms(fcb, cidx), scale=P.sms(fcw, cidx * 3 + 2))
            em.copy("pool", stF[:, cidx, :], A["hs"][:, hi, T:T + 2], R=hb, W=stF.b)

            def tail(hi=hi, hb=hb, ab_=ab_, kind=kind, ch=ch, cidx=cidx):
                dve_taps(P, A["acc"][:, hi, :], ab_, lambda k: A["hs"][:, hi, k:k + T], hb, [1, 0], fcw, cidx, 3)
                if kind == "h1":
                    si = ch % 4
                    s1slot[ch] = si
                    em.act(A["s1"][:, si, :], A["acc"][:, hi, :], AF.Silu, R=ab_, W=A["s1"].bl(si))
                else:
                    si = s1slot[ch]
                    em.tt("dve", A["g"][:, ch, :], A["acc"][:, hi, :], A["s1"][:, si, :], ALU.mult,
                          R=ab_ + A["s1"].bl(si), W=A["g"].bl(ch))
            sk.push(tail)
    sk.flush()
    downs = [g for g in _PLANS[i] if g["stage"] == "fdown"]
    gate_groups = [g for g in _PLANS[i] if g["stage"] == "pgate"]
    proj_group = [g for g in _PLANS[i] if g["stage"] == "pproj"][0]
    for g in downs:
        slot, wv = P.wget(g)
        m = g["tags"][0][1]
        bk = P.bank()
        for kc in range(NFF):
            em.mm(bk[:, :], wv[:, kc, :], A["g"][:, kc, :], kc == 0, kc == NFF - 1,
                  R=slot.b + A["g"].bl(kc), W=bk.b)
        em.stt(P.xf[:, m, :], P.xf[:, m, :], ALPHA, bk[:, :], ALU.mult, ALU.add, R=bk.b + P.xf.bl(m), W=P.xf.bl(m))
    for g in gate_groups:
        slot, wv = P.wget(g)
        for mi, (kind, m) in enumerate(g["tags"]):
            bk = P.bank()
            for kc in range(8):
                em.mm(bk[:, :], wv[:, kc, mi * 128:(mi + 1) * 128], P.xb[:, kc, :], kc == 0, kc == 7,
                      R=slot.b + P.xb.bl(kc), W=bk.b)
            em.act(A["ffo"][:, m, :], bk[:, :], AF.Sigmoid, R=bk.b, W=A["ffo"].bl(m))
    slot, wv = P.wget(proj_group)
    st = Stats(P)
    for m in range(NCH):
        bk = P.bank()
        for kc in range(2):
            em.mm(bk[:, :], wv[:, kc, m * 128:(m + 1) * 128], P.pb[:, kc, :], kc == 0, kc == 1,
                  R=slot.b + P.pb.b, W=bk.b)
        em.tt("dve", A["ffo"][:, m, :], bk[:, :], A["ffo"][:, m, :], ALU.mult, R=bk.b + A["ffo"].bl(m), W=A["ffo"].bl(m))
        em.tt("dve", P.xf[:, m, :], P.xf[:, m, :], A["ffo"][:, m, :], ALU.add, R=P.xf.bl(m) + A["ffo"].bl(m), W=P.xf.bl(m))
        st.add(P.xf[:, m, :], P.xf.bl(m))
    residual_ln(P, st, f"g2_{i}", f"b2_{i}")


def odd_mixer(P, i, tile):
    em = P.em
    j = i // 2
    A = P.carve([("v", [NCH, T], BF16), ("cv", [NCH, T + 2], BF16), ("bg", [NCH, T], BF16),
                 ("mx", [NCH, T], BF16), ("acc", [2, T], F32)])
    stC = P.stC[j]
    ocw = f"ocw{j}"
    em.copy("pool", A["cv"][:, :, 0:2], stC[:, :, :], R=stC.b, W=A["cv"].b)
    for g in [g for g in _PLANS[i] if g["stage"] == "oin"]:
        slot, wv = P.wget(g)
        for mi, (kind, c) in enumerate(g["tags"]):
            bk = P.bank()
            for kc in range(8):
                em.mm(bk[:, :], wv[:, kc, mi * 128:(mi + 1) * 128], P.xb[:, kc, :], kc == 0, kc == 7,
                      R=slot.b + P.xb.bl(kc), W=bk.b)
            if kind == "v":
                em.copy("act", A["v"][:, c, :], bk[:, :], R=bk.b, W=A["v"].bl(c))
            elif kind == "cg":
                em.tt("dve", A["cv"][:, c, 2:T + 2], bk[:, :], A["v"][:, c, :], ALU.mult,
                      R=bk.b + A["v"].bl(c), W=A["cv"].bl(c))
            else:
                em.copy("act", A["bg"][:, c, :], bk[:, :], R=bk.b, W=A["bg"].bl(c))
    em.copy("pool", stC[:, :, :], A["cv"][:, :, T:T + 2], R=A["cv"].b, W=stC.b)
    for c in range(NCH):
        ai = c % 2
        ab_ = A["acc"].bl(ai)
        em.ts("dve", A["acc"][:, ai, :], A["cv"][:, c, 2:T + 2], P.sms(ocw, c * 3 + 2), None, ALU.mult, None,
              R=A["cv"].bl(c) + P.sm.b, W=ab_)
        dve_taps(P, A["acc"][:, ai, :], ab_, lambda k: A["cv"][:, c, k:k + T], A["cv"].bl(c), [1, 0], ocw, c, 3)
        em.tt("dve", A["mx"][:, c, :], A["acc"][:, ai, :], A["bg"][:, c, :], ALU.mult, R=ab_ + A["bg"].bl(c), W=A["mx"].bl(c))
    st = Stats(P)
    for g in [g for g in _PLANS[i] if g["stage"] == "oout"]:
        slot, wv = P.wget(g)
        for mi, (kind, m) in enumerate(g["tags"]):
            bk = P.bank()
            for kc in range(8):
                em.mm(bk[:, :], wv[:, kc, mi * 128:(mi + 1) * 128], A["mx"][:, kc, :], kc == 0, kc == 7,
                      R=slot.b + A["mx"].bl(kc), W=bk.b)
            em.stt(P.xf[:, m, :], P.xf[:, m, :], ALPHA, bk[:, :], ALU.mult, ALU.add, R=bk.b + P.xf.bl(m), W=P.xf.bl(m))
            st.add(P.xf[:, m, :], P.xf.bl(m))
    residual_ln(P, st, f"g1_{i}", f"b1_{i}")


def even_mixer(P, i, tile):
    em = P.em
    j = i // 2
    A = P.carve([
        ("sgya", [NCH, T], BF16),
        ("zs", [NCH, T], BF16),
        ("yb", [NCH, T], BF16, "zs"),
        ("ub", [16, T + 3], BF16),
        ("cf", [NCH, T], F32),
        ("ys", [NCH, T], F32, "cf"),
        ("acc", [4, T], F32, "top"),
        ("dtt", [4, 16], F32), ("dta", [4, 16], F32),
        ("ab", [NCH, T + 30], BF16),
        ("dga", [2, 31, 128], BF16),
        ("xdt", [2, 16, 64], BF16, "ab"), ("xdd", [2, 16, 64], BF16), ("btok", [2, 4, 128], BF16),
        ("rhsu", [2, 16, 128], F32), ("E", [2, 16, 128], BF16), ("cbm", [2, 4, 128], BF16),
        ("Wp", [2, 16, 128], BF16), ("ytok", [2, 8, 128], F32), ("ex3", [2, 48], F32),
    ])
    stA, stB = P.stA[j], P.stB[j]
    H = P.H[j]
    caw, cbw = f"caw{j}", f"cbw{j}"
    em.copy("pool", A["ab"][:, :, 0:30], stA[:, :, :], R=stA.b, W=A["ab"].b)
    em.copy("pool", A["ub"][:, :, 0:3], stB[:, :, :], R=stB.b, W=A["ub"].b)
    st_a = Stats(P)

    def conv_a_chunk(c):
        di = c % 2
        cfb = A["cf"].bl(c)
        dgb = A["dga"].bl(di)
        for k in range(31):
            em.ts("dve", A["dga"][:, di, k, :], P.identb[:, :], P.sms(caw, c * 31 + k), None, ALU.mult, None,
                  R=P.identb.b + P.sm.b, W=dgb)
        bk = P.bank()
        for k in range(31):
            em.mm(bk[:, :], A["dga"][:, di, k, :], A["ab"][:, c, k:k + T], k == 0, k == 30,
                  R=dgb + A["ab"].bl(c), W=bk.b)
        em.act(A["cf"][:, c, :], bk[:, :], AF.Identity, R=bk.b + P.sm.b, W=cfb, bias=P.sms(f"cab{j}", c))
        st_a.add(A["cf"][:, c, :], cfb)

    sk = Skew(1)
    conv_sched = {4: [0, 1], 5: [2, 3], 6: [4], 7: [5], 8: [6], 9: [7]}
    for gi, g in enumerate([g for g in _PLANS[i] if g["stage"] == "ein"]):
        slot, wv = P.wget(g)
        for mi, (kind, c) in enumerate(g["tags"]):
            bk = P.bank()
            for kc in range(8):
                em.mm(bk[:, :], wv[:, kc, mi * 128:(mi + 1) * 128], P.xb[:, kc, :], kc == 0, kc == 7,
                      R=slot.b + P.xb.bl(kc), W=bk.b)
            if kind == "a2":
                em.act(A["sgya"][:, c, :], bk[:, :], AF.Sigmoid, R=bk.b, W=A["sgya"].bl(c))
            elif kind == "a1":
                em.tt("dve", A["ab"][:, c, 30:T + 30], bk[:, :], A["sgya"][:, c, :], ALU.mult,
                      R=bk.b + A["sgya"].bl(c), W=A["ab"].bl(c))
            elif kind == "z":
                em.act(A["zs"][:, c, :], bk[:, :], AF.Silu, R=bk.b, W=A["zs"].bl(c))
            else:
                ai = c % 4
                em.copy("act", A["ub"][:, c, 3:T + 3], bk[:, :], R=bk.b, W=A["ub"].bl(c))
                em.act(A["acc"][:, ai, :], bk[:, :], AF.Identity, R=bk.b + P.sm.b, W=A["acc"].bl(ai),
                       bias=P.sms(f"cbb{j}", c), scale=P.sms(cbw, c * 4 + 3))

                def tail(c=c, ai=ai):
                    dve_taps(P, A["acc"][:, ai, :], A["acc"].bl(ai), lambda k: A["ub"][:, c, k:k + T],
                             A["ub"].bl(c), [2, 1, 0], cbw, c, 4)
                    em.copy("pool", stB[:, c, :], A["ub"][:, c, T:T + 3], R=A["ub"].bl(c), W=stB.b)
                    em.act(A["ub"][:, c, 3:T + 3], A["acc"][:, ai, :], AF.Silu, R=A["acc"].bl(ai), W=A["ub"].bl(c))
                sk.push(tail)
        if gi == 3:
            em.copy("pool", stA[:, :, :], A["ab"][:, :, T:T + 30], R=A["ab"].b, W=stA.b)
        for c in conv_sched.get(gi, []):
            conv_a_chunk(c)
    sk.flush()
    g = [g for g in _PLANS[i] if g["stage"] == "edt"][0]
    slot, wv = P.wget(g)
    bkd = P.bank()
    for q in range(4):
        for kc in range(8):
            em.mm(bkd[:, q * 16:(q + 1) * 16], P.xb[:, kc, q * 128:(q + 1) * 128], wv[:, kc, :], kc == 0, kc == 7,
                  R=slot.b + P.xb.bl(kc), W=bkd.b)
    dtb = P.sm[:, _SL[f"dtb{j}"]:_SL[f"dtb{j}"] + 16]
    em.tt("dve", A["dtt"][:, :, :], bkd[:, 0:64].rearrange("p (q h) -> p q h", q=4),
          dtb.unsqueeze(1).to_broadcast([128, 4, 16]), ALU.add, R=bkd.b + P.sm.b, W=A["dtt"].b)
    em.act(A["dtt"][:, :, :], A["dtt"][:, :, :], AF.Exp, R=A["dtt"].b, W=A["dtt"].b)
    em.act(A["dtt"][:, :, :], A["dtt"][:, :, :], AF.Ln, R=A["dtt"].b, W=A["dtt"].b, bias=P.cs[:, 3, 0:1])
    em.tt("dve", A["dta"][:, :, :], A["dtt"][:, :, :], P.arow[:, j, :].unsqueeze(1).to_broadcast([128, 4, 16]),
          ALU.mult, R=A["dtt"].b + P.arow.b, W=A["dta"].b)
    st_a.finish()
    for c in range(NCH):
        cfb = A["cf"].bl(c)
        em.tt("dve", A["cf"][:, c, :], A["cf"][:, c, :], P.mean[:, :], ALU.subtract, R=cfb + P.mean.b, W=cfb)
        em.tt("dve", A["cf"][:, c, :], A["cf"][:, c, :], P.rstd[:, :], ALU.mult, R=cfb + P.rstd.b, W=cfb)
        em.act(A["sgya"][:, c, :], A["cf"][:, c, :], AF.Silu, R=cfb + P.sm.b, W=A["sgya"].bl(c),
               bias=P.sms(f"lab{j}", c), scale=P.sms(f"lag{j}", c))

    U = P.cs[:, 1, :]
    SLm = P.cs[:, 2, :]
    ones = P.cs[:, 3, :]
    identf = P.cs[:, 0, :]
    csb = P.cs.b
    for q in range(4):
        qi = q % 2
        tq = slice(3 + q * 128, 3 + (q + 1) * 128)
        xdt, xdd, btok, rhsu, E_, cbm, Wp, ytok = (A[n][:, qi] for n in ("xdt", "xdd", "btok", "rhsu", "E", "cbm", "Wp", "ytok"))
        b_xdt, b_xdd, b_btok, b_rhsu, b_E, b_cbm, b_Wp, b_ytok, b_ex3 = (
            A[n].bl(qi) for n in ("xdt", "xdd", "btok", "rhsu", "E", "cbm", "Wp", "ytok", "ex3"))
        ex3 = A["ex3"][:, qi, :]
        bA = P.bank()
        bB = P.bank()
        pa = bA[:, :].bitcast(BF16).rearrange("p (c l) -> p c l", c=8)
        pb_ = bB[:, 0:256].bitcast(BF16).rearrange("p (c l) -> p c l", c=4)
        for c in range(8):
            em.tr(pa[:, c, :], A["ub"][:, c, tq], P.identb[:, :], R=A["ub"].bl(c) + P.identb.b, W=bA.b)
        for c in range(4):
            em.tr(pb_[:, c, :], A["ub"][:, 8 + c, tq], P.identb[:, :], R=A["ub"].bl(8 + c) + P.identb.b, W=bB.b)
        em.tt("dve", xdt, bA[:, :].bitcast(BF16).rearrange("p (h d) -> p h d", h=16),
              A["dtt"][:, q, :].unsqueeze(2).to_broadcast([128, 16, 64]), ALU.mult,
              R=bA.b + A["dtt"].b, W=b_xdt)
        em.copy("act", btok, pb_, R=bB.b, W=b_btok)
        bS = P.bank()
        dta_q = A["dta"][:, q, :]
        em.mm(bS[:, 0:16], U, dta_q, True, True, R=csb + A["dta"].b, W=bS.b)
        em.mm(bS[:, 16:32], SLm, dta_q, True, True, R=csb + A["dta"].b, W=bS.b)
        em.mm(bS[:, 32:48], ones, dta_q, True, True, R=csb + A["dta"].b, W=bS.b)
        em.act(ex3, bS[:, 0:48], AF.Exp, R=bS.b, W=b_ex3)
        em.tt("dve", xdd, xdt, ex3[:, 16:32].unsqueeze(2).to_broadcast([128, 16, 64]), ALU.mult,
              R=b_xdt + b_ex3, W=b_xdd)
        em.tt("dve", rhsu, dta_q.unsqueeze(2).to_broadcast([128, 16, 128]),
              U.unsqueeze(1).to_broadcast([128, 16, 128]), ALU.mult, R=A["dta"].b + csb, W=b_rhsu)
        for b4 in range(4):
            bk = P.bank()
            em.mm(bk[:, :], SLm, rhsu[:, 4 * b4:4 * b4 + 4, :].rearrange("p h l -> p (h l)"), True, True,
                  R=csb + b_rhsu, W=bk.b)
            em.act(E_[:, 4 * b4:4 * b4 + 4, :].rearrange("p h l -> p (h l)"), bk[:, :], AF.Exp, R=bk.b, W=b_E)
        bC = P.bank()
        for gg in range(4):
            em.mm(bC[:, gg * 128:(gg + 1) * 128], A["ub"][:, 8 + gg, tq], A["ub"][:, 12 + gg, tq], True, True,
                  R=A["ub"].bl(8 + gg) + A["ub"].bl(12 + gg), W=bC.b)
        em.tt("dve", cbm, bC[:, :].rearrange("p (g l) -> p g l", g=4),
              P.maskb[:, :].unsqueeze(1).to_broadcast([128, 4, 128]), ALU.mult, R=bC.b + P.maskb.b, W=b_cbm)
        em.tt("dve", Wp.rearrange("p (g h) l -> p g h l", g=4), E_.rearrange("p (g h) l -> p g h l", g=4),
              cbm.unsqueeze(2).to_broadcast([128, 4, 4, 128]), ALU.mult, R=b_E + b_cbm, W=b_Wp)
        bY0, bY1 = P.bank(), P.bank()
        for h in range(16):
            bk = bY0 if h < 8 else bY1
            em.mm(bk[:, (h % 8) * 64:(h % 8 + 1) * 64], Wp[:, h, :], xdt[:, h, :], True, True,
                  R=b_Wp + b_xdt, W=bk.b)
        bO0, bO1 = P.bank(), P.bank()
        for gg in range(4):
            bk = bO0 if gg < 2 else bO1
            em.mm(bk[:, (gg % 2) * 256:(gg % 2 + 1) * 256], A["ub"][:, 12 + gg, tq], P.Hb[:, gg * 256:(gg + 1) * 256],
                  True, True, R=A["ub"].bl(12 + gg) + P.Hb.b, W=bk.b)
        for hh, (bo, by) in enumerate(((bO0, bY0), (bO1, bY1))):
            yt = ytok[:, 4 * hh:4 * hh + 4, :].rearrange("p c l -> p (c l)")
            em.tt("dve", yt.rearrange("p (h d) -> p h d", h=8), bo[:, :].rearrange("p (h d) -> p h d", h=8),
                  ex3[:, 8 * hh:8 * hh + 8].unsqueeze(2).to_broadcast([128, 8, 64]), ALU.mult,
                  R=bo.b + b_ex3, W=b_ytok)
            em.tt("dve", yt, yt, by[:, :], ALU.add, R=by.b + b_ytok, W=b_ytok)
        bH0, bH1 = P.bank(), P.bank()
        for gg in range(4):
            bk = bH0 if gg < 2 else bH1
            em.mm(bk[:, (gg % 2) * 256:(gg % 2 + 1) * 256], btok[:, gg, :],
                  xdd[:, 4 * gg:4 * gg + 4, :].rearrange("p h d -> p (h d)"), True, True,
                  R=b_btok + b_xdd, W=bk.b)
        em.tt("dve", H[:, :].rearrange("p (h d) -> p h d", h=16), H[:, :].rearrange("p (h d) -> p h d", h=16),
              ex3[:, 32:48].unsqueeze(2).to_broadcast([128, 16, 64]), ALU.mult, R=H.b + b_ex3, W=H.b)
        for hh, bk in enumerate((bH0, bH1)):
            em.tt("dve", H[:, hh * 512:(hh + 1) * 512], H[:, hh * 512:(hh + 1) * 512], bk[:, :], ALU.add,
                  R=H.b + bk.b, W=H.b)
        em.copy("act", P.Hb[:, :], H[:, :], R=H.b, W=P.Hb.b)
        bT0, bT1 = P.bank(), P.bank()
        for c in range(8):
            bk = bT0 if c < 4 else bT1
            em.tr(bk[:, (c % 4) * 128:(c % 4 + 1) * 128], ytok[:, c, :], identf, R=b_ytok + csb, W=bk.b)
        for hh, bk in enumerate((bT0, bT1)):
            wb_ = []
            for c in range(4 * hh, 4 * hh + 4):
                wb_ += A["ys"].bl(c)
            em.copy("act", A["ys"][:, 4 * hh:4 * hh + 4, q * 128:(q + 1) * 128],
                    bk[:, :].rearrange("p (c l) -> p c l", c=4), R=bk.b, W=wb_)
    st_b = Stats(P, want_mean=False)
    for c in range(NCH):
        ysb = A["ys"].bl(c)
        em.stt(A["ys"][:, c, :], A["ub"][:, c, 3:T + 3], P.sms(f"dsk{j}", c), A["ys"][:, c, :], ALU.mult, ALU.add,
               R=A["ub"].bl(c) + ysb + P.sm.b, W=ysb)
        em.tt("dve", A["ys"][:, c, :], A["ys"][:, c, :], A["zs"][:, c, :], ALU.mult, R=ysb + A["zs"].bl(c), W=ysb)
        st_b.add(A["ys"][:, c, :], ysb)
    st_b.finish()
    for c in range(NCH):
        em.stt(A["yb"][:, c, :], A["ys"][:, c, :], P.sms(f"nbg{j}", c), P.rstd[:, :], ALU.mult, ALU.mult,
               R=A["ys"].bl(c) + P.sm.b + P.rstd.b, W=A["yb"].bl(c))
    st = Stats(P)
    for g in [g for g in _PLANS[i] if g["stage"] == "eout"]:
        slot, wv = P.wget(g)
        for mi, (kind, m) in enumerate(g["tags"]):
            bk = P.bank()
            for kc in range(16):
                src = A["sgya"] if kc < 8 else A["yb"]
                em.mm(bk[:, :], wv[:, kc, mi * 128:(mi + 1) * 128], src[:, kc % 8, :], kc == 0, kc == 15,
                      R=slot.b + src.bl(kc % 8), W=bk.b)
            em.stt(P.xf[:, m, :], P.xf[:, m, :], ALPHA, bk[:, :], ALU.mult, ALU.add, R=bk.b + P.xf.bl(m), W=P.xf.bl(m))
            st.add(P.xf[:, m, :], P.xf.bl(m))
    residual_ln(P, st, f"g1_{i}", f"b1_{i}")


def build_program(n_tiles=SEQ // T, layers=(0, 1, 2, 3), seq_len=SEQ):
    P = Prog(n_tiles, list(layers), seq_len)
    nc, em = P.nc, P.em
    em.dma("sp", P.ch_c, P.sm[:, :], P.sm_d, W=P.sm.b)
    em.dma("sp", P.ch_c, P.cs[:, :, :], P.cs_d.rearrange("p (a b) -> p a b", a=5), W=P.cs.b)
    em.copy("dve", P.identb[:, :], P.cs[:, 0, :], R=P.cs.b, W=P.identb.b)
    em.copy("dve", P.maskb[:, :], P.cs[:, 1, :], R=P.cs.b, W=P.maskb.b)
    em.copy("dve", P.onesb[:, :], P.cs[:, 4, :], R=P.cs.b, W=P.onesb.b)
    em.memset("dve", P.epsb[:, :], LN_EPS, W=P.epsb.b)
    for j in range(2):
        al = P.sm[:, _SL[f"alog{j}"]:_SL[f"alog{j}"] + 16]
        em.act(P.arow[:, j, :], al, AF.Exp, R=P.sm.b, W=P.arow.b)
    em.ts("dve", P.arow[:, :, :], P.arow[:, :, :], -1.0, None, ALU.mult, None, R=P.arow.b, W=P.arow.b)
    for i in range(DEPTH):
        em.memset("pool", P.stF[i][:, :, :], 0.0, W=P.stF[i].b)
    for j in range(2):
        em.memset("pool", P.stA[j][:, :, :], 0.0, W=P.stA[j].b)
        em.memset("pool", P.stB[j][:, :, :], 0.0, W=P.stB[j].b)
        em.memset("pool", P.stC[j][:, :, :], 0.0, W=P.stC[j].b)
        em.memset("pool", P.H[j][:, :], 0.0, W=P.H[j].b)
    xTr = P.xT.rearrange("(c p) t -> p c t", p=128)
    yTr = P.yT.rearrange("(c p) t -> p c t", p=128)
    P.issue_conv(P.layers[0])
    for t in range(n_tiles):
        t0 = t * T
        em.dma("act", P.ch_x, P.xf[:, :, :], xTr[:, :, t0:t0 + T], W=P.xf.b)
        for c in range(NCH):
            em.copy("dve", P.xb[:, c, :], P.xf[:, c, :], R=P.xf.bl(c), W=P.xb.bl(c))
        for li, i in enumerate(P.layers):
            if t == 0 and li + 1 < len(P.layers):
                P.issue_conv(P.layers[li + 1])
            j = i // 2
            pTr = P.pT[i].rearrange("(c p) t -> p c t", p=128)
            em.dma("pool", P.ch_p, P.pb[:, :, :], pTr[:, :, t0:t0 + T], W=P.pb.b)
            if i % 2 == 0:
                em.copy("act", P.Hb[:, :], P.H[j][:, :], R=P.H[j].b, W=P.Hb.b)
                even_mixer(P, i, t)
            else:
                odd_mixer(P, i, t)
            ffn_stage(P, i, t)
        em.dma("act", P.ch_y, yTr[:, :, t0:t0 + T], P.xf[:, :, :], R=P.xf.b)
    em.wait_all("act", P.xf.b)
    em.flush()
    return P


_CACHE = {}


def kernel(**inputs):
    x = np.asarray(inputs["x"], dtype=np.float32)
    p = np.asarray(inputs["p"], dtype=np.float32)
    B = x.shape[0]
    wf = pack_weights(inputs)
    sm = pack_smalls(inputs)
    cs = make_consts()
    if "P" not in _CACHE:
        _CACHE["P"] = build_program()
    P = _CACHE["P"]
    in_maps = []
    for b in range(B):
        in_maps.append({
            "xT": np.ascontiguousarray(x[b].T),
            "pT": np.ascontiguousarray(p[:, b].transpose(0, 2, 1)),
            "wf": wf, "smalls": sm, "consts": cs,
        })
    res = run_bass_kernel_spmd(P.nc, in_maps, core_ids=list(range(B)))
    out = np.stack([np.ascontiguousarray(r["yT"].T) for r in res.results], axis=0)
    return out.astype(np.float32)
```

```python
import numpy as np
import concourse.bass as bass
import concourse.mybir as mybir
from concourse.bass_utils import run_bass_kernel_spmd

F32 = mybir.dt.float32
BF16 = mybir.dt.bfloat16
AF = mybir.ActivationFunctionType
ALU = mybir.AluOpType

D = 1024
SEQ = 4096
DEPTH = 4
T = 512
NCH = D // 128
PLE = 256
DFF = 2816
NFF = DFF // 128
E_IN = 5136
LN_EPS = 1e-5
ALPHA = float((2.0 * DEPTH) ** 0.25)
WSLOT = 4096
NSLOT = 3
CONV_ROWS = 512


class Buf:
    __slots__ = ("name", "w", "r")

    def __init__(self, name=""):
        self.name = name
        self.w = None
        self.r = {}


class Emit:
    def __init__(self, nc):
        self.nc = nc
        self.eng = {"pe": nc.tensor, "act": nc.scalar, "dve": nc.vector, "pool": nc.gpsimd, "sp": nc.sync}
        self.sem = {k: nc.alloc_semaphore("s_" + k) for k in self.eng}
        self.cnt = {k: 0 for k in self.eng}
        self.waited = {}
        self.prog = {k: [] for k in self.eng}
        self.n_inst = 0
        self.n_wait = 0

    def dma_chan(self, name):
        key = "dma:" + name
        self.sem[key] = self.nc.alloc_semaphore("d_" + name)
        self.cnt[key] = 0
        return key

    def _need(self, e, deps):
        best = {}
        for k, c in deps:
            if best.get(k, -1) < c:
                best[k] = c
        for k, c in best.items():
            if self.waited.get((e, k), -1) >= c:
                continue
            self.prog[e].append(("w", self.sem[k], c))
            self.waited[(e, k)] = c
            self.n_wait += 1

    @staticmethod
    def _deps(reads, writes):
        deps = []
        for b in reads:
            if b.w is not None:
                deps.append(b.w)
        for b in writes:
            if b.w is not None:
                deps.append(b.w)
            deps.extend(b.r.items())
        return deps

    def op(self, e, fn, R=(), W=(), same_ok=False):
        deps = self._deps(R, W)
        if same_ok:
            deps = [d for d in deps if d[0] != e]
        self._need(e, deps)
        self.cnt[e] += 1
        c = self.cnt[e]
        self.prog[e].append(("i", fn, self.sem[e], 1))
        for b in R:
            b.r[e] = c
        for b in W:
            b.w = (e, c)
            b.r = {}
        self.n_inst += 1

    def dma(self, q, chan, out, in_, R=(), W=(), **kw):
        self._need(q, self._deps(R, W))
        self.cnt[chan] += 16
        c = self.cnt[chan]
        eng = self.eng[q]
        self.prog[q].append(("i", (lambda: eng.dma_start(out=out, in_=in_, **kw)), self.sem[chan], 16))
        for b in R:
            b.r[chan] = c
        for b in W:
            b.w = (chan, c)
            b.r = {}
        self.n_inst += 1

    def wait_all(self, e, bufs):
        deps = []
        for b in bufs:
            if b.w is not None:
                deps.append(b.w)
            deps.extend(b.r.items())
        self._need(e, deps)

    def barrier(self):
        ks = ["pe", "act", "dve", "pool"]
        for e in ks:
            self._need(e, [(k, self.cnt[k]) for k in ks if k != e and self.cnt[k] > 0])

    def flush(self):
        with self.nc.Block() as block:
            for k, reg in (("sp", block.sync), ("act", block.scalar), ("dve", block.vector),
                           ("pool", block.gpsimd), ("pe", block.tensor)):
                def body(eng, prog=self.prog[k]):
                    for it in prog:
                        if it[0] == "w":
                            eng.wait_ge(it[1], it[2])
                        else:
                            it[1]().then_inc(it[2], it[3])
                reg(body)

    def mm(self, out, lhsT, rhs, start, stop, R, W):
        nc = self.nc
        self.op("pe", lambda: nc.tensor.matmul(out, lhsT=lhsT, rhs=rhs, start=start, stop=stop),
                R=R, W=W, same_ok=True)

    def tr(self, out, in_, ident, R, W):
        nc = self.nc
        self.op("pe", lambda: nc.tensor.transpose(out, in_, ident), R=R, W=W, same_ok=True)

    def act(self, out, in_, func, R, W, bias=None, scale=None):
        nc = self.nc
        kw = {}
        if bias is not None:
            kw["bias"] = bias
        if scale is not None:
            kw["scale"] = scale
        self.op("act", lambda: nc.scalar.activation(out=out, in_=in_, func=func, **kw), R=R, W=W)

    def tt(self, e, out, in0, in1, op, R, W):
        eng = self.eng[e]
        self.op(e, lambda: eng.tensor_tensor(out=out, in0=in0, in1=in1, op=op), R=R, W=W)

    def stt(self, out, in0, scalar, in1, op0, op1, R, W):
        nc = self.nc
        self.op("dve", lambda: nc.vector.scalar_tensor_tensor(out=out, in0=in0, scalar=scalar, in1=in1,
                                                              op0=op0, op1=op1), R=R, W=W)

    def ts(self, e, out, in0, s1, s2, op0, op1, R, W):
        eng = self.eng[e]
        if op1 is None:
            self.op(e, lambda: eng.tensor_scalar(out=out, in0=in0, scalar1=s1, scalar2=None, op0=op0), R=R, W=W)
        else:
            self.op(e, lambda: eng.tensor_scalar(out=out, in0=in0, scalar1=s1, scalar2=s2, op0=op0, op1=op1),
                    R=R, W=W)

    def copy(self, e, out, in_, R, W):
        if e == "act":
            nc = self.nc
            self.op("act", lambda: nc.scalar.copy(out=out, in_=in_), R=R, W=W)
        else:
            eng = self.eng[e]
            self.op(e, lambda: eng.tensor_copy(out=out, in_=in_), R=R, W=W)

    def memset(self, e, ap, val, W):
        eng = self.eng[e]
        self.op(e, lambda: eng.memset(ap, val), W=W)

    def recip(self, out, in_, R, W):
        nc = self.nc
        self.op("dve", lambda: nc.vector.reciprocal(out=out, in_=in_), R=R, W=W)


class TL:
    def __init__(self, nc, name, shape, dtype, nb=1, psum=False):
        if psum:
            self.t = nc.alloc_psum_tensor(name, shape, dtype)
        else:
            self.t = nc.alloc_sbuf_tensor(name, shape, dtype)
        self.b = [Buf(f"{name}{i}") for i in range(nb)]
        self.shape = shape

    def __getitem__(self, k):
        return self.t[k]


def _even_in_groups():
    g = []
    for s, kind, c0 in ((1024, "a2", 0), (1536, "a2", 4), (0, "a1", 0), (512, "a1", 4),
                        (2048, "z", 0), (2560, "z", 4), (3072, "u", 0), (3584, "u", 4),
                        (4096, "u", 8), (4608, "u", 12)):
        g.append((list(range(s, s + 512)), [(kind, c0 + i) for i in range(4)]))
    return g


def _ffn_up_groups():
    g = []
    for i in range(5):
        g.append((list(range(512 * i, 512 * i + 512)), [("h1", 4 * i + k) for k in range(4)]))
        g.append((list(range(DFF + 512 * i, DFF + 512 * i + 512)), [("h2", 4 * i + k) for k in range(4)]))
    cols = list(range(2560, 2816)) + list(range(DFF + 2560, DFF + 2816))
    g.append((cols, [("h1", 20), ("h1", 21), ("h2", 20), ("h2", 21)]))
    return g


def layer_plan(i):
    j = i // 2
    plan = []
    if i % 2 == 0:
        for cols, tags in _even_in_groups():
            plan.append(dict(src="e_w_in", idx=j, cols=cols, KC=8, tags=tags, stage="ein"))
        plan.append(dict(src="e_w_in", idx=j, cols=list(range(5120, 5136)), KC=8, tags=[("dt", 0)], stage="edt"))
        for m in range(4):
            plan.append(dict(src="e_w_out", idx=j, cols=list(range(256 * m, 256 * m + 256)), KC=16,
                             tags=[("o", 2 * m), ("o", 2 * m + 1)], stage="eout"))
    else:
        for s, kind in ((2048, "v"), (2560, "v"), (1024, "cg"), (1536, "cg"), (0, "bg"), (512, "bg")):
            c0 = ((s % 1024) // 512) * 4
            plan.append(dict(src="o_w_in", idx=j, cols=list(range(s, s + 512)), KC=8,
                             tags=[(kind, c0 + k) for k in range(4)], stage="oin"))
        for m in range(2):
            plan.append(dict(src="o_w_out", idx=j, cols=list(range(512 * m, 512 * m + 512)), KC=8,
                             tags=[("o", 4 * m + k) for k in range(4)], stage="oout"))
    for cols, tags in _ffn_up_groups():
        plan.append(dict(src="f_w_up", idx=i, cols=cols, KC=8, tags=tags, stage="fup"))
    for m in range(8):
        plan.append(dict(src="f_w_down", idx=i, cols=list(range(128 * m, 128 * m + 128)), KC=NFF,
                         tags=[("d", m)], stage="fdown"))
    for m in range(2):
        plan.append(dict(src="ple_w_gate", idx=i, cols=list(range(512 * m, 512 * m + 512)), KC=8,
                         tags=[("g", 4 * m + k) for k in range(4)], stage="pgate"))
    plan.append(dict(src="ple_w_proj", idx=i, cols=list(range(1024)), KC=2, tags=[("p", k) for k in range(8)],
                     stage="pproj"))
    return plan


def full_plan():
    off = 0
    plans = []
    for i in range(DEPTH):
        p = layer_plan(i)
        for g in p:
            g["G"] = len(g["cols"])
            g["off"] = off
            n = 128 * g["KC"] * g["G"]
            off += n
        blk = CONV_ROWS * 2048
        off = ((off + blk - 1) // blk) * blk
        plans.append(p)
    return plans, off


_PLANS, _WTOTAL = full_plan()


def pack_weights(inp):
    flat = np.zeros(_WTOTAL, dtype=np.float32)
    for p in _PLANS:
        for g in p:
            Wm = np.asarray(inp[g["src"]][g["idx"]])
            sub = Wm[:, g["cols"]]
            sub = sub.reshape(g["KC"], 128, g["G"]).transpose(1, 0, 2)
            n = sub.size
            flat[g["off"]:g["off"] + n] = sub.reshape(-1)
    return flat.reshape(-1, 2048)


def _smalls_layout():
    lay = {}
    off = 0

    def add(name, n):
        nonlocal off
        lay[name] = off
        off += n
    for j in range(2):
        add(f"caw{j}", 8 * 31); add(f"cab{j}", 8); add(f"lag{j}", 8); add(f"lab{j}", 8)
        add(f"cbw{j}", 16 * 4); add(f"cbb{j}", 16); add(f"nbg{j}", 8); add(f"dsk{j}", 8)
        add(f"dtb{j}", 16); add(f"alog{j}", 16)
        add(f"ocw{j}", 8 * 3)
    for i in range(DEPTH):
        add(f"fcw{i}", 44 * 3); add(f"fcb{i}", 44)
        add(f"g1_{i}", 8); add(f"b1_{i}", 8); add(f"g2_{i}", 8); add(f"b2_{i}", 8)
    return lay, off


_SL, _NS = _smalls_layout()


def pack_smalls(inp):
    s = np.zeros((128, _NS), dtype=np.float32)

    def vec(name, v):
        v = np.asarray(v, dtype=np.float32)
        n = v.shape[0] // 128
        s[:, _SL[name]:_SL[name] + n] = v.reshape(n, 128).T

    def conv(name, w):
        w = np.asarray(w, dtype=np.float32)
        K, C = w.shape
        n = C // 128
        s[:, _SL[name]:_SL[name] + n * K] = w.reshape(K, n, 128).transpose(2, 1, 0).reshape(128, n * K)

    def row(name, v):
        v = np.asarray(v, dtype=np.float32)
        s[:, _SL[name]:_SL[name] + v.shape[0]] = v[None, :]

    for j in range(2):
        conv(f"caw{j}", inp["e_conv_a_w"][j]); vec(f"cab{j}", inp["e_conv_a_b"][j])
        vec(f"lag{j}", inp["e_ln_a_g"][j]); vec(f"lab{j}", inp["e_ln_a_b"][j])
        conv(f"cbw{j}", inp["e_conv_b_w"][j]); vec(f"cbb{j}", inp["e_conv_b_b"][j])
        vec(f"nbg{j}", inp["e_norm_b_g"][j])
        vec(f"dsk{j}", np.repeat(np.asarray(inp["e_d_skip"][j]), 64))
        row(f"dtb{j}", inp["e_dt_bias"][j]); row(f"alog{j}", inp["e_a_log"][j])
        conv(f"ocw{j}", inp["o_conv_w"][j])
    for i in range(DEPTH):
        conv(f"fcw{i}", inp["f_conv_w"][i]); vec(f"fcb{i}", inp["f_conv_b"][i])
        vec(f"g1_{i}", inp["ln_g"][i, 0]); vec(f"b1_{i}", inp["ln_b"][i, 0])
        vec(f"g2_{i}", inp["ln_g"][i, 1]); vec(f"b2_{i}", inp["ln_b"][i, 1])
    return s


def make_consts():
    c = np.zeros((128, 5, 128), dtype=np.float32)
    j = np.arange(128)[:, None]
    l = np.arange(128)[None, :]
    c[:, 0] = (j == l)
    c[:, 1] = (j <= l)
    c[:, 2] = (j > l)
    c[:, 3] = 1.0
    c[:, 4] = 1.0 / 1024.0
    return c.reshape(128, 640)


F32R = mybir.dt.float32r
GRAN = 256
NPE_A = 15


class View:
    def __init__(self, ap, grans_per_chunk):
        self.ap = ap
        self._g = grans_per_chunk
        self.b = []
        seen = set()
        for gl in grans_per_chunk:
            for g in gl:
                if id(g) not in seen:
                    seen.add(id(g))
                    self.b.append(g)

    def bl(self, c):
        return self._g[c]

    def __getitem__(self, k):
        return self.ap[k]


def _tl_bl(self, c):
    return [self.b[c]]


TL.bl = _tl_bl


class Prog:
    def __init__(self, n_tiles, layers, seq_len):
        self.n_tiles = n_tiles
        self.layers = layers
        self.seq_len = seq_len
        nc = bass.Bass("TRN2", target_bir_lowering=False)
        self.nc = nc
        em = Emit(nc)
        self.em = em
        L = seq_len
        self.xT = nc.dram_tensor("xT", [D, L], F32, kind="ExternalInput").ap()
        self.pT = nc.dram_tensor("pT", [DEPTH, PLE, L], F32, kind="ExternalInput").ap()
        self.wf = nc.dram_tensor("wf", [_WTOTAL // 2048, 2048], F32, kind="ExternalInput").ap()
        self.sm_d = nc.dram_tensor("smalls", [128, _NS], F32, kind="ExternalInput").ap()
        self.cs_d = nc.dram_tensor("consts", [128, 640], F32, kind="ExternalInput").ap()
        self.yT = nc.dram_tensor("yT", [D, L], F32, kind="ExternalOutput").ap()
        self.wsb = nc.dram_tensor("wsb", [_WTOTAL // 2048, 2048], BF16, kind="Internal").ap()
        self.conv_bufs = [Buf(f"cv{r}") for r in range(_WTOTAL // (2048 * CONV_ROWS))]
        self.ch_conv = em.dma_chan("conv")
        self.conv_done = set()

        self.sm = TL(nc, "sm", [128, _NS], F32)
        self.cs = TL(nc, "cs", [128, 5, 128], F32)
        self.identb = TL(nc, "identb", [128, 128], BF16)
        self.maskb = TL(nc, "maskb", [128, 128], BF16)
        self.arow = TL(nc, "arow", [128, 2, 16], F32)
        self.epsb = TL(nc, "epsb", [128, 1], F32)
        self.xf = TL(nc, "xf", [128, NCH, T], F32, nb=NCH)
        self.xb = TL(nc, "xb", [128, NCH, T], BF16, nb=NCH)
        self.pb = TL(nc, "pb", [128, 2, T], BF16)
        self.wr = [TL(nc, f"wr{s}", [128, WSLOT], BF16) for s in range(NSLOT)]
        self.ch_w = [em.dma_chan(f"w{s}") for s in range(NSLOT)]
        self.ch_x = em.dma_chan("x")
        self.ch_p = em.dma_chan("p")
        self.ch_y = em.dma_chan("y")
        self.ch_c = em.dma_chan("c")
        self.banks = [TL(nc, f"bk{i}", [128, 512], F32, psum=True) for i in range(8)]
        self.bk_i = 0
        self.held = set()
        self.mean = TL(nc, "mean", [128, T], F32)
        self.rstd = TL(nc, "rstd", [128, T], F32)
        self.var = TL(nc, "var", [128, T], F32)
        self.sq = [TL(nc, f"sq{i}", [128, T], BF16) for i in range(4)]
        self.xbt = [TL(nc, f"xbt{i}", [128, T], BF16) for i in range(4)]
        self.sq_i = 0
        self.onesb = TL(nc, "onesb", [128, 128], BF16)
        self.stF = [TL(nc, f"stF{i}", [128, 44, 2], F32) for i in range(DEPTH)]
        self.stA = [TL(nc, f"stA{j}", [128, 8, 30], BF16) for j in range(2)]
        self.stB = [TL(nc, f"stB{j}", [128, 16, 3], BF16) for j in range(2)]
        self.stC = [TL(nc, f"stC{j}", [128, 8, 2], BF16) for j in range(2)]
        self.H = [TL(nc, f"H{j}", [128, 1024], F32) for j in range(2)]
        self.Hb = TL(nc, "Hb", [128, 1024], BF16)
        ARENA = 120 * 1024
        self.arena = nc.alloc_sbuf_tensor("arena", [128, ARENA // 2], BF16)
        self.arena_size = ARENA
        self.gran = [Buf(f"gr{k}") for k in range(ARENA // GRAN)]

        self.wseq_n = 0
        self.wissued = 0
        self.wflat_seq = []
        for t in range(n_tiles):
            for i in layers:
                for g in _PLANS[i]:
                    self.wflat_seq.append((t, i, g))

    def carve(self, specs):
        out = {}
        off = 0
        top = 0
        offs = {}
        for spec in specs:
            name, shp, dt = spec[0], spec[1], spec[2]
            at = spec[3] if len(spec) > 3 else None
            esz = 4 if dt == F32 else 2
            nchunk = shp[0]
            cel = int(np.prod(shp[1:]))
            cb = cel * esz
            stride_b = (cb + GRAN - 1) // GRAN * GRAN
            if at == "top":
                off = top
            elif at is not None:
                off = offs[at]
            off = (off + GRAN - 1) // GRAN * GRAN
            offs[name] = off
            tot_b = nchunk * stride_b
            a = self.arena[:, off // 2: (off + tot_b) // 2]
            if dt == F32:
                a = a.bitcast(F32)
            a = a.rearrange("p (a s) -> p a s", a=nchunk)[:, :, 0:cel]
            if len(shp) == 3:
                a = a.rearrange("p a (b c) -> p a b c", b=shp[1])
            grans = []
            for k in range(nchunk):
                s0 = off + k * stride_b
                grans.append(self.gran[s0 // GRAN: (s0 + cb - 1) // GRAN + 1])
            out[name] = View(a, grans)
            off += tot_b
            top = max(top, off)
        assert top <= self.arena_size, (top, self.arena_size)
        self.arena_top = max(getattr(self, "arena_top", 0), top)
        return out

    def bank(self, hold=False):
        for _ in range(8):
            i = self.bk_i
            self.bk_i = (self.bk_i + 1) % 8
            if i not in self.held:
                if hold:
                    self.held.add(i)
                return self.banks[i]
        raise RuntimeError("all PSUM banks held")

    def release(self, bk):
        self.held.discard(self.banks.index(bk))

    def issue_conv(self, layer):
        if layer in self.conv_done or layer >= DEPTH:
            return
        self.conv_done.add(layer)
        p = _PLANS[layer]
        lo = p[0]["off"] // (2048 * CONV_ROWS)
        last = p[-1]
        hi = (last["off"] + 128 * last["KC"] * last["G"] + 2048 * CONV_ROWS - 1) // (2048 * CONV_ROWS)
        for r in range(lo, hi):
            self.em.dma("pool", self.ch_conv, self.wsb[r * CONV_ROWS:(r + 1) * CONV_ROWS, :],
                        self.wf[r * CONV_ROWS:(r + 1) * CONV_ROWS, :], W=[self.conv_bufs[r]])

    def _issue_w(self, n):
        t, i, g = self.wflat_seq[n]
        s = n % NSLOT
        KC, G = g["KC"], g["G"]
        cnt = 128 * KC * G
        r0 = g["off"] // (2048 * CONV_ROWS)
        r1 = (g["off"] + cnt - 1) // (2048 * CONV_ROWS)
        src = bass.AP(self.wsb.tensor, g["off"], [[KC * G, 128], [1, KC * G]])
        self.em.dma("sp", self.ch_w[s], self.wr[s][:, 0:KC * G], src,
                    R=[self.conv_bufs[r] for r in range(r0, r1 + 1)], W=self.wr[s].b)

    def wget(self, g):
        n = self.wseq_n
        assert self.wflat_seq[n][2] is g, (n, g["src"], self.wflat_seq[n][2]["src"])
        while self.wissued < min(len(self.wflat_seq), n + NSLOT):
            self._issue_w(self.wissued)
            self.wissued += 1
        self.wseq_n += 1
        s = n % NSLOT
        KC, G = g["KC"], g["G"]
        v = self.wr[s][:, 0:KC * G].rearrange("p (k g) -> p k g", k=KC)
        return self.wr[s], v

    def sms(self, name, idx):
        o = _SL[name] + idx
        return self.sm[:, o:o + 1]


class Skew:
    def __init__(self, depth=1):
        self.depth = depth
        self.q = []

    def push(self, fn):
        self.q.append(fn)
        while len(self.q) > self.depth:
            self.q.pop(0)()

    def flush(self):
        while self.q:
            self.q.pop(0)()


class Stats:
    def __init__(self, P, want_mean=True):
        self.P = P
        self.want_mean = want_mean
        self.bm = P.bank(hold=True) if want_mean else None
        self.bq = P.bank(hold=True)
        self.n = 0
        self.pend = None

    def _flush(self):
        if self.pend is not None:
            self.pend()
            self.pend = None

    def add(self, ap, bufs):
        P, em = self.P, self.P.em
        onesb = P.onesb[:, :]
        sq = P.sq[P.sq_i]
        xbt = P.xbt[P.sq_i]
        P.sq_i = (P.sq_i + 1) % 4
        c = self.n
        if c % 2 == 0:
            em.act(sq[:, :], ap, AF.Square, R=bufs, W=sq.b)
        else:
            em.tt("dve", sq[:, :], ap, ap, ALU.mult, R=bufs, W=sq.b)
        if self.want_mean:
            em.copy("act", xbt[:, :], ap, R=bufs, W=xbt.b)
        self._flush()

        def mms():
            if self.want_mean:
                em.mm(self.bm[:, :], onesb, xbt[:, :], c == 0, c == NCH - 1, R=xbt.b + P.onesb.b, W=self.bm.b)
            em.mm(self.bq[:, :], onesb, sq[:, :], c == 0, c == NCH - 1, R=sq.b + P.onesb.b, W=self.bq.b)
        self.pend = mms
        self.n += 1

    def finish(self):
        P, em = self.P, self.P.em
        assert self.n == NCH
        self._flush()
        if self.want_mean:
            em.copy("act", P.mean[:, :], self.bm[:, :], R=self.bm.b, W=P.mean.b)
            em.act(P.var[:, :], self.bm[:, :], AF.Square, R=self.bm.b, W=P.var.b)
            em.tt("dve", P.var[:, :], self.bq[:, :], P.var[:, :], ALU.subtract, R=self.bq.b + P.var.b, W=P.var.b)
            em.act(P.var[:, :], P.var[:, :], AF.Sqrt, R=P.var.b + P.epsb.b, W=P.var.b, bias=P.epsb[:, 0:1])
            P.release(self.bm)
        else:
            em.act(P.var[:, :], self.bq[:, :], AF.Sqrt, R=self.bq.b + P.epsb.b, W=P.var.b, bias=P.epsb[:, 0:1])
        P.release(self.bq)
        em.recip(P.rstd[:, :], P.var[:, :], R=P.var.b, W=P.rstd.b)


def residual_ln(P, st, gname, bname):
    em = P.em
    st.finish()
    for c in range(NCH):
        xb_ = P.xf.bl(c)
        em.tt("dve", P.xf[:, c, :], P.xf[:, c, :], P.mean[:, :], ALU.subtract, R=xb_ + P.mean.b, W=xb_)
        em.tt("dve", P.xf[:, c, :], P.xf[:, c, :], P.rstd[:, :], ALU.mult, R=xb_ + P.rstd.b, W=xb_)
        em.act(P.xb[:, c, :], P.xf[:, c, :], AF.Identity, R=xb_ + P.sm.b, W=P.xb.bl(c),
               bias=P.sms(bname, c), scale=P.sms(gname, c))
        em.ts("pool", P.xf[:, c, :], P.xf[:, c, :], P.sms(gname, c), P.sms(bname, c), ALU.mult, ALU.add,
              R=xb_ + P.sm.b, W=xb_)


def build_diag(P, dst_ap, taps, wname, cidx, nt, W):
    em = P.em
    for n, k in enumerate(taps):
        em.ts("pool", dst_ap[:, n, :], P.identb[:, :], P.sms(wname, cidx * nt + k), 0.0, ALU.mult, ALU.add,
              R=P.identb.b + P.sm.b, W=W)


def dve_taps(P, acc_ap, acc_b, src_ap_fn, src_b, taps, wname, cidx, nt):
    em = P.em
    for k in taps:
        em.stt(acc_ap, src_ap_fn(k), P.sms(wname, cidx * nt + k), acc_ap, ALU.mult, ALU.add,
               R=src_b + acc_b + P.sm.b, W=acc_b)


def ffn_stage(P, i, tile):
    em = P.em
    A = P.carve([("g", [NFF, T], BF16), ("hs", [4, T + 2], F32), ("acc", [4, T], F32),
                 ("s1", [4, T], BF16), ("ffo", [NCH, T], F32)])
    plan = [g for g in _PLANS[i] if g["stage"] == "fup"]
    n_i = 0
    stF = P.stF[i]
    s1slot = {}
    sk = Skew(1)
    fcw, fcb = f"fcw{i}", f"fcb{i}"
    for g in plan:
        slot, wv = P.wget(g)
        for mi, (kind, ch) in enumerate(g["tags"]):
            cidx = ch if kind == "h1" else NFF + ch
            bk = P.bank()
            for kc in range(8):
                em.mm(bk[:, :], wv[:, kc, mi * 128:(mi + 1) * 128], P.xb[:, kc, :], kc == 0, kc == 7,
                      R=slot.b + P.xb.bl(kc), W=bk.b)
            hi = n_i % 4
            n_i += 1
            hb = A["hs"].bl(hi)
            ab_ = A["acc"].bl(hi)
            em.copy("pool", A["hs"][:, hi, 0:2], stF[:, cidx, :], R=stF.b, W=hb)
            em.copy("act", A["hs"][:, hi, 2:T + 2], bk[:, :], R=bk.b, W=hb)
            em.act(A["acc"][:, hi, :], bk[:, :], AF.Identity, R=bk.b + P.sm.b, W=ab_,
                   bias=P.sms(fcb, cidx), scale=P.sms(fcw, cidx * 3 + 2))
            em.copy("pool", stF[:, cidx, :], A["hs"][:, hi, T:T + 2], R=hb, W=stF.b)

            def tail(hi=hi, hb=hb, ab_=ab_, kind=kind, ch=ch, cidx=cidx):
                dve_taps(P, A["acc"][:, hi, :], ab_, lambda k: A["hs"][:, hi, k:k + T], hb, [1, 0], fcw, cidx, 3)
                if kind == "h1":
                    si = ch % 4
                    s1slot[ch] = si
                    em.act(A["s1"][:, si, :], A["acc"][:, hi, :], AF.Silu, R=ab_, W=A["s1"].bl(si))
                else:
                    si = s1slot[ch]
                    em.tt("dve", A["g"][:, ch, :], A["acc"][:, hi, :], A["s1"][:, si, :], ALU.mult,
                          R=ab_ + A["s1"].bl(si), W=A["g"].bl(ch))
            sk.push(tail)
    sk.flush()
    downs = [g for g in _PLANS[i] if g["stage"] == "fdown"]
    gate_groups = [g for g in _PLANS[i] if g["stage"] == "pgate"]
    proj_group = [g for g in _PLANS[i] if g["stage"] == "pproj"][0]
    for g in downs:
        slot, wv = P.wget(g)
        m = g["tags"][0][1]
        bk = P.bank()
        for kc in range(NFF):
            em.mm(bk[:, :], wv[:, kc, :], A["g"][:, kc, :], kc == 0, kc == NFF - 1,
                  R=slot.b + A["g"].bl(kc), W=bk.b)
        em.stt(P.xf[:, m, :], P.xf[:, m, :], ALPHA, bk[:, :], ALU.mult, ALU.add, R=bk.b + P.xf.bl(m), W=P.xf.bl(m))
    for g in gate_groups:
        slot, wv = P.wget(g)
        for mi, (kind, m) in enumerate(g["tags"]):
            bk = P.bank()
            for kc in range(8):
                em.mm(bk[:, :], wv[:, kc, mi * 128:(mi + 1) * 128], P.xb[:, kc, :], kc == 0, kc == 7,
                      R=slot.b + P.xb.bl(kc), W=bk.b)
            em.act(A["ffo"][:, m, :], bk[:, :], AF.Sigmoid, R=bk.b, W=A["ffo"].bl(m))
    slot, wv = P.wget(proj_group)
    st = Stats(P)
    for m in range(NCH):
        bk = P.bank()
        for kc in range(2):
            em.mm(bk[:, :], wv[:, kc, m * 128:(m + 1) * 128], P.pb[:, kc, :], kc == 0, kc == 1,
                  R=slot.b + P.pb.b, W=bk.b)
        em.tt("dve", A["ffo"][:, m, :], bk[:, :], A["ffo"][:, m, :], ALU.mult, R=bk.b + A["ffo"].bl(m), W=A["ffo"].bl(m))
        em.tt("dve", P.xf[:, m, :], P.xf[:, m, :], A["ffo"][:, m, :], ALU.add, R=P.xf.bl(m) + A["ffo"].bl(m), W=P.xf.bl(m))
        st.add(P.xf[:, m, :], P.xf.bl(m))
    residual_ln(P, st, f"g2_{i}", f"b2_{i}")


def odd_mixer(P, i, tile):
    em = P.em
    j = i // 2
    A = P.carve([("v", [NCH, T], BF16), ("cv", [NCH, T + 2], BF16), ("bg", [NCH, T], BF16),
                 ("mx", [NCH, T], BF16), ("acc", [2, T], F32)])
    stC = P.stC[j]
    ocw = f"ocw{j}"
    em.copy("pool", A["cv"][:, :, 0:2], stC[:, :, :], R=stC.b, W=A["cv"].b)
    for g in [g for g in _PLANS[i] if g["stage"] == "oin"]:
        slot, wv = P.wget(g)
        for mi, (kind, c) in enumerate(g["tags"]):
            bk = P.bank()
            for kc in range(8):
                em.mm(bk[:, :], wv[:, kc, mi * 128:(mi + 1) * 128], P.xb[:, kc, :], kc == 0, kc == 7,
                      R=slot.b + P.xb.bl(kc), W=bk.b)
            if kind == "v":
                em.copy("act", A["v"][:, c, :], bk[:, :], R=bk.b, W=A["v"].bl(c))
            elif kind == "cg":
                em.tt("dve", A["cv"][:, c, 2:T + 2], bk[:, :], A["v"][:, c, :], ALU.mult,
                      R=bk.b + A["v"].bl(c), W=A["cv"].bl(c))
            else:
                em.copy("act", A["bg"][:, c, :], bk[:, :], R=bk.b, W=A["bg"].bl(c))
    em.copy("pool", stC[:, :, :], A["cv"][:, :, T:T + 2], R=A["cv"].b, W=stC.b)
    for c in range(NCH):
        ai = c % 2
        ab_ = A["acc"].bl(ai)
        em.ts("dve", A["acc"][:, ai, :], A["cv"][:, c, 2:T + 2], P.sms(ocw, c * 3 + 2), None, ALU.mult, None,
              R=A["cv"].bl(c) + P.sm.b, W=ab_)
        dve_taps(P, A["acc"][:, ai, :], ab_, lambda k: A["cv"][:, c, k:k + T], A["cv"].bl(c), [1, 0], ocw, c, 3)
        em.tt("dve", A["mx"][:, c, :], A["acc"][:, ai, :], A["bg"][:, c, :], ALU.mult, R=ab_ + A["bg"].bl(c), W=A["mx"].bl(c))
    st = Stats(P)
    for g in [g for g in _PLANS[i] if g["stage"] == "oout"]:
        slot, wv = P.wget(g)
        for mi, (kind, m) in enumerate(g["tags"]):
            bk = P.bank()
            for kc in range(8):
                em.mm(bk[:, :], wv[:, kc, mi * 128:(mi + 1) * 128], A["mx"][:, kc, :], kc == 0, kc == 7,
                      R=slot.b + A["mx"].bl(kc), W=bk.b)
            em.stt(P.xf[:, m, :], P.xf[:, m, :], ALPHA, bk[:, :], ALU.mult, ALU.add, R=bk.b + P.xf.bl(m), W=P.xf.bl(m))
            st.add(P.xf[:, m, :], P.xf.bl(m))
    residual_ln(P, st, f"g1_{i}", f"b1_{i}")


def even_mixer(P, i, tile):
    em = P.em
    j = i // 2
    A = P.carve([
        ("sgya", [NCH, T], BF16),
        ("zs", [NCH, T], BF16),
        ("yb", [NCH, T], BF16, "zs"),
        ("ub", [16, T + 3], BF16),
        ("cf", [NCH, T], F32),
        ("ys", [NCH, T], F32, "cf"),
        ("acc", [4, T], F32, "top"),
        ("dtt", [4, 16], F32), ("dta", [4, 16], F32),
        ("ab", [NCH, T + 30], BF16),
        ("dga", [2, 31, 128], BF16),
        ("xdt", [2, 16, 64], BF16, "ab"), ("xdd", [2, 16, 64], BF16), ("btok", [2, 4, 128], BF16),
        ("rhsu", [2, 16, 128], F32), ("E", [2, 16, 128], BF16), ("cbm", [2, 4, 128], BF16),
        ("Wp", [2, 16, 128], BF16), ("ytok", [2, 8, 128], F32), ("ex3", [2, 48], F32),
    ])
    stA, stB = P.stA[j], P.stB[j]
    H = P.H[j]
    caw, cbw = f"caw{j}", f"cbw{j}"
    em.copy("pool", A["ab"][:, :, 0:30], stA[:, :, :], R=stA.b, W=A["ab"].b)
    em.copy("pool", A["ub"][:, :, 0:3], stB[:, :, :], R=stB.b, W=A["ub"].b)
    st_a = Stats(P)

    def conv_a_chunk(c):
        di = c % 2
        cfb = A["cf"].bl(c)
        dgb = A["dga"].bl(di)
        for k in range(31):
            em.ts("dve", A["dga"][:, di, k, :], P.identb[:, :], P.sms(caw, c * 31 + k), None, ALU.mult, None,
                  R=P.identb.b + P.sm.b, W=dgb)
        bk = P.bank()
        for k in range(31):
            em.mm(bk[:, :], A["dga"][:, di, k, :], A["ab"][:, c, k:k + T], k == 0, k == 30,
                  R=dgb + A["ab"].bl(c), W=bk.b)
        em.act(A["cf"][:, c, :], bk[:, :], AF.Identity, R=bk.b + P.sm.b, W=cfb, bias=P.sms(f"cab{j}", c))
        st_a.add(A["cf"][:, c, :], cfb)

    sk = Skew(1)
    conv_sched = {4: [0, 1], 5: [2, 3], 6: [4], 7: [5], 8: [6], 9: [7]}
    for gi, g in enumerate([g for g in _PLANS[i] if g["stage"] == "ein"]):
        slot, wv = P.wget(g)
        for mi, (kind, c) in enumerate(g["tags"]):
            bk = P.bank()
            for kc in range(8):
                em.mm(bk[:, :], wv[:, kc, mi * 128:(mi + 1) * 128], P.xb[:, kc, :], kc == 0, kc == 7,
                      R=slot.b + P.xb.bl(kc), W=bk.b)
            if kind == "a2":
                em.act(A["sgya"][:, c, :], bk[:, :], AF.Sigmoid, R=bk.b, W=A["sgya"].bl(c))
            elif kind == "a1":
                em.tt("dve", A["ab"][:, c, 30:T + 30], bk[:, :], A["sgya"][:, c, :], ALU.mult,
                      R=bk.b + A["sgya"].bl(c), W=A["ab"].bl(c))
            elif kind == "z":
                em.act(A["zs"][:, c, :], bk[:, :], AF.Silu, R=bk.b, W=A["zs"].bl(c))
            else:
                ai = c % 4
                em.copy("act", A["ub"][:, c, 3:T + 3], bk[:, :], R=bk.b, W=A["ub"].bl(c))
                em.act(A["acc"][:, ai, :], bk[:, :], AF.Identity, R=bk.b + P.sm.b, W=A["acc"].bl(ai),
                       bias=P.sms(f"cbb{j}", c), scale=P.sms(cbw, c * 4 + 3))

                def tail(c=c, ai=ai):
                    dve_taps(P, A["acc"][:, ai, :], A["acc"].bl(ai), lambda k: A["ub"][:, c, k:k + T],
                             A["ub"].bl(c), [2, 1, 0], cbw, c, 4)
                    em.copy("pool", stB[:, c, :], A["ub"][:, c, T:T + 3], R=A["ub"].bl(c), W=stB.b)
                    em.act(A["ub"][:, c, 3:T + 3], A["acc"][:, ai, :], AF.Silu, R=A["acc"].bl(ai), W=A["ub"].bl(c))
                sk.push(tail)
        if gi == 3:
            em.copy("pool", stA[:, :, :], A["ab"][:, :, T:T + 30], R=A["ab"].b, W=stA.b)
        for c in conv_sched.get(gi, []):
            conv_a_chunk(c)
    sk.flush()
    g = [g for g in _PLANS[i] if g["stage"] == "edt"][0]
    slot, wv = P.wget(g)
    bkd = P.bank()
    for q in range(4):
        for kc in range(8):
            em.mm(bkd[:, q * 16:(q + 1) * 16], P.xb[:, kc, q * 128:(q + 1) * 128], wv[:, kc, :], kc == 0, kc == 7,
                  R=slot.b + P.xb.bl(kc), W=bkd.b)
    dtb = P.sm[:, _SL[f"dtb{j}"]:_SL[f"dtb{j}"] + 16]
    em.tt("dve", A["dtt"][:, :, :], bkd[:, 0:64].rearrange("p (q h) -> p q h", q=4),
          dtb.unsqueeze(1).to_broadcast([128, 4, 16]), ALU.add, R=bkd.b + P.sm.b, W=A["dtt"].b)
    em.act(A["dtt"][:, :, :], A["dtt"][:, :, :], AF.Exp, R=A["dtt"].b, W=A["dtt"].b)
    em.act(A["dtt"][:, :, :], A["dtt"][:, :, :], AF.Ln, R=A["dtt"].b, W=A["dtt"].b, bias=P.cs[:, 3, 0:1])
    em.tt("dve", A["dta"][:, :, :], A["dtt"][:, :, :], P.arow[:, j, :].unsqueeze(1).to_broadcast([128, 4, 16]),
          ALU.mult, R=A["dtt"].b + P.arow.b, W=A["dta"].b)
    st_a.finish()
    for c in range(NCH):
        cfb = A["cf"].bl(c)
        em.tt("dve", A["cf"][:, c, :], A["cf"][:, c, :], P.mean[:, :], ALU.subtract, R=cfb + P.mean.b, W=cfb)
        em.tt("dve", A["cf"][:, c, :], A["cf"][:, c, :], P.rstd[:, :], ALU.mult, R=cfb + P.rstd.b, W=cfb)
        em.act(A["sgya"][:, c, :], A["cf"][:, c, :], AF.Silu, R=cfb + P.sm.b, W=A["sgya"].bl(c),
               bias=P.sms(f"lab{j}", c), scale=P.sms(f"lag{j}", c))

    U = P.cs[:, 1, :]
    SLm = P.cs[:, 2, :]
    ones = P.cs[:, 3, :]
    identf = P.cs[:, 0, :]
    csb = P.cs.b
    for q in range(4):
        qi = q % 2
        tq = slice(3 + q * 128, 3 + (q + 1) * 128)
        xdt, xdd, btok, rhsu, E_, cbm, Wp, ytok = (A[n][:, qi] for n in ("xdt", "xdd", "btok", "rhsu", "E", "cbm", "Wp", "ytok"))
        b_xdt, b_xdd, b_btok, b_rhsu, b_E, b_cbm, b_Wp, b_ytok, b_ex3 = (
            A[n].bl(qi) for n in ("xdt", "xdd", "btok", "rhsu", "E", "cbm", "Wp", "ytok", "ex3"))
        ex3 = A["ex3"][:, qi, :]
        bA = P.bank()
        bB = P.bank()
        pa = bA[:, :].bitcast(BF16).rearrange("p (c l) -> p c l", c=8)
        pb_ = bB[:, 0:256].bitcast(BF16).rearrange("p (c l) -> p c l", c=4)
        for c in range(8):
            em.tr(pa[:, c, :], A["ub"][:, c, tq], P.identb[:, :], R=A["ub"].bl(c) + P.identb.b, W=bA.b)
        for c in range(4):
            em.tr(pb_[:, c, :], A["ub"][:, 8 + c, tq], P.identb[:, :], R=A["ub"].bl(8 + c) + P.identb.b, W=bB.b)
        em.tt("dve", xdt, bA[:, :].bitcast(BF16).rearrange("p (h d) -> p h d", h=16),
              A["dtt"][:, q, :].unsqueeze(2).to_broadcast([128, 16, 64]), ALU.mult,
              R=bA.b + A["dtt"].b, W=b_xdt)
        em.copy("act", btok, pb_, R=bB.b, W=b_btok)
        bS = P.bank()
        dta_q = A["dta"][:, q, :]
        em.mm(bS[:, 0:16], U, dta_q, True, True, R=csb + A["dta"].b, W=bS.b)
        em.mm(bS[:, 16:32], SLm, dta_q, True, True, R=csb + A["dta"].b, W=bS.b)
        em.mm(bS[:, 32:48], ones, dta_q, True, True, R=csb + A["dta"].b, W=bS.b)
        em.act(ex3, bS[:, 0:48], AF.Exp, R=bS.b, W=b_ex3)
        em.tt("dve", xdd, xdt, ex3[:, 16:32].unsqueeze(2).to_broadcast([128, 16, 64]), ALU.mult,
              R=b_xdt + b_ex3, W=b_xdd)
        em.tt("dve", rhsu, dta_q.unsqueeze(2).to_broadcast([128, 16, 128]),
              U.unsqueeze(1).to_broadcast([128, 16, 128]), ALU.mult, R=A["dta"].b + csb, W=b_rhsu)
        for b4 in range(4):
            bk = P.bank()
            em.mm(bk[:, :], SLm, rhsu[:, 4 * b4:4 * b4 + 4, :].rearrange("p h l -> p (h l)"), True, True,
                  R=csb + b_rhsu, W=bk.b)
            em.act(E_[:, 4 * b4:4 * b4 + 4, :].rearrange("p h l -> p (h l)"), bk[:, :], AF.Exp, R=bk.b, W=b_E)
        bC = P.bank()
        for gg in range(4):
            em.mm(bC[:, gg * 128:(gg + 1) * 128], A["ub"][:, 8 + gg, tq], A["ub"][:, 12 + gg, tq], True, True,
                  R=A["ub"].bl(8 + gg) + A["ub"].bl(12 + gg), W=bC.b)
        em.tt("dve", cbm, bC[:, :].rearrange("p (g l) -> p g l", g=4),
              P.maskb[:, :].unsqueeze(1).to_broadcast([128, 4, 128]), ALU.mult, R=bC.b + P.maskb.b, W=b_cbm)
        em.tt("dve", Wp.rearrange("p (g h) l -> p g h l", g=4), E_.rearrange("p (g h) l -> p g h l", g=4),
              cbm.unsqueeze(2).to_broadcast([128, 4, 4, 128]), ALU.mult, R=b_E + b_cbm, W=b_Wp)
        bY0, bY1 = P.bank(), P.bank()
        for h in range(16):
            bk = bY0 if h < 8 else bY1
            em.mm(bk[:, (h % 8) * 64:(h % 8 + 1) * 64], Wp[:, h, :], xdt[:, h, :], True, True,
                  R=b_Wp + b_xdt, W=bk.b)
        bO0, bO1 = P.bank(), P.bank()
        for gg in range(4):
            bk = bO0 if gg < 2 else bO1
            em.mm(bk[:, (gg % 2) * 256:(gg % 2 + 1) * 256], A["ub"][:, 12 + gg, tq], P.Hb[:, gg * 256:(gg + 1) * 256],
                  True, True, R=A["ub"].bl(12 + gg) + P.Hb.b, W=bk.b)
        for hh, (bo, by) in enumerate(((bO0, bY0), (bO1, bY1))):
            yt = ytok[:, 4 * hh:4 * hh + 4, :].rearrange("p c l -> p (c l)")
            em.tt("dve", yt.rearrange("p (h d) -> p h d", h=8), bo[:, :].rearrange("p (h d) -> p h d", h=8),
                  ex3[:, 8 * hh:8 * hh + 8].unsqueeze(2).to_broadcast([128, 8, 64]), ALU.mult,
                  R=bo.b + b_ex3, W=b_ytok)
            em.tt("dve", yt, yt, by[:, :], ALU.add, R=by.b + b_ytok, W=b_ytok)
        bH0, bH1 = P.bank(), P.bank()
        for gg in range(4):
            bk = bH0 if gg < 2 else bH1
            em.mm(bk[:, (gg % 2) * 256:(gg % 2 + 1) * 256], btok[:, gg, :],
                  xdd[:, 4 * gg:4 * gg + 4, :].rearrange("p h d -> p (h d)"), True, True,
                  R=b_btok + b_xdd, W=bk.b)
        em.tt("dve", H[:, :].rearrange("p (h d) -> p h d", h=16), H[:, :].rearrange("p (h d) -> p h d", h=16),
              ex3[:, 32:48].unsqueeze(2).to_broadcast([128, 16, 64]), ALU.mult, R=H.b + b_ex3, W=H.b)
        for hh, bk in enumerate((bH0, bH1)):
            em.tt("dve", H[:, hh * 512:(hh + 1) * 512], H[:, hh * 512:(hh + 1) * 512], bk[:, :], ALU.add,
                  R=H.b + bk.b, W=H.b)
        em.copy("act", P.Hb[:, :], H[:, :], R=H.b, W=P.Hb.b)
        bT0, bT1 = P.bank(), P.bank()
        for c in range(8):
            bk = bT0 if c < 4 else bT1
            em.tr(bk[:, (c % 4) * 128:(c % 4 + 1) * 128], ytok[:, c, :], identf, R=b_ytok + csb, W=bk.b)
        for hh, bk in enumerate((bT0, bT1)):
            wb_ = []
            for c in range(4 * hh, 4 * hh + 4):
                wb_ += A["ys"].bl(c)
            em.copy("act", A["ys"][:, 4 * hh:4 * hh + 4, q * 128:(q + 1) * 128],
                    bk[:, :].rearrange("p (c l) -> p c l", c=4), R=bk.b, W=wb_)
    st_b = Stats(P, want_mean=False)
    for c in range(NCH):
        ysb = A["ys"].bl(c)
        em.stt(A["ys"][:, c, :], A["ub"][:, c, 3:T + 3], P.sms(f"dsk{j}", c), A["ys"][:, c, :], ALU.mult, ALU.add,
               R=A["ub"].bl(c) + ysb + P.sm.b, W=ysb)
        em.tt("dve", A["ys"][:, c, :], A["ys"][:, c, :], A["zs"][:, c, :], ALU.mult, R=ysb + A["zs"].bl(c), W=ysb)
        st_b.add(A["ys"][:, c, :], ysb)
    st_b.finish()
    for c in range(NCH):
        em.stt(A["yb"][:, c, :], A["ys"][:, c, :], P.sms(f"nbg{j}", c), P.rstd[:, :], ALU.mult, ALU.mult,
               R=A["ys"].bl(c) + P.sm.b + P.rstd.b, W=A["yb"].bl(c))
    st = Stats(P)
    for g in [g for g in _PLANS[i] if g["stage"] == "eout"]:
        slot, wv = P.wget(g)
        for mi, (kind, m) in enumerate(g["tags"]):
            bk = P.bank()
            for kc in range(16):
                src = A["sgya"] if kc < 8 else A["yb"]
                em.mm(bk[:, :], wv[:, kc, mi * 128:(mi + 1) * 128], src[:, kc % 8, :], kc == 0, kc == 15,
                      R=slot.b + src.bl(kc % 8), W=bk.b)
            em.stt(P.xf[:, m, :], P.xf[:, m, :], ALPHA, bk[:, :], ALU.mult, ALU.add, R=bk.b + P.xf.bl(m), W=P.xf.bl(m))
            st.add(P.xf[:, m, :], P.xf.bl(m))
    residual_ln(P, st, f"g1_{i}", f"b1_{i}")


def build_program(n_tiles=SEQ // T, layers=(0, 1, 2, 3), seq_len=SEQ):
    P = Prog(n_tiles, list(layers), seq_len)
    nc, em = P.nc, P.em
    em.dma("sp", P.ch_c, P.sm[:, :], P.sm_d, W=P.sm.b)
    em.dma("sp", P.ch_c, P.cs[:, :, :], P.cs_d.rearrange("p (a b) -> p a b", a=5), W=P.cs.b)
    em.copy("dve", P.identb[:, :], P.cs[:, 0, :], R=P.cs.b, W=P.identb.b)
    em.copy("dve", P.maskb[:, :], P.cs[:, 1, :], R=P.cs.b, W=P.maskb.b)
    em.copy("dve", P.onesb[:, :], P.cs[:, 4, :], R=P.cs.b, W=P.onesb.b)
    em.memset("dve", P.epsb[:, :], LN_EPS, W=P.epsb.b)
    for j in range(2):
        al = P.sm[:, _SL[f"alog{j}"]:_SL[f"alog{j}"] + 16]
        em.act(P.arow[:, j, :], al, AF.Exp, R=P.sm.b, W=P.arow.b)
    em.ts("dve", P.arow[:, :, :], P.arow[:, :, :], -1.0, None, ALU.mult, None, R=P.arow.b, W=P.arow.b)
    for i in range(DEPTH):
        em.memset("pool", P.stF[i][:, :, :], 0.0, W=P.stF[i].b)
    for j in range(2):
        em.memset("pool", P.stA[j][:, :, :], 0.0, W=P.stA[j].b)
        em.memset("pool", P.stB[j][:, :, :], 0.0, W=P.stB[j].b)
        em.memset("pool", P.stC[j][:, :, :], 0.0, W=P.stC[j].b)
        em.memset("pool", P.H[j][:, :], 0.0, W=P.H[j].b)
    xTr = P.xT.rearrange("(c p) t -> p c t", p=128)
    yTr = P.yT.rearrange("(c p) t -> p c t", p=128)
    P.issue_conv(P.layers[0])
    for t in range(n_tiles):
        t0 = t * T
        em.dma("act", P.ch_x, P.xf[:, :, :], xTr[:, :, t0:t0 + T], W=P.xf.b)
        for c in range(NCH):
            em.copy("dve", P.xb[:, c, :], P.xf[:, c, :], R=P.xf.bl(c), W=P.xb.bl(c))
        for li, i in enumerate(P.layers):
            if t == 0 and li + 1 < len(P.layers):
                P.issue_conv(P.layers[li + 1])
            j = i // 2
            pTr = P.pT[i].rearrange("(c p) t -> p c t", p=128)
            em.dma("pool", P.ch_p, P.pb[:, :, :], pTr[:, :, t0:t0 + T], W=P.pb.b)
            if i % 2 == 0:
                em.copy("act", P.Hb[:, :], P.H[j][:, :], R=P.H[j].b, W=P.Hb.b)
                even_mixer(P, i, t)
            else:
                odd_mixer(P, i, t)
            ffn_stage(P, i, t)
        em.dma("act", P.ch_y, yTr[:, :, t0:t0 + T], P.xf[:, :, :], R=P.xf.b)
    em.wait_all("act", P.xf.b)
    em.flush()
    return P


_CACHE = {}


def kernel(**inputs):
    x = np.asarray(inputs["x"], dtype=np.float32)
    p = np.asarray(inputs["p"], dtype=np.float32)
    B = x.shape[0]
    wf = pack_weights(inputs)
    sm = pack_smalls(inputs)
    cs = make_consts()
    if "P" not in _CACHE:
        _CACHE["P"] = build_program()
    P = _CACHE["P"]
    in_maps = []
    for b in range(B):
        in_maps.append({
            "xT": np.ascontiguousarray(x[b].T),
            "pT": np.ascontiguousarray(p[:, b].transpose(0, 2, 1)),
            "wf": wf, "smalls": sm, "consts": cs,
        })
    res = run_bass_kernel_spmd(P.nc, in_maps, core_ids=list(range(B)))
    out = np.stack([np.ascontiguousarray(r["yT"].T) for r in res.results], axis=0)
    return out.astype(np.float32)
```

```python
import numpy as np
import concourse.bass as bass
import concourse.mybir as mybir
from concourse.bass_utils import run_bass_kernel_spmd

F32 = mybir.dt.float32
BF16 = mybir.dt.bfloat16
AF = mybir.ActivationFunctionType
ALU = mybir.AluOpType

D = 1024
SEQ = 4096
DEPTH = 4
T = 512
NCH = D // 128
PLE = 256
DFF = 2816
NFF = DFF // 128
E_IN = 5136
LN_EPS = 1e-5
ALPHA = float((2.0 * DEPTH) ** 0.25)
WSLOT = 4096
NSLOT = 3
CONV_ROWS = 512


class Buf:
    __slots__ = ("name", "w", "r")

    def __init__(self, name=""):
        self.name = name
        self.w = None
        self.r = {}


class Emit:
    def __init__(self, nc):
        self.nc = nc
        self.eng = {"pe": nc.tensor, "act": nc.scalar, "dve": nc.vector, "pool": nc.gpsimd, "sp": nc.sync}
        self.sem = {k: nc.alloc_semaphore("s_" + k) for k in self.eng}
        self.cnt = {k: 0 for k in self.eng}
        self.waited = {}
        self.prog = {k: [] for k in self.eng}
        self.n_inst = 0
        self.n_wait = 0

    def dma_chan(self, name):
        key = "dma:" + name
        self.sem[key] = self.nc.alloc_semaphore("d_" + name)
        self.cnt[key] = 0
        return key

    def _need(self, e, deps):
        best = {}
        for k, c in deps:
            if best.get(k, -1) < c:
                best[k] = c
        for k, c in best.items():
            if self.waited.get((e, k), -1) >= c:
                continue
            self.prog[e].append(("w", self.sem[k], c))
            self.waited[(e, k)] = c
            self.n_wait += 1

    @staticmethod
    def _deps(reads, writes):
        deps = []
        for b in reads:
            if b.w is not None:
                deps.append(b.w)
        for b in writes:
            if b.w is not None:
                deps.append(b.w)
            deps.extend(b.r.items())
        return deps

    def op(self, e, fn, R=(), W=(), same_ok=False, inc=True):
        deps = self._deps(R, W)
        if same_ok:
            deps = [d for d in deps if d[0] != e]
        self._need(e, deps)
        if inc:
            self.cnt[e] += 1
            c = self.cnt[e]
            self.prog[e].append(("i", fn, self.sem[e], 1))
        else:
            c = self.cnt[e] + 1
            self.prog[e].append(("i", fn, self.sem[e], 0))
        for b in R:
            b.r[e] = c
        for b in W:
            b.w = (e, c)
            b.r = {}
        self.n_inst += 1

    def dma(self, q, chan, out, in_, R=(), W=(), **kw):
        self._need(q, self._deps(R, W))
        self.cnt[chan] += 16
        c = self.cnt[chan]
        eng = self.eng[q]
        self.prog[q].append(("i", (lambda: eng.dma_start(out=out, in_=in_, **kw)), self.sem[chan], 16))
        for b in R:
            b.r[chan] = c
        for b in W:
            b.w = (chan, c)
            b.r = {}
        self.n_inst += 1

    def wait_all(self, e, bufs):
        deps = []
        for b in bufs:
            if b.w is not None:
                deps.append(b.w)
            deps.extend(b.r.items())
        self._need(e, deps)

    def barrier(self):
        ks = ["pe", "act", "dve", "pool"]
        for e in ks:
            self._need(e, [(k, self.cnt[k]) for k in ks if k != e and self.cnt[k] > 0])

    def flush(self):
        with self.nc.Block() as block:
            for k, reg in (("sp", block.sync), ("act", block.scalar), ("dve", block.vector),
                           ("pool", block.gpsimd), ("pe", block.tensor)):
                def body(eng, prog=self.prog[k]):
                    for it in prog:
                        if it[0] == "w":
                            eng.wait_ge(it[1], it[2])
                        elif it[3]:
                            it[1]().then_inc(it[2], it[3])
                        else:
                            it[1]()
                reg(body)

    def mm(self, out, lhsT, rhs, start, stop, R, W, inc=None):
        nc = self.nc
        if inc is None:
            inc = bool(stop)
        self.op("pe", lambda: nc.tensor.matmul(out, lhsT=lhsT, rhs=rhs, start=start, stop=stop),
                R=R, W=W, same_ok=True, inc=inc)

    def tr(self, out, in_, ident, R, W, inc=True):
        nc = self.nc
        self.op("pe", lambda: nc.tensor.transpose(out, in_, ident), R=R, W=W, same_ok=True, inc=inc)

    def act(self, out, in_, func, R, W, bias=None, scale=None):
        nc = self.nc
        kw = {}
        if bias is not None:
            kw["bias"] = bias
        if scale is not None:
            kw["scale"] = scale
        self.op("act", lambda: nc.scalar.activation(out=out, in_=in_, func=func, **kw), R=R, W=W)

    def tt(self, e, out, in0, in1, op, R, W):
        eng = self.eng[e]
        self.op(e, lambda: eng.tensor_tensor(out=out, in0=in0, in1=in1, op=op), R=R, W=W)

    def stt(self, out, in0, scalar, in1, op0, op1, R, W):
        nc = self.nc
        self.op("dve", lambda: nc.vector.scalar_tensor_tensor(out=out, in0=in0, scalar=scalar, in1=in1,
                                                              op0=op0, op1=op1), R=R, W=W)

    def ts(self, e, out, in0, s1, s2, op0, op1, R, W):
        eng = self.eng[e]
        if op1 is None:
            self.op(e, lambda: eng.tensor_scalar(out=out, in0=in0, scalar1=s1, scalar2=None, op0=op0), R=R, W=W)
        else:
            self.op(e, lambda: eng.tensor_scalar(out=out, in0=in0, scalar1=s1, scalar2=s2, op0=op0, op1=op1),
                    R=R, W=W)

    def copy(self, e, out, in_, R, W):
        if e == "act":
            nc = self.nc
            self.op("act", lambda: nc.scalar.copy(out=out, in_=in_), R=R, W=W)
        else:
            eng = self.eng[e]
            self.op(e, lambda: eng.tensor_copy(out=out, in_=in_), R=R, W=W)

    def memset(self, e, ap, val, W):
        eng = self.eng[e]
        self.op(e, lambda: eng.memset(ap, val), W=W)

    def recip(self, out, in_, R, W):
        nc = self.nc
        self.op("dve", lambda: nc.vector.reciprocal(out=out, in_=in_), R=R, W=W)


class TL:
    def __init__(self, nc, name, shape, dtype, nb=1, psum=False):
        if psum:
            self.t = nc.alloc_psum_tensor(name, shape, dtype)
        else:
            self.t = nc.alloc_sbuf_tensor(name, shape, dtype)
        self.b = [Buf(f"{name}{i}") for i in range(nb)]
        self.shape = shape

    def __getitem__(self, k):
        return self.t[k]


def _even_in_groups():
    g = []
    for s, kind, c0 in ((1024, "a2", 0), (1536, "a2", 4), (0, "a1", 0), (512, "a1", 4),
                        (2048, "z", 0), (2560, "z", 4), (3072, "u", 0), (3584, "u", 4),
                        (4096, "u", 8), (4608, "u", 12)):
        g.append((list(range(s, s + 512)), [(kind, c0 + i) for i in range(4)]))
    return g


def _ffn_up_groups():
    g = []
    for i in range(5):
        g.append((list(range(512 * i, 512 * i + 512)), [("h1", 4 * i + k) for k in range(4)]))
        g.append((list(range(DFF + 512 * i, DFF + 512 * i + 512)), [("h2", 4 * i + k) for k in range(4)]))
    cols = list(range(2560, 2816)) + list(range(DFF + 2560, DFF + 2816))
    g.append((cols, [("h1", 20), ("h1", 21), ("h2", 20), ("h2", 21)]))
    return g


def layer_plan(i):
    j = i // 2
    plan = []
    if i % 2 == 0:
        for cols, tags in _even_in_groups():
            plan.append(dict(src="e_w_in", idx=j, cols=cols, KC=8, tags=tags, stage="ein"))
        plan.append(dict(src="e_w_in", idx=j, cols=list(range(5120, 5136)), KC=8, tags=[("dt", 0)], stage="edt"))
        for m in range(4):
            plan.append(dict(src="e_w_out", idx=j, cols=list(range(256 * m, 256 * m + 256)), KC=16,
                             tags=[("o", 2 * m), ("o", 2 * m + 1)], stage="eout"))
    else:
        for s, kind in ((2048, "v"), (2560, "v"), (1024, "cg"), (1536, "cg"), (0, "bg"), (512, "bg")):
            c0 = ((s % 1024) // 512) * 4
            plan.append(dict(src="o_w_in", idx=j, cols=list(range(s, s + 512)), KC=8,
                             tags=[(kind, c0 + k) for k in range(4)], stage="oin"))
        for m in range(2):
            plan.append(dict(src="o_w_out", idx=j, cols=list(range(512 * m, 512 * m + 512)), KC=8,
                             tags=[("o", 4 * m + k) for k in range(4)], stage="oout"))
    for cols, tags in _ffn_up_groups():
        plan.append(dict(src="f_w_up", idx=i, cols=cols, KC=8, tags=tags, stage="fup"))
    for m in range(8):
        plan.append(dict(src="f_w_down", idx=i, cols=list(range(128 * m, 128 * m + 128)), KC=NFF,
                         tags=[("d", m)], stage="fdown"))
    for m in range(2):
        plan.append(dict(src="ple_w_gate", idx=i, cols=list(range(512 * m, 512 * m + 512)), KC=8,
                         tags=[("g", 4 * m + k) for k in range(4)], stage="pgate"))
    plan.append(dict(src="ple_w_proj", idx=i, cols=list(range(1024)), KC=2, tags=[("p", k) for k in range(8)],
                     stage="pproj"))
    return plan


def full_plan():
    off = 0
    plans = []
    for i in range(DEPTH):
        p = layer_plan(i)
        for g in p:
            g["G"] = len(g["cols"])
            g["off"] = off
            n = 128 * g["KC"] * g["G"]
            off += n
        blk = CONV_ROWS * 2048
        off = ((off + blk - 1) // blk) * blk
        plans.append(p)
    return plans, off


_PLANS, _WTOTAL = full_plan()


def pack_weights(inp):
    flat = np.zeros(_WTOTAL, dtype=np.float32)
    for p in _PLANS:
        for g in p:
            Wm = np.asarray(inp[g["src"]][g["idx"]])
            sub = Wm[:, g["cols"]]
            sub = sub.reshape(g["KC"], 128, g["G"]).transpose(1, 0, 2)
            n = sub.size
            flat[g["off"]:g["off"] + n] = sub.reshape(-1)
    return flat.reshape(-1, 2048)


def _smalls_layout():
    lay = {}
    off = 0

    def add(name, n):
        nonlocal off
        lay[name] = off
        off += n
    for j in range(2):
        add(f"caw{j}", 8 * 31); add(f"cab{j}", 8); add(f"lag{j}", 8); add(f"lab{j}", 8)
        add(f"cbw{j}", 16 * 4); add(f"cbb{j}", 16); add(f"nbg{j}", 8); add(f"dsk{j}", 8)
        add(f"dtb{j}", 16); add(f"alog{j}", 16)
        add(f"ocw{j}", 8 * 3)
    for i in range(DEPTH):
        add(f"fcw{i}", 44 * 3); add(f"fcb{i}", 44)
        add(f"g1_{i}", 8); add(f"b1_{i}", 8); add(f"g2_{i}", 8); add(f"b2_{i}", 8)
    return lay, off


_SL, _NS = _smalls_layout()


def pack_smalls(inp):
    s = np.zeros((128, _NS), dtype=np.float32)

    def vec(name, v):
        v = np.asarray(v, dtype=np.float32)
        n = v.shape[0] // 128
        s[:, _SL[name]:_SL[name] + n] = v.reshape(n, 128).T

    def conv(name, w):
        w = np.asarray(w, dtype=np.float32)
        K, C = w.shape
        n = C // 128
        s[:, _SL[name]:_SL[name] + n * K] = w.reshape(K, n, 128).transpose(2, 1, 0).reshape(128, n * K)

    def row(name, v):
        v = np.asarray(v, dtype=np.float32)
        s[:, _SL[name]:_SL[name] + v.shape[0]] = v[None, :]

    for j in range(2):
        conv(f"caw{j}", inp["e_conv_a_w"][j]); vec(f"cab{j}", inp["e_conv_a_b"][j])
        vec(f"lag{j}", inp["e_ln_a_g"][j]); vec(f"lab{j}", inp["e_ln_a_b"][j])
        conv(f"cbw{j}", inp["e_conv_b_w"][j]); vec(f"cbb{j}", inp["e_conv_b_b"][j])
        vec(f"nbg{j}", inp["e_norm_b_g"][j])
        vec(f"dsk{j}", np.repeat(np.asarray(inp["e_d_skip"][j]), 64))
        row(f"dtb{j}", inp["e_dt_bias"][j]); row(f"alog{j}", inp["e_a_log"][j])
        conv(f"ocw{j}", inp["o_conv_w"][j])
    for i in range(DEPTH):
        conv(f"fcw{i}", inp["f_conv_w"][i]); vec(f"fcb{i}", inp["f_conv_b"][i])
        vec(f"g1_{i}", inp["ln_g"][i, 0]); vec(f"b1_{i}", inp["ln_b"][i, 0])
        vec(f"g2_{i}", inp["ln_g"][i, 1]); vec(f"b2_{i}", inp["ln_b"][i, 1])
    return s


def make_consts():
    c = np.zeros((128, 5, 128), dtype=np.float32)
    j = np.arange(128)[:, None]
    l = np.arange(128)[None, :]
    c[:, 0] = (j == l)
    c[:, 1] = (j <= l)
    c[:, 2] = (j > l)
    c[:, 3] = 1.0
    c[:, 4] = 1.0 / 1024.0
    return c.reshape(128, 640)


F32R = mybir.dt.float32r
GRAN = 256
NPE_A = 15


class View:
    def __init__(self, ap, grans_per_chunk):
        self.ap = ap
        self._g = grans_per_chunk
        self.b = []
        seen = set()
        for gl in grans_per_chunk:
            for g in gl:
                if id(g) not in seen:
                    seen.add(id(g))
                    self.b.append(g)

    def bl(self, c):
        return self._g[c]

    def __getitem__(self, k):
        return self.ap[k]


def _tl_bl(self, c):
    return [self.b[c]]


TL.bl = _tl_bl


class Prog:
    def __init__(self, n_tiles, layers, seq_len):
        self.n_tiles = n_tiles
        self.layers = layers
        self.seq_len = seq_len
        nc = bass.Bass("TRN2", target_bir_lowering=False)
        self.nc = nc
        em = Emit(nc)
        self.em = em
        L = seq_len
        self.xT = nc.dram_tensor("xT", [D, L], F32, kind="ExternalInput").ap()
        self.pT = nc.dram_tensor("pT", [DEPTH, PLE, L], F32, kind="ExternalInput").ap()
        self.wf = nc.dram_tensor("wf", [_WTOTAL // 2048, 2048], F32, kind="ExternalInput").ap()
        self.sm_d = nc.dram_tensor("smalls", [128, _NS], F32, kind="ExternalInput").ap()
        self.cs_d = nc.dram_tensor("consts", [128, 640], F32, kind="ExternalInput").ap()
        self.yT = nc.dram_tensor("yT", [D, L], F32, kind="ExternalOutput").ap()
        self.wsb = nc.dram_tensor("wsb", [_WTOTAL // 2048, 2048], BF16, kind="Internal").ap()
        self.conv_bufs = [Buf(f"cv{r}") for r in range(_WTOTAL // (2048 * CONV_ROWS))]
        self.ch_conv = [em.dma_chan(f"cv{r}") for r in range(_WTOTAL // (2048 * CONV_ROWS))]
        self.conv_done = set()

        self.sm = TL(nc, "sm", [128, _NS], F32)
        self.cs = TL(nc, "cs", [128, 5, 128], F32)
        self.identb = TL(nc, "identb", [128, 128], BF16)
        self.maskb = TL(nc, "maskb", [128, 128], BF16)
        self.arow = TL(nc, "arow", [128, 2, 16], F32)
        self.epsb = TL(nc, "epsb", [128, 1], F32)
        self.xf = TL(nc, "xf", [128, NCH, T], F32, nb=NCH)
        self.xb = TL(nc, "xb", [128, NCH, T], BF16, nb=NCH)
        self.pb = TL(nc, "pb", [128, 2, T], BF16)
        self.wr = [TL(nc, f"wr{s}", [128, WSLOT], BF16) for s in range(NSLOT)]
        self.ch_w = [em.dma_chan(f"w{s}") for s in range(NSLOT)]
        self.ch_x = em.dma_chan("x")
        self.ch_p = em.dma_chan("p")
        self.ch_y = em.dma_chan("y")
        self.ch_c = em.dma_chan("c")
        self.ch_c2 = em.dma_chan("c2")
        self.banks = [TL(nc, f"bk{i}", [128, 512], F32, psum=True) for i in range(8)]
        self.bk_i = 0
        self.held = set()
        self.mean = TL(nc, "mean", [128, T], F32)
        self.rstd = TL(nc, "rstd", [128, T], F32)
        self.var = TL(nc, "var", [128, T], F32)
        self.sq = [TL(nc, f"sq{i}", [128, T], BF16) for i in range(4)]
        self.xbt = [TL(nc, f"xbt{i}", [128, T], BF16) for i in range(4)]
        self.sq_i = 0
        self.onesb = TL(nc, "onesb", [128, 128], BF16)
        self.stF = [TL(nc, f"stF{i}", [128, 44, 2], F32) for i in range(DEPTH)]
        self.stA = [TL(nc, f"stA{j}", [128, 8, 30], BF16) for j in range(2)]
        self.stB = [TL(nc, f"stB{j}", [128, 16, 3], BF16) for j in range(2)]
        self.stC = [TL(nc, f"stC{j}", [128, 8, 2], BF16) for j in range(2)]
        self.H = [TL(nc, f"H{j}", [128, 1024], F32) for j in range(2)]
        self.Hb = TL(nc, "Hb", [128, 1024], BF16)
        ARENA = 120 * 1024
        self.arena = nc.alloc_sbuf_tensor("arena", [128, ARENA // 2], BF16)
        self.arena_size = ARENA
        self.gran = [Buf(f"gr{k}") for k in range(ARENA // GRAN)]

        self.wseq_n = 0
        self.wissued = 0
        self.wflat_seq = []
        for t in range(n_tiles):
            for i in layers:
                for g in _PLANS[i]:
                    self.wflat_seq.append((t, i, g))

    def carve(self, specs):
        out = {}
        off = 0
        top = 0
        offs = {}
        for spec in specs:
            name, shp, dt = spec[0], spec[1], spec[2]
            at = spec[3] if len(spec) > 3 else None
            esz = 4 if dt == F32 else 2
            nchunk = shp[0]
            cel = int(np.prod(shp[1:]))
            cb = cel * esz
            stride_b = (cb + GRAN - 1) // GRAN * GRAN
            if at == "top":
                off = top
            elif at is not None:
                off = offs[at]
            off = (off + GRAN - 1) // GRAN * GRAN
            offs[name] = off
            tot_b = nchunk * stride_b
            a = self.arena[:, off // 2: (off + tot_b) // 2]
            if dt == F32:
                a = a.bitcast(F32)
            a = a.rearrange("p (a s) -> p a s", a=nchunk)[:, :, 0:cel]
            if len(shp) == 3:
                a = a.rearrange("p a (b c) -> p a b c", b=shp[1])
            grans = []
            for k in range(nchunk):
                s0 = off + k * stride_b
                grans.append(self.gran[s0 // GRAN: (s0 + cb - 1) // GRAN + 1])
            out[name] = View(a, grans)
            off += tot_b
            top = max(top, off)
        assert top <= self.arena_size, (top, self.arena_size)
        self.arena_top = max(getattr(self, "arena_top", 0), top)
        return out

    def bank(self, hold=False):
        for _ in range(8):
            i = self.bk_i
            self.bk_i = (self.bk_i + 1) % 8
            if i not in self.held:
                if hold:
                    self.held.add(i)
                return self.banks[i]
        raise RuntimeError("all PSUM banks held")

    def release(self, bk):
        self.held.discard(self.banks.index(bk))

    def issue_conv(self, layer):
        if layer in self.conv_done or layer >= DEPTH:
            return
        self.conv_done.add(layer)
        p = _PLANS[layer]
        lo = p[0]["off"] // (2048 * CONV_ROWS)
        last = p[-1]
        hi = (last["off"] + 128 * last["KC"] * last["G"] + 2048 * CONV_ROWS - 1) // (2048 * CONV_ROWS)
        for r in range(lo, hi):
            self.em.dma("pool", self.ch_conv[r], self.wsb[r * CONV_ROWS:(r + 1) * CONV_ROWS, :],
                        self.wf[r * CONV_ROWS:(r + 1) * CONV_ROWS, :], W=[self.conv_bufs[r]])

    def _issue_w(self, n):
        t, i, g = self.wflat_seq[n]
        s = n % NSLOT
        KC, G = g["KC"], g["G"]
        cnt = 128 * KC * G
        r0 = g["off"] // (2048 * CONV_ROWS)
        r1 = (g["off"] + cnt - 1) // (2048 * CONV_ROWS)
        src = bass.AP(self.wsb.tensor, g["off"], [[KC * G, 128], [1, KC * G]])
        self.em.dma("sp", self.ch_w[s], self.wr[s][:, 0:KC * G], src,
                    R=[self.conv_bufs[r] for r in range(r0, r1 + 1)], W=self.wr[s].b)

    def wget(self, g):
        n = self.wseq_n
        assert self.wflat_seq[n][2] is g, (n, g["src"], self.wflat_seq[n][2]["src"])
        while self.wissued < min(len(self.wflat_seq), n + NSLOT):
            self._issue_w(self.wissued)
            self.wissued += 1
        self.wseq_n += 1
        s = n % NSLOT
        KC, G = g["KC"], g["G"]
        v = self.wr[s][:, 0:KC * G].rearrange("p (k g) -> p k g", k=KC)
        return self.wr[s], v

    def sms(self, name, idx):
        o = _SL[name] + idx
        return self.sm[:, o:o + 1]


class Skew:
    def __init__(self, depth=1):
        self.depth = depth
        self.q = []

    def push(self, fn):
        self.q.append(fn)
        while len(self.q) > self.depth:
            self.q.pop(0)()

    def flush(self):
        while self.q:
            self.q.pop(0)()


class Stats:
    def __init__(self, P, want_mean=True):
        self.P = P
        self.want_mean = want_mean
        self.bm = P.bank(hold=True) if want_mean else None
        self.bq = P.bank(hold=True)
        self.n = 0
        self.pend = None

    def _flush(self):
        if self.pend is not None:
            self.pend()
            self.pend = None

    def add(self, ap, bufs):
        P, em = self.P, self.P.em
        onesb = P.onesb[:, :]
        sq = P.sq[P.sq_i]
        xbt = P.xbt[P.sq_i]
        P.sq_i = (P.sq_i + 1) % 4
        c = self.n
        if c % 2 == 0:
            em.act(sq[:, :], ap, AF.Square, R=bufs, W=sq.b)
        else:
            em.tt("dve", sq[:, :], ap, ap, ALU.mult, R=bufs, W=sq.b)
        if self.want_mean:
            em.copy("act", xbt[:, :], ap, R=bufs, W=xbt.b)
        self._flush()

        def mms():
            if self.want_mean:
                em.mm(self.bm[:, :], onesb, xbt[:, :], c == 0, c == NCH - 1, R=xbt.b + P.onesb.b, W=self.bm.b, inc=True)
            em.mm(self.bq[:, :], onesb, sq[:, :], c == 0, c == NCH - 1, R=sq.b + P.onesb.b, W=self.bq.b, inc=True)
        self.pend = mms
        self.n += 1

    def finish(self):
        P, em = self.P, self.P.em
        assert self.n == NCH
        self._flush()
        if self.want_mean:
            em.copy("act", P.mean[:, :], self.bm[:, :], R=self.bm.b, W=P.mean.b)
            em.act(P.var[:, :], self.bm[:, :], AF.Square, R=self.bm.b, W=P.var.b)
            em.tt("dve", P.var[:, :], self.bq[:, :], P.var[:, :], ALU.subtract, R=self.bq.b + P.var.b, W=P.var.b)
            em.act(P.var[:, :], P.var[:, :], AF.Sqrt, R=P.var.b + P.epsb.b, W=P.var.b, bias=P.epsb[:, 0:1])
            P.release(self.bm)
        else:
            em.act(P.var[:, :], self.bq[:, :], AF.Sqrt, R=self.bq.b + P.epsb.b, W=P.var.b, bias=P.epsb[:, 0:1])
        P.release(self.bq)
        em.recip(P.rstd[:, :], P.var[:, :], R=P.var.b, W=P.rstd.b)


def residual_ln(P, st, gname, bname):
    em = P.em
    st.finish()
    for c in range(NCH):
        xb_ = P.xf.bl(c)
        em.tt("dve", P.xf[:, c, :], P.xf[:, c, :], P.mean[:, :], ALU.subtract, R=xb_ + P.mean.b, W=xb_)
        em.tt("dve", P.xf[:, c, :], P.xf[:, c, :], P.rstd[:, :], ALU.mult, R=xb_ + P.rstd.b, W=xb_)
        em.act(P.xb[:, c, :], P.xf[:, c, :], AF.Identity, R=xb_ + P.sm.b, W=P.xb.bl(c),
               bias=P.sms(bname, c), scale=P.sms(gname, c))
        em.ts("pool", P.xf[:, c, :], P.xf[:, c, :], P.sms(gname, c), P.sms(bname, c), ALU.mult, ALU.add,
              R=xb_ + P.sm.b, W=xb_)


def build_diag(P, dst_ap, taps, wname, cidx, nt, W):
    em = P.em
    for n, k in enumerate(taps):
        em.ts("pool", dst_ap[:, n, :], P.identb[:, :], P.sms(wname, cidx * nt + k), 0.0, ALU.mult, ALU.add,
              R=P.identb.b + P.sm.b, W=W)


def dve_taps(P, acc_ap, acc_b, src_ap_fn, src_b, taps, wname, cidx, nt):
    em = P.em
    for k in taps:
        em.stt(acc_ap, src_ap_fn(k), P.sms(wname, cidx * nt + k), acc_ap, ALU.mult, ALU.add,
               R=src_b + acc_b + P.sm.b, W=acc_b)


def ffn_stage(P, i, tile):
    em = P.em
    A = P.carve([("g", [NFF, T], BF16), ("hs", [4, T + 2], F32), ("acc", [4, T], F32),
                 ("s1", [4, T], BF16), ("ffo", [NCH, T], F32)])
    plan = [g for g in _PLANS[i] if g["stage"] == "fup"]
    n_i = 0
    stF = P.stF[i]
    s1slot = {}
    sk = Skew(1)
    fcw, fcb = f"fcw{i}", f"fcb{i}"
    for g in plan:
        slot, wv = P.wget(g)
        for mi, (kind, ch) in enumerate(g["tags"]):
            cidx = ch if kind == "h1" else NFF + ch
            bk = P.bank()
            for kc in range(8):
                em.mm(bk[:, :], wv[:, kc, mi * 128:(mi + 1) * 128], P.xb[:, kc, :], kc == 0, kc == 7,
                      R=slot.b + P.xb.bl(kc), W=bk.b)
            hi = n_i % 4
            n_i += 1
            hb = A["hs"].bl(hi)
            ab_ = A["acc"].bl(hi)
            em.copy("pool", A["hs"][:, hi, 0:2], stF[:, cidx, :], R=stF.b, W=hb)
            em.copy("act", A["hs"][:, hi, 2:T + 2], bk[:, :], R=bk.b, W=hb)
            em.act(A["acc"][:, hi, :], bk[:, :], AF.Identity, R=bk.b + P.sm.b, W=ab_,
                   bias=P.sms(fcb, cidx), scale=P.sms(fcw, cidx * 3 + 2))
            em.copy("pool", stF[:, cidx, :], A["hs"][:, hi, T:T + 2], R=hb, W=stF.b)

            def tail(hi=hi, hb=hb, ab_=ab_, kind=kind, ch=ch, cidx=cidx):
                dve_taps(P, A["acc"][:, hi, :], ab_, lambda k: A["hs"][:, hi, k:k + T], hb, [1, 0], fcw, cidx, 3)
                if kind == "h1":
                    si = ch % 4
                    s1slot[ch] = si
                    em.act(A["s1"][:, si, :], A["acc"][:, hi, :], AF.Silu, R=ab_, W=A["s1"].bl(si))
                else:
                    si = s1slot[ch]
                    em.tt("dve", A["g"][:, ch, :], A["acc"][:, hi, :], A["s1"][:, si, :], ALU.mult,
                          R=ab_ + A["s1"].bl(si), W=A["g"].bl(ch))
            sk.push(tail)
    sk.flush()
    downs = [g for g in _PLANS[i] if g["stage"] == "fdown"]
    gate_groups = [g for g in _PLANS[i] if g["stage"] == "pgate"]
    proj_group = [g for g in _PLANS[i] if g["stage"] == "pproj"][0]
    for g in downs:
        slot, wv = P.wget(g)
        m = g["tags"][0][1]
        bk = P.bank()
        for kc in range(NFF):
            em.mm(bk[:, :], wv[:, kc, :], A["g"][:, kc, :], kc == 0, kc == NFF - 1,
                  R=slot.b + A["g"].bl(kc), W=bk.b)
        em.stt(P.xf[:, m, :], P.xf[:, m, :], ALPHA, bk[:, :], ALU.mult, ALU.add, R=bk.b + P.xf.bl(m), W=P.xf.bl(m))
    for g in gate_groups:
        slot, wv = P.wget(g)
        for mi, (kind, m) in enumerate(g["tags"]):
            bk = P.bank()
            for kc in range(8):
                em.mm(bk[:, :], wv[:, kc, mi * 128:(mi + 1) * 128], P.xb[:, kc, :], kc == 0, kc == 7,
                      R=slot.b + P.xb.bl(kc), W=bk.b)
            em.act(A["ffo"][:, m, :], bk[:, :], AF.Sigmoid, R=bk.b, W=A["ffo"].bl(m))
    slot, wv = P.wget(proj_group)
    st = Stats(P)
    for m in range(NCH):
        bk = P.bank()
        for kc in range(2):
            em.mm(bk[:, :], wv[:, kc, m * 128:(m + 1) * 128], P.pb[:, kc, :], kc == 0, kc == 1,
                  R=slot.b + P.pb.b, W=bk.b)
        em.tt("dve", A["ffo"][:, m, :], bk[:, :], A["ffo"][:, m, :], ALU.mult, R=bk.b + A["ffo"].bl(m), W=A["ffo"].bl(m))
        em.tt("dve", P.xf[:, m, :], P.xf[:, m, :], A["ffo"][:, m, :], ALU.add, R=P.xf.bl(m) + A["ffo"].bl(m), W=P.xf.bl(m))
        st.add(P.xf[:, m, :], P.xf.bl(m))
    residual_ln(P, st, f"g2_{i}", f"b2_{i}")


def odd_mixer(P, i, tile):
    em = P.em
    j = i // 2
    A = P.carve([("v", [NCH, T], BF16), ("cv", [NCH, T + 2], BF16), ("bg", [NCH, T], BF16),
                 ("mx", [NCH, T], BF16), ("acc", [2, T], F32)])
    stC = P.stC[j]
    ocw = f"ocw{j}"
    em.copy("pool", A["cv"][:, :, 0:2], stC[:, :, :], R=stC.b, W=A["cv"].b)
    for g in [g for g in _PLANS[i] if g["stage"] == "oin"]:
        slot, wv = P.wget(g)
        for mi, (kind, c) in enumerate(g["tags"]):
            bk = P.bank()
            for kc in range(8):
                em.mm(bk[:, :], wv[:, kc, mi * 128:(mi + 1) * 128], P.xb[:, kc, :], kc == 0, kc == 7,
                      R=slot.b + P.xb.bl(kc), W=bk.b)
            if kind == "v":
                em.copy("act", A["v"][:, c, :], bk[:, :], R=bk.b, W=A["v"].bl(c))
            elif kind == "cg":
                em.tt("dve", A["cv"][:, c, 2:T + 2], bk[:, :], A["v"][:, c, :], ALU.mult,
                      R=bk.b + A["v"].bl(c), W=A["cv"].bl(c))
            else:
                em.copy("act", A["bg"][:, c, :], bk[:, :], R=bk.b, W=A["bg"].bl(c))
    em.copy("pool", stC[:, :, :], A["cv"][:, :, T:T + 2], R=A["cv"].b, W=stC.b)
    for c in range(NCH):
        ai = c % 2
        ab_ = A["acc"].bl(ai)
        em.ts("dve", A["acc"][:, ai, :], A["cv"][:, c, 2:T + 2], P.sms(ocw, c * 3 + 2), None, ALU.mult, None,
              R=A["cv"].bl(c) + P.sm.b, W=ab_)
        dve_taps(P, A["acc"][:, ai, :], ab_, lambda k: A["cv"][:, c, k:k + T], A["cv"].bl(c), [1, 0], ocw, c, 3)
        em.tt("dve", A["mx"][:, c, :], A["acc"][:, ai, :], A["bg"][:, c, :], ALU.mult, R=ab_ + A["bg"].bl(c), W=A["mx"].bl(c))
    st = Stats(P)
    for g in [g for g in _PLANS[i] if g["stage"] == "oout"]:
        slot, wv = P.wget(g)
        for mi, (kind, m) in enumerate(g["tags"]):
            bk = P.bank()
            for kc in range(8):
                em.mm(bk[:, :], wv[:, kc, mi * 128:(mi + 1) * 128], A["mx"][:, kc, :], kc == 0, kc == 7,
                      R=slot.b + A["mx"].bl(kc), W=bk.b)
            em.stt(P.xf[:, m, :], P.xf[:, m, :], ALPHA, bk[:, :], ALU.mult, ALU.add, R=bk.b + P.xf.bl(m), W=P.xf.bl(m))
            st.add(P.xf[:, m, :], P.xf.bl(m))
    residual_ln(P, st, f"g1_{i}", f"b1_{i}")


def even_mixer(P, i, tile):
    em = P.em
    j = i // 2
    A = P.carve([
        ("sgya", [NCH, T], BF16),
        ("zs", [NCH, T], BF16),
        ("yb", [NCH, T], BF16, "zs"),
        ("ub", [16, T + 3], BF16),
        ("cf", [NCH, T], F32),
        ("ys", [NCH, T], F32, "cf"),
        ("acc", [4, T], F32, "top"),
        ("dtt", [4, 16], F32), ("dta", [4, 16], F32),
        ("ab", [NCH, T + 30], BF16),
        ("dga", [2, 31, 128], BF16),
        ("xdt", [2, 16, 64], BF16, "ab"), ("xdd", [2, 16, 64], BF16), ("btok", [2, 4, 128], BF16),
        ("rhsu", [2, 16, 128], F32), ("E", [2, 16, 128], BF16), ("cbm", [2, 4, 128], BF16),
        ("Wp", [2, 16, 128], BF16), ("ytok", [2, 8, 128], F32), ("ex3", [2, 48], F32),
    ])
    stA, stB = P.stA[j], P.stB[j]
    H = P.H[j]
    caw, cbw = f"caw{j}", f"cbw{j}"
    em.copy("pool", A["ab"][:, :, 0:30], stA[:, :, :], R=stA.b, W=A["ab"].b)
    em.copy("pool", A["ub"][:, :, 0:3], stB[:, :, :], R=stB.b, W=A["ub"].b)
    st_a = Stats(P)

    caw_off = _SL[caw]

    def build_diag_a(c):
        di = c % 2
        wv_ = P.sm[:, caw_off + c * 31: caw_off + (c + 1) * 31]
        em.tt("dve", A["dga"][:, di], P.identb[:, :].unsqueeze(1).to_broadcast([128, 31, 128]),
              wv_.unsqueeze(2).to_broadcast([128, 31, 128]), ALU.mult,
              R=P.identb.b + P.sm.b, W=A["dga"].bl(di))

    def conv_a_chunk(c):
        di = c % 2
        cfb = A["cf"].bl(c)
        dgb = A["dga"].bl(di)
        bk = P.bank()
        for k in range(31):
            em.mm(bk[:, :], A["dga"][:, di, k, :], A["ab"][:, c, k:k + T], k == 0, k == 30,
                  R=dgb + A["ab"].bl(c), W=bk.b)
        em.act(A["cf"][:, c, :], bk[:, :], AF.Identity, R=bk.b + P.sm.b, W=cfb, bias=P.sms(f"cab{j}", c))
        st_a.add(A["cf"][:, c, :], cfb)
        if c + 2 < NCH:
            build_diag_a(c + 2)

    build_diag_a(0)
    build_diag_a(1)
    sk = Skew(1)
    conv_sched = {4: [0, 1], 5: [2, 3], 6: [4], 7: [5], 8: [6], 9: [7]}
    for gi, g in enumerate([g for g in _PLANS[i] if g["stage"] == "ein"]):
        slot, wv = P.wget(g)
        for mi, (kind, c) in enumerate(g["tags"]):
            bk = P.bank()
            for kc in range(8):
                em.mm(bk[:, :], wv[:, kc, mi * 128:(mi + 1) * 128], P.xb[:, kc, :], kc == 0, kc == 7,
                      R=slot.b + P.xb.bl(kc), W=bk.b)
            if kind == "a2":
                em.act(A["sgya"][:, c, :], bk[:, :], AF.Sigmoid, R=bk.b, W=A["sgya"].bl(c))
            elif kind == "a1":
                em.tt("dve", A["ab"][:, c, 30:T + 30], bk[:, :], A["sgya"][:, c, :], ALU.mult,
                      R=bk.b + A["sgya"].bl(c), W=A["ab"].bl(c))
            elif kind == "z":
                em.act(A["zs"][:, c, :], bk[:, :], AF.Silu, R=bk.b, W=A["zs"].bl(c))
            else:
                ai = c % 4
                em.copy("act", A["ub"][:, c, 3:T + 3], bk[:, :], R=bk.b, W=A["ub"].bl(c))
                em.act(A["acc"][:, ai, :], bk[:, :], AF.Identity, R=bk.b + P.sm.b, W=A["acc"].bl(ai),
                       bias=P.sms(f"cbb{j}", c), scale=P.sms(cbw, c * 4 + 3))

                def tail(c=c, ai=ai):
                    dve_taps(P, A["acc"][:, ai, :], A["acc"].bl(ai), lambda k: A["ub"][:, c, k:k + T],
                             A["ub"].bl(c), [2, 1, 0], cbw, c, 4)
                    em.copy("pool", stB[:, c, :], A["ub"][:, c, T:T + 3], R=A["ub"].bl(c), W=stB.b)
                    em.act(A["ub"][:, c, 3:T + 3], A["acc"][:, ai, :], AF.Silu, R=A["acc"].bl(ai), W=A["ub"].bl(c))
                sk.push(tail)
        if gi == 3:
            em.copy("pool", stA[:, :, :], A["ab"][:, :, T:T + 30], R=A["ab"].b, W=stA.b)
        for c in conv_sched.get(gi, []):
            conv_a_chunk(c)
    sk.flush()
    g = [g for g in _PLANS[i] if g["stage"] == "edt"][0]
    slot, wv = P.wget(g)
    bkd = P.bank()
    for q in range(4):
        for kc in range(8):
            em.mm(bkd[:, q * 16:(q + 1) * 16], P.xb[:, kc, q * 128:(q + 1) * 128], wv[:, kc, :], kc == 0, kc == 7,
                  R=slot.b + P.xb.bl(kc), W=bkd.b)
    dtb = P.sm[:, _SL[f"dtb{j}"]:_SL[f"dtb{j}"] + 16]
    em.tt("dve", A["dtt"][:, :, :], bkd[:, 0:64].rearrange("p (q h) -> p q h", q=4),
          dtb.unsqueeze(1).to_broadcast([128, 4, 16]), ALU.add, R=bkd.b + P.sm.b, W=A["dtt"].b)
    em.act(A["dtt"][:, :, :], A["dtt"][:, :, :], AF.Exp, R=A["dtt"].b, W=A["dtt"].b)
    em.act(A["dtt"][:, :, :], A["dtt"][:, :, :], AF.Ln, R=A["dtt"].b, W=A["dtt"].b, bias=P.cs[:, 3, 0:1])
    em.tt("dve", A["dta"][:, :, :], A["dtt"][:, :, :], P.arow[:, j, :].unsqueeze(1).to_broadcast([128, 4, 16]),
          ALU.mult, R=A["dtt"].b + P.arow.b, W=A["dta"].b)
    st_a.finish()
    for c in range(NCH):
        cfb = A["cf"].bl(c)
        em.tt("dve", A["cf"][:, c, :], A["cf"][:, c, :], P.mean[:, :], ALU.subtract, R=cfb + P.mean.b, W=cfb)
        em.tt("dve", A["cf"][:, c, :], A["cf"][:, c, :], P.rstd[:, :], ALU.mult, R=cfb + P.rstd.b, W=cfb)
        em.act(A["sgya"][:, c, :], A["cf"][:, c, :], AF.Silu, R=cfb + P.sm.b, W=A["sgya"].bl(c),
               bias=P.sms(f"lab{j}", c), scale=P.sms(f"lag{j}", c))

    U = P.cs[:, 1, :]
    SLm = P.cs[:, 2, :]
    ones = P.cs[:, 3, :]
    identf = P.cs[:, 0, :]
    csb = P.cs.b
    ctx = {}

    def front(q):
        qi = q % 2
        tq = slice(3 + q * 128, 3 + (q + 1) * 128)
        xdt, xdd, btok, rhsu, E_, cbm, Wp = (A[n][:, qi] for n in ("xdt", "xdd", "btok", "rhsu", "E", "cbm", "Wp"))
        b_xdt, b_xdd, b_btok, b_rhsu, b_E, b_cbm, b_Wp, b_ex3 = (
            A[n].bl(qi) for n in ("xdt", "xdd", "btok", "rhsu", "E", "cbm", "Wp", "ex3"))
        ex3 = A["ex3"][:, qi, :]
        bA = P.bank()
        bB = P.bank()
        pa = bA[:, :].bitcast(BF16).rearrange("p (c l) -> p c l", c=8)
        pb_ = bB[:, 0:256].bitcast(BF16).rearrange("p (c l) -> p c l", c=4)
        for c in range(8):
            em.tr(pa[:, c, :], A["ub"][:, c, tq], P.identb[:, :], R=A["ub"].bl(c) + P.identb.b, W=bA.b, inc=(c == 7))
        for c in range(4):
            em.tr(pb_[:, c, :], A["ub"][:, 8 + c, tq], P.identb[:, :], R=A["ub"].bl(8 + c) + P.identb.b, W=bB.b, inc=(c == 3))
        em.tt("dve", xdt, bA[:, :].bitcast(BF16).rearrange("p (h d) -> p h d", h=16),
              A["dtt"][:, q, :].unsqueeze(2).to_broadcast([128, 16, 64]), ALU.mult,
              R=bA.b + A["dtt"].b, W=b_xdt)
        em.copy("act", btok, pb_, R=bB.b, W=b_btok)
        bS = P.bank()
        dta_q = A["dta"][:, q, :]
        em.mm(bS[:, 0:16], U, dta_q, True, True, R=csb + A["dta"].b, W=bS.b, inc=False)
        em.mm(bS[:, 16:32], SLm, dta_q, True, True, R=csb + A["dta"].b, W=bS.b, inc=False)
        em.mm(bS[:, 32:48], ones, dta_q, True, True, R=csb + A["dta"].b, W=bS.b)
        em.act(ex3, bS[:, 0:48], AF.Exp, R=bS.b, W=b_ex3)
        em.tt("dve", rhsu, dta_q.unsqueeze(2).to_broadcast([128, 16, 128]),
              U.unsqueeze(1).to_broadcast([128, 16, 128]), ALU.mult, R=A["dta"].b + csb, W=b_rhsu)
        em.tt("dve", xdd, xdt, ex3[:, 16:32].unsqueeze(2).to_broadcast([128, 16, 64]), ALU.mult,
              R=b_xdt + b_ex3, W=b_xdd)
        bC = P.bank()
        for gg in range(4):
            em.mm(bC[:, gg * 128:(gg + 1) * 128], A["ub"][:, 8 + gg, tq], A["ub"][:, 12 + gg, tq], True, True,
                  R=A["ub"].bl(8 + gg) + A["ub"].bl(12 + gg), W=bC.b, inc=(gg == 3))
        em.tt("dve", cbm, bC[:, :].rearrange("p (g l) -> p g l", g=4),
              P.maskb[:, :].unsqueeze(1).to_broadcast([128, 4, 128]), ALU.mult, R=bC.b + P.maskb.b, W=b_cbm)
        for b4 in range(4):
            bk = P.bank()
            em.mm(bk[:, :], SLm, rhsu[:, 4 * b4:4 * b4 + 4, :].rearrange("p h l -> p (h l)"), True, True,
                  R=csb + b_rhsu, W=bk.b)
            em.act(E_[:, 4 * b4:4 * b4 + 4, :].rearrange("p h l -> p (h l)"), bk[:, :], AF.Exp, R=bk.b, W=b_E)
        em.tt("dve", Wp.rearrange("p (g h) l -> p g h l", g=4), E_.rearrange("p (g h) l -> p g h l", g=4),
              cbm.unsqueeze(2).to_broadcast([128, 4, 4, 128]), ALU.mult, R=b_E + b_cbm, W=b_Wp)
        bY0, bY1 = P.bank(hold=True), P.bank(hold=True)
        for h in range(16):
            bk = bY0 if h < 8 else bY1
            em.mm(bk[:, (h % 8) * 64:(h % 8 + 1) * 64], Wp[:, h, :], xdt[:, h, :], True, True,
                  R=b_Wp + b_xdt, W=bk.b, inc=(h % 8 == 7))
        ctx[q] = (bY0, bY1)

    def back(q):
        qi = q % 2
        tq = slice(3 + q * 128, 3 + (q + 1) * 128)
        xdd, btok, ytok = (A[n][:, qi] for n in ("xdd", "btok", "ytok"))
        b_xdd, b_btok, b_ytok, b_ex3 = (A[n].bl(qi) for n in ("xdd", "btok", "ytok", "ex3"))
        ex3 = A["ex3"][:, qi, :]
        bY0, bY1 = ctx.pop(q)
        bH0, bH1 = P.bank(), P.bank()
        bO0, bO1 = P.bank(), P.bank()
        for gg in range(4):
            bk = bO0 if gg < 2 else bO1
            em.mm(bk[:, (gg % 2) * 256:(gg % 2 + 1) * 256], A["ub"][:, 12 + gg, tq], P.Hb[:, gg * 256:(gg + 1) * 256],
                  True, True, R=A["ub"].bl(12 + gg) + P.Hb.b, W=bk.b, inc=(gg % 2 == 1))
        for gg in range(4):
            bk = bH0 if gg < 2 else bH1
            em.mm(bk[:, (gg % 2) * 256:(gg % 2 + 1) * 256], btok[:, gg, :],
                  xdd[:, 4 * gg:4 * gg + 4, :].rearrange("p h d -> p (h d)"), True, True,
                  R=b_btok + b_xdd, W=bk.b, inc=(gg % 2 == 1))
        em.tt("dve", H[:, :].rearrange("p (h d) -> p h d", h=16), H[:, :].rearrange("p (h d) -> p h d", h=16),
              ex3[:, 32:48].unsqueeze(2).to_broadcast([128, 16, 64]), ALU.mult, R=H.b + b_ex3, W=H.b)
        for hh, bk in enumerate((bH0, bH1)):
            em.tt("dve", H[:, hh * 512:(hh + 1) * 512], H[:, hh * 512:(hh + 1) * 512], bk[:, :], ALU.add,
                  R=H.b + bk.b, W=H.b)
        em.copy("act", P.Hb[:, :], H[:, :], R=H.b, W=P.Hb.b)
        for hh, (bo, by) in enumerate(((bO0, bY0), (bO1, bY1))):
            yt = ytok[:, 4 * hh:4 * hh + 4, :].rearrange("p c l -> p (c l)")
            em.tt("dve", yt.rearrange("p (h d) -> p h d", h=8), bo[:, :].rearrange("p (h d) -> p h d", h=8),
                  ex3[:, 8 * hh:8 * hh + 8].unsqueeze(2).to_broadcast([128, 8, 64]), ALU.mult,
                  R=bo.b + b_ex3, W=b_ytok)
            em.tt("dve", yt, yt, by[:, :], ALU.add, R=by.b + b_ytok, W=b_ytok)
        P.release(bY0)
        P.release(bY1)
        bT0, bT1 = P.bank(), P.bank()
        for c in range(8):
            bk = bT0 if c < 4 else bT1
            em.tr(bk[:, (c % 4) * 128:(c % 4 + 1) * 128], ytok[:, c, :], identf, R=b_ytok + csb, W=bk.b, inc=(c % 4 == 3))
        for hh, bk in enumerate((bT0, bT1)):
            wb_ = []
            for c in range(4 * hh, 4 * hh + 4):
                wb_ += A["ys"].bl(c)
            em.copy("act", A["ys"][:, 4 * hh:4 * hh + 4, q * 128:(q + 1) * 128],
                    bk[:, :].rearrange("p (c l) -> p c l", c=4), R=bk.b, W=wb_)

    front(0)
    for q in range(4):
        if q + 1 < 4:
            front(q + 1)
        back(q)
    st_b = Stats(P, want_mean=False)
    for c in range(NCH):
        ysb = A["ys"].bl(c)
        em.stt(A["ys"][:, c, :], A["ub"][:, c, 3:T + 3], P.sms(f"dsk{j}", c), A["ys"][:, c, :], ALU.mult, ALU.add,
               R=A["ub"].bl(c) + ysb + P.sm.b, W=ysb)
        em.tt("dve", A["ys"][:, c, :], A["ys"][:, c, :], A["zs"][:, c, :], ALU.mult, R=ysb + A["zs"].bl(c), W=ysb)
        st_b.add(A["ys"][:, c, :], ysb)
    st_b.finish()
    for c in range(NCH):
        em.stt(A["yb"][:, c, :], A["ys"][:, c, :], P.sms(f"nbg{j}", c), P.rstd[:, :], ALU.mult, ALU.mult,
               R=A["ys"].bl(c) + P.sm.b + P.rstd.b, W=A["yb"].bl(c))
    st = Stats(P)
    for g in [g for g in _PLANS[i] if g["stage"] == "eout"]:
        slot, wv = P.wget(g)
        for mi, (kind, m) in enumerate(g["tags"]):
            bk = P.bank()
            for kc in range(16):
                src = A["sgya"] if kc < 8 else A["yb"]
                em.mm(bk[:, :], wv[:, kc, mi * 128:(mi + 1) * 128], src[:, kc % 8, :], kc == 0, kc == 15,
                      R=slot.b + src.bl(kc % 8), W=bk.b)
            em.stt(P.xf[:, m, :], P.xf[:, m, :], ALPHA, bk[:, :], ALU.mult, ALU.add, R=bk.b + P.xf.bl(m), W=P.xf.bl(m))
            st.add(P.xf[:, m, :], P.xf.bl(m))
    residual_ln(P, st, f"g1_{i}", f"b1_{i}")


def build_program(n_tiles=SEQ // T, layers=(0, 1, 2, 3), seq_len=SEQ):
    P = Prog(n_tiles, list(layers), seq_len)
    nc, em = P.nc, P.em
    em.dma("sp", P.ch_c, P.sm[:, :], P.sm_d, W=P.sm.b)
    em.dma("sp", P.ch_c2, P.cs[:, :, :], P.cs_d.rearrange("p (a b) -> p a b", a=5), W=P.cs.b)
    em.copy("dve", P.identb[:, :], P.cs[:, 0, :], R=P.cs.b, W=P.identb.b)
    em.copy("dve", P.maskb[:, :], P.cs[:, 1, :], R=P.cs.b, W=P.maskb.b)
    em.copy("dve", P.onesb[:, :], P.cs[:, 4, :], R=P.cs.b, W=P.onesb.b)
    em.memset("dve", P.epsb[:, :], LN_EPS, W=P.epsb.b)
    for j in range(2):
        al = P.sm[:, _SL[f"alog{j}"]:_SL[f"alog{j}"] + 16]
        em.act(P.arow[:, j, :], al, AF.Exp, R=P.sm.b, W=P.arow.b)
    em.ts("dve", P.arow[:, :, :], P.arow[:, :, :], -1.0, None, ALU.mult, None, R=P.arow.b, W=P.arow.b)
    for i in range(DEPTH):
        em.memset("pool", P.stF[i][:, :, :], 0.0, W=P.stF[i].b)
    for j in range(2):
        em.memset("pool", P.stA[j][:, :, :], 0.0, W=P.stA[j].b)
        em.memset("pool", P.stB[j][:, :, :], 0.0, W=P.stB[j].b)
        em.memset("pool", P.stC[j][:, :, :], 0.0, W=P.stC[j].b)
        em.memset("pool", P.H[j][:, :], 0.0, W=P.H[j].b)
    xTr = P.xT.rearrange("(c p) t -> p c t", p=128)
    yTr = P.yT.rearrange("(c p) t -> p c t", p=128)
    P.issue_conv(P.layers[0])
    for t in range(n_tiles):
        t0 = t * T
        em.dma("act", P.ch_x, P.xf[:, :, :], xTr[:, :, t0:t0 + T], W=P.xf.b)
        for c in range(NCH):
            em.copy("dve", P.xb[:, c, :], P.xf[:, c, :], R=P.xf.bl(c), W=P.xb.bl(c))
        for li, i in enumerate(P.layers):
            if t == 0 and li + 1 < len(P.layers):
                P.issue_conv(P.layers[li + 1])
            j = i // 2
            pTr = P.pT[i].rearrange("(c p) t -> p c t", p=128)
            em.dma("pool", P.ch_p, P.pb[:, :, :], pTr[:, :, t0:t0 + T], W=P.pb.b)
            if i % 2 == 0:
                em.copy("act", P.Hb[:, :], P.H[j][:, :], R=P.H[j].b, W=P.Hb.b)
                even_mixer(P, i, t)
            else:
                odd_mixer(P, i, t)
            ffn_stage(P, i, t)
        em.dma("act", P.ch_y, yTr[:, :, t0:t0 + T], P.xf[:, :, :], R=P.xf.b)
    em.wait_all("act", P.xf.b)
    em.flush()
    return P


_CACHE = {}


def kernel(**inputs):
    x = np.asarray(inputs["x"], dtype=np.float32)
    p = np.asarray(inputs["p"], dtype=np.float32)
    B = x.shape[0]
    wf = pack_weights(inputs)
    sm = pack_smalls(inputs)
    cs = make_consts()
    if "P" not in _CACHE:
        _CACHE["P"] = build_program()
    P = _CACHE["P"]
    in_maps = []
    for b in range(B):
        in_maps.append({
            "xT": np.ascontiguousarray(x[b].T),
            "pT": np.ascontiguousarray(p[:, b].transpose(0, 2, 1)),
            "wf": wf, "smalls": sm, "consts": cs,
        })
    res = run_bass_kernel_spmd(P.nc, in_maps, core_ids=list(range(B)))
    out = np.stack([np.ascontiguousarray(r["yT"].T) for r in res.results], axis=0)
    return out.astype(np.float32)
```
